# Optimizing a Trainium2 kernel written in Bass

```python
import jax, jax.numpy as jnp
from jax import lax
import numpy as np

D_MODEL = 1024
BATCH = 8
SEQ = 4096
DEPTH = 1

HG_HEADS = 4
HG_KEY_DIM = 128
HG_VAL_DIM = 128
HG_KEY_WIDTH = HG_HEADS * HG_KEY_DIM
HG_WIDTH = HG_HEADS * HG_VAL_DIM
HG_CHUNK = 64
MLA_HEADS = 4
MLA_NOPE_DIM = 128
MLA_ROPE_DIM = 64
MLA_V_DIM = 128
MLA_Q_RANK = 256
MLA_KV_RANK = 256
MLA_WIDTH = MLA_HEADS * MLA_V_DIM
ROPE_THETA = 10000.0
Q_BLOCK = 128
MIX_WIDTH = HG_WIDTH + MLA_WIDTH
D_FF = 4 * D_MODEL
N_MOD = 6
ADA_INIT = 0.5
RMS_EPS = 1e-6
LN_EPS = 1e-5
DN_ALPHA = (2.0 * DEPTH) ** 0.25
DN_BETA = (8.0 * DEPTH) ** -0.25
IN_SIZES = (HG_KEY_WIDTH, HG_KEY_WIDTH, HG_WIDTH, HG_WIDTH, MLA_Q_RANK, MLA_KV_RANK, MLA_ROPE_DIM)
IN_COLS = sum(IN_SIZES)
IN_SPLITS = tuple(int(s) for s in np.cumsum(IN_SIZES)[:-1])

kernel_name = 'hybrid_hgrn2_mla_deepnorm_adaln'


def rms_norm(x, w):
    xf = x.astype(jnp.float32)
    y = xf * lax.rsqrt(jnp.mean(xf * xf, axis=-1, keepdims=True) + RMS_EPS) * w.astype(jnp.float32)
    return y.astype(x.dtype)


def layer_norm(x, g, b):
    xf = x.astype(jnp.float32)
    mu = jnp.mean(xf, axis=-1, keepdims=True)
    xc = xf - mu
    var = jnp.mean(xc * xc, axis=-1, keepdims=True)
    y = xc * lax.rsqrt(var + LN_EPS) * g.astype(jnp.float32) + b.astype(jnp.float32)
    return y.astype(x.dtype)


def rope_tables(positions):
    inv_freq = 1.0 / (ROPE_THETA ** (jnp.arange(0, MLA_ROPE_DIM, 2, dtype=jnp.float32) / MLA_ROPE_DIM))
    ang = positions.astype(jnp.float32)[..., None] * inv_freq
    return jnp.cos(ang), jnp.sin(ang)


def apply_rope(x, cos, sin):
    xf = x.astype(jnp.float32)
    x1, x2 = jnp.split(xf, 2, axis=-1)
    return jnp.concatenate([x1 * cos - x2 * sin, x2 * cos + x1 * sin], axis=-1).astype(x.dtype)


def hgrn2_chunkwise(q, f_logit, v, lb):
    B, T, H, dk = q.shape
    dv = v.shape[-1]
    C = HG_CHUNK
    n = T // C
    f32 = jnp.float32
    forget = lb.astype(f32) + (1.0 - lb.astype(f32)) * jax.nn.sigmoid(f_logit.astype(f32))
    k = 1.0 - forget
    log_f = jnp.log(forget)

    def chunked(a):
        return a.astype(f32).reshape(B, n, C, H, a.shape[-1]).transpose(0, 3, 1, 2, 4)

    q, k, v, log_f = chunked(q), chunked(k), chunked(v), chunked(log_f)
    b = jnp.cumsum(log_f, axis=3)
    b_ref = b[:, :, :, C // 2 - 1:C // 2, :]
    b_last = b[:, :, :, C - 1:C, :]
    causal = jnp.tril(jnp.ones((C, C), dtype=bool))
    a = jnp.einsum('bhncd,bhnsd->bhncs', q * jnp.exp(b - b_ref), k * jnp.exp(b_ref - b))
    a = jnp.where(causal, a, 0.0)
    o_intra = jnp.einsum('bhncs,bhnse->bhnce', a, v)
    kv = jnp.einsum('bhnsd,bhnse->nbhde', k * jnp.exp(b_last - b), v)
    decay = jnp.exp(b_last[:, :, :, 0, :]).transpose(2, 0, 1, 3)

    def step(state, inp):
        d, kv_n = inp
        return d[..., None] * state + kv_n, state

    s0 = jnp.zeros((B, H, dk, dv), f32)
    _, s_prev = lax.scan(step, s0, (decay, kv))
    o_inter = jnp.einsum('bhncd,nbhde->bhnce', q * jnp.exp(b), s_prev)
    return (o_intra + o_inter).transpose(0, 2, 3, 1, 4).reshape(B, T, H, dv)


def mla_causal_attention(q_nope, q_pe, k_nope, k_pe, v):
    B, T, H, _ = q_nope.shape
    nb = T // Q_BLOCK
    scale = (MLA_NOPE_DIM + MLA_ROPE_DIM) ** -0.5
    qn = q_nope.reshape(B, nb, Q_BLOCK, H, MLA_NOPE_DIM).transpose(1, 0, 2, 3, 4)
    qp = q_pe.reshape(B, nb, Q_BLOCK, H, MLA_ROPE_DIM).transpose(1, 0, 2, 3, 4)
    starts = jnp.arange(nb, dtype=jnp.int32) * Q_BLOCK
    key_idx = jnp.arange(T, dtype=jnp.int32)
    neg = jnp.finfo(jnp.float32).min

    def block(args):
        qn_b, qp_b, start = args
        s = (jnp.einsum('bqhd,bkhd->bhqk', qn_b, k_nope).astype(jnp.float32)
             + jnp.einsum('bqhd,bkd->bhqk', qp_b, k_pe).astype(jnp.float32)) * scale
        q_idx = start + jnp.arange(Q_BLOCK, dtype=jnp.int32)
        s = jnp.where(key_idx[None, :] <= q_idx[:, None], s, neg)
        p = jax.nn.softmax(s, axis=-1)
        return jnp.einsum('bhqk,bkhd->bqhd', p.astype(v.dtype), v)

    out = lax.map(block, (qn, qp, starts))
    return out.transpose(1, 0, 2, 3, 4).reshape(B, T, H, MLA_V_DIM)


def hybrid_mixer(u, cos, sin, lb, w_in, hg_norm_w, q_norm_w, w_q_up, kv_norm_w, w_kv_up, w_out):
    B, T, _ = u.shape
    z = u @ w_in
    hq, hf, hi, hg, c_q, c_kv, k_pe = jnp.split(z, IN_SPLITS, axis=-1)
    o_hg = hgrn2_chunkwise(hq.reshape(B, T, HG_HEADS, HG_KEY_DIM),
                           hf.reshape(B, T, HG_HEADS, HG_KEY_DIM),
                           hi.reshape(B, T, HG_HEADS, HG_VAL_DIM),
                           lb.reshape(HG_HEADS, HG_KEY_DIM))
    o_hg = rms_norm(o_hg, hg_norm_w.reshape(HG_HEADS, HG_VAL_DIM))
    o_hg = (o_hg * jax.nn.silu(hg.astype(jnp.float32)).reshape(B, T, HG_HEADS, HG_VAL_DIM))
    o_hg = o_hg.astype(u.dtype).reshape(B, T, HG_WIDTH)
    q = (rms_norm(c_q, q_norm_w) @ w_q_up).reshape(B, T, MLA_HEADS, MLA_NOPE_DIM + MLA_ROPE_DIM)
    q_nope, q_pe = q[..., :MLA_NOPE_DIM], q[..., MLA_NOPE_DIM:]
    kvu = (rms_norm(c_kv, kv_norm_w) @ w_kv_up).reshape(B, T, MLA_HEADS, MLA_NOPE_DIM + MLA_V_DIM)
    k_nope, v = kvu[..., :MLA_NOPE_DIM], kvu[..., MLA_NOPE_DIM:]
    q_pe = apply_rope(q_pe, cos[:, :, None, :], sin[:, :, None, :])
    k_pe = apply_rope(k_pe, cos, sin)
    o_mla = mla_causal_attention(q_nope, q_pe, k_nope, k_pe, v).reshape(B, T, MLA_WIDTH)
    return jnp.concatenate([o_hg, o_mla.astype(u.dtype)], axis=-1) @ w_out


def setup_inputs(seed: int = 0) -> dict:
    key = jax.random.key(seed)
    ks = jax.random.split(key, 20)

    def nrm(k, shape, scale):
        return jax.random.normal(k, shape, jnp.float32) * scale

    x = nrm(ks[0], (BATCH, SEQ, D_MODEL), 1.0)
    c = nrm(ks[1], (BATCH, D_MODEL), 1.0)
    offsets = jax.random.randint(ks[2], (BATCH, 1), 0, 1024, dtype=jnp.int32)
    positions = (jnp.arange(SEQ, dtype=jnp.int32)[None, :] + offsets).astype(jnp.int32)
    return {
        'x': x,
        'c': c,
        'positions': positions,
        'w_ada': nrm(ks[3], (DEPTH, D_MODEL, N_MOD * D_MODEL), ADA_INIT * D_MODEL ** -0.5),
        'b_ada': nrm(ks[4], (DEPTH, N_MOD * D_MODEL), 0.02),
        'w_in': nrm(ks[5], (DEPTH, D_MODEL, IN_COLS), D_MODEL ** -0.5),
        'hg_lower_bounds': nrm(ks[6], (DEPTH + 1, HG_KEY_WIDTH), 0.1),
        'hg_norm_w': 1.0 + nrm(ks[7], (DEPTH, HG_WIDTH), 0.02),
        'mla_q_norm_w': 1.0 + nrm(ks[8], (DEPTH, MLA_Q_RANK), 0.02),
        'w_q_up': nrm(ks[9], (DEPTH, MLA_Q_RANK, MLA_HEADS * (MLA_NOPE_DIM + MLA_ROPE_DIM)), MLA_Q_RANK ** -0.5),
        'mla_kv_norm_w': 1.0 + nrm(ks[10], (DEPTH, MLA_KV_RANK), 0.02),
        'w_kv_up': nrm(ks[11], (DEPTH, MLA_KV_RANK, MLA_HEADS * (MLA_NOPE_DIM + MLA_V_DIM)), MLA_KV_RANK ** -0.5),
        'w_out': nrm(ks[12], (DEPTH, MIX_WIDTH, D_MODEL), DN_BETA * MIX_WIDTH ** -0.5),
        'ln1_g': 1.0 + nrm(ks[13], (DEPTH, D_MODEL), 0.02),
        'ln1_b': nrm(ks[14], (DEPTH, D_MODEL), 0.02),
        'w_mlp_in': nrm(ks[15], (DEPTH, D_MODEL, D_FF), D_MODEL ** -0.5),
        'w_mlp_out': nrm(ks[16], (DEPTH, D_FF, D_MODEL), DN_BETA * D_FF ** -0.5),
        'ln2_g': 1.0 + nrm(ks[17], (DEPTH, D_MODEL), 0.02),
        'ln2_b': nrm(ks[18], (DEPTH, D_MODEL), 0.02),
    }


def reference(x, c, positions, w_ada, b_ada, w_in, hg_lower_bounds, hg_norm_w, mla_q_norm_w, w_q_up,
              mla_kv_norm_w, w_kv_up, w_out, ln1_g, ln1_b, w_mlp_in, w_mlp_out, ln2_g, ln2_b):
    cos, sin = rope_tables(positions)
    lbs = jnp.cumsum(jax.nn.softmax(hg_lower_bounds.astype(jnp.float32), axis=0), axis=0)[:DEPTH]
    cond = jax.nn.silu(c)
    for l in range(DEPTH):
        mod = (cond @ w_ada[l] + b_ada[l])[:, None, :]
        sh_a, sc_a, g_a, sh_m, sc_m, g_m = jnp.split(mod, N_MOD, axis=-1)
        u = x * (1.0 + sc_a) + sh_a
        mix = hybrid_mixer(u, cos, sin, lbs[l], w_in[l], hg_norm_w[l], mla_q_norm_w[l], w_q_up[l],
                           mla_kv_norm_w[l], w_kv_up[l], w_out[l])
        x = layer_norm(DN_ALPHA * x + (1.0 + g_a) * mix, ln1_g[l], ln1_b[l])
        u = x * (1.0 + sc_m) + sh_m
        h = jnp.square(jax.nn.relu(u @ w_mlp_in[l])) @ w_mlp_out[l]
        x = layer_norm(DN_ALPHA * x + (1.0 + g_m) * h, ln2_g[l], ln2_b[l])
    return x
```

```python
import contextlib
import math
import numpy as np
import concourse.bass as bass
import concourse.mybir as mybir
from concourse.bass_utils import run_bass_kernel_spmd

F32 = mybir.dt.float32
BF16 = mybir.dt.bfloat16
I32 = mybir.dt.int32
AF = mybir.ActivationFunctionType
ALU = mybir.AluOpType

D = 1024
T = 4096
NT = T // 128
MT = 256
NM = T // MT
DFF = 4096
ALPHA = 2.0 ** 0.25
SCALE = 192.0 ** -0.5
WIN_COLS = 2688
PI = math.pi
PI_SAFE = 3.1415925


class _Op:
    __slots__ = ("id", "eng", "fn", "reads", "writes", "deps", "is_dma", "sem_key", "signal", "count", "group")


class Sched:
    ENGS = ("pe", "act", "dve", "pool", "sp")

    def __init__(self, nc):
        self.nc = nc
        self.ops = []
        self.last_writer = {}
        self.readers = {}
        self.dma_keys = []
        self.default_writer = None
        self.limit = None

    def add(self, eng, fn, reads=(), writes=(), dma=False, sem_key=None):
        if self.limit is not None and len(self.ops) >= self.limit:
            return None
        psr = [("ps", r[1]) for r in list(reads) + list(writes) if isinstance(r, tuple) and r and r[0] == "ps"]
        reads = tuple(r for r in reads if not (isinstance(r, tuple) and r and r[0] == "ps"))
        writes = tuple(dict.fromkeys([w for w in writes if not (isinstance(w, tuple) and w and w[0] == "ps")] + psr))
        op = _Op()
        op.id = len(self.ops)
        op.eng = eng
        op.fn = fn
        op.reads = reads
        op.writes = writes
        op.is_dma = dma
        op.group = sem_key if sem_key is not None else (writes[0] if dma else None)
        op.sem_key = op.group
        op.signal = False
        op.count = 0
        deps = set()
        for r in reads:
            w = self.last_writer.get(r, self.default_writer)
            if w is not None:
                deps.add((w, 0))
        for w_ in writes:
            w = self.last_writer.get(w_, self.default_writer)
            if w is not None:
                deps.add((w, 1))
            for rd in self.readers.get(w_, ()):
                deps.add((rd, 1))
        keep = set()
        for d, kind in deps:
            if d == op.id:
                continue
            dop = self.ops[d]
            if (not dop.is_dma) and (not dma) and dop.eng == eng:
                if eng == "pe":
                    continue
            keep.add(d)
        op.deps = sorted(keep)
        for r in reads:
            self.readers.setdefault(r, []).append(op.id)
        for w_ in writes:
            self.last_writer[w_] = op.id
            self.readers[w_] = []
        if dma and op.sem_key not in self.dma_keys:
            self.dma_keys.append(op.sem_key)
        self.ops.append(op)
        return op

    def mark(self, name):
        print("MARK", name, len(self.ops))

    def barrier(self, fn):
        allres = set(self.last_writer) | set(self.readers)
        op = self.add("sp", fn, reads=(), writes=["__bar%d" % len(self.ops)] + sorted(allres, key=str), dma=True)
        self.default_writer = op.id
        self.last_writer = {}
        self.readers = {}
        return op

    def emit(self, st, final_wait_keys=()):
        nc = self.nc
        ops = self.ops
        for op in ops:
            for d in op.deps:
                ops[d].signal = True
        eng_cnt = {e: 0 for e in self.ENGS}
        dma_cnt = {k: 0 for k in self.dma_keys}
        for op in ops:
            if op.is_dma:
                dma_cnt[op.sem_key] += 16
                op.count = dma_cnt[op.sem_key]
            elif op.signal:
                eng_cnt[op.eng] += 1
                op.count = eng_cnt[op.eng]
        esem = {e: st.enter_context(nc.semaphore("s_" + e)) for e in ("pe", "act", "dve", "pool")}
        dsem = {k: st.enter_context(nc.semaphore("d%d" % i)) for i, k in enumerate(self.dma_keys)}
        block = st.enter_context(nc.Block())
        by_eng = {e: [op for op in ops if op.eng == e] for e in self.ENGS}

        def run(ename, engine):
            seen = {}
            for op in by_eng[ename]:
                need = {}
                for d in op.deps:
                    dop = ops[d]
                    if dop.is_dma:
                        key = ("d", dop.sem_key)
                        sem = dsem[dop.sem_key]
                    else:
                        key = ("e", dop.eng)
                        sem = esem[dop.eng]
                    if dop.count > seen.get(key, 0) and dop.count > need.get(key, (None, 0))[1]:
                        need[key] = (sem, dop.count)
                for key, (sem, cnt) in need.items():
                    engine.wait_ge(sem, cnt)
                    seen[key] = cnt
                ins = op.fn(engine)
                if op.is_dma:
                    ins.then_inc(dsem[op.sem_key], 16)
                elif op.signal:
                    ins.then_inc(esem[op.eng], 1)
            if ename == "sp":
                for k in final_wait_keys:
                    if k in dma_cnt:
                        engine.wait_ge(dsem[k], dma_cnt[k])

        @block.tensor
        def _(e):
            run("pe", e)

        @block.scalar
        def _(e):
            run("act", e)

        @block.vector
        def _(e):
            run("dve", e)

        @block.gpsimd
        def _(e):
            run("pool", e)

        @block.sync
        def _(e):
            run("sp", e)


def build_nc(debug=False, n_macro=NM, do_phase2=True, limit=None):
    nc = bass.Bass("TRN2", target_bir_lowering=False, dynamic_dma_scratch_size=4096)

    def din(name, shape, dt=F32):
        return nc.dram_tensor(name, list(shape), dt, kind="ExternalInput").ap()

    x_d = din("x", [T, D])
    pos_d = din("pos", [T], I32)
    ccol_d = din("ccol", [128, 8])
    wada_d = din("w_ada", [D, 6 * D])
    bada_d = din("bada", [128, 48])
    win_d = din("w_in", [D, WIN_COLS])
    lb_d = din("lb", [2, 512])
    nw_d = din("nw", [128, 8])
    wq_d = din("w_q", [256, 1024])
    wkv_d = din("w_kv", [256, 1024])
    wout_d = din("w_out", [D, D])
    ln_d = din("lncols", [128, 32])
    w1_d = din("w1", [D, DFF])
    w2_d = din("w2", [DFF, D])
    cst_d = din("cst", [128, 768])
    misc_d = din("misc", [128, 4])
    out_d = nc.dram_tensor("out", [T, D], F32, kind="ExternalOutput").ap()
    x1s_d = nc.dram_tensor("x1s", [128, 8 * T], F32, kind="Internal").ap()
    bar_d = nc.dram_tensor("bar", [1, 64], F32, kind="Internal").ap()
    dbg = {}
    if debug:
        for nm, shp in debug.items():
            dbg[nm] = nc.dram_tensor("dbg_" + nm, list(shp), F32, kind="ExternalOutput").ap()

    S = Sched(nc)
    S.limit = limit
    st = contextlib.ExitStack()
    with st:
        def sb(name, shape, dt=F32):
            return st.enter_context(nc.sbuf_tensor("sb_" + name, list(shape), dt))

        ps = st.enter_context(nc.psum_tensor("ps", [128, 4096], F32))

        def bank(b, lo=0, hi=512):
            return ps[:, b * 512 + lo:b * 512 + hi]

        def pres(b, half=None):
            if half is None:
                return [("ps", b, 0), ("ps", b, 1)]
            return [("ps", b, half)]

        arena = sb("arena", [128, 73728], BF16)
        a_off = [0]

        def carve(n):
            o = a_off[0]
            a_off[0] += n
            return arena[:, o:o + n]

        KnT = carve(4 * T).rearrange("p (h t) -> p h t", t=T)
        Vc = carve(NT * 512).rearrange("p (t c) -> p t c", c=512)
        KpeT = carve(T)
        winb = carve(8 * WIN_COLS).rearrange("p (k c) -> p k c", c=WIN_COLS)
        wqb = carve(2 * 1024).rearrange("p (k c) -> p k c", c=1024)
        wkvb = carve(2 * 1024).rearrange("p (k c) -> p k c", c=1024)
        woutb = carve(8 * 1024).rearrange("p (k c) -> p k c", c=1024)
        w1b = arena[:, 0:8 * DFF].rearrange("p (k c) -> p k c", c=DFF)
        w2b = arena[:, 8 * DFF:8 * DFF + 32 * D].rearrange("p (f c) -> p f c", c=D)
        hT3 = arena[:, 65536:73728].rearrange("p (f t) -> p f t", t=MT)

        def hTbuf(f):
            return hT3[:, f, :]
        wadab = [arena[:, i * 6144:(i + 1) * 6144].rearrange("p (k c) -> p k c", c=768) for i in range(2)]

        cst = sb("cst", [128, 512])
        ident_f = cst[:, 0:128]
        bd12 = cst[:, 128:384]
        bdlast = cst[:, 384:512]
        hmask = cst[:, 256:384]
        cstb = sb("cstb", [128, 256], BF16)
        ident_b = cstb[:, 0:128]
        amask_b = cstb[:, 128:256]
        ones_f = sb("ones_f", [128, 128])
        onesm_f = sb("onesm_f", [128, 128])
        ones_b = sb("ones_b", [128, 128], BF16)
        misc = sb("misc", [128, 4])
        invf = misc[:, 0:1]
        sgn = misc[:, 1:2]
        oml_bc = sb("oml_bc", [128, 512])
        ccol = sb("ccol", [128, 8])
        condb = sb("condb", [128, 8], BF16)
        bada = sb("bada", [128, 48])
        modT = sb("modT", [128, 48])
        mod1 = sb("mod1", [128, 48])
        nw = sb("nw", [128, 8])
        lnc = sb("lnc", [128, 32])
        kmax2 = sb("kmax2", [128, 4])
        lnca = sb("lnca", [128, 16])
        scm = sb("scm", [128, 8])
        small = sb("small", [128, 16])

        xT = sb("xT", [128, 2048])
        xT3 = xT[:, :].rearrange("p (c t) -> p c t", t=MT)
        xTb = sb("xTb", [128, 2048])
        xTb3 = xTb[:, :].rearrange("p (c t) -> p c t", t=MT)
        uT = sb("uT", [128, 2048], BF16)
        uT3 = uT[:, :].rearrange("p (c t) -> p c t", t=MT)
        qTs = sb("qTs", [128, 1024])
        qTs3 = qTs[:, :].rearrange("p (h t) -> p h t", t=MT)
        cqTb = sb("cqTb", [128, 512], BF16)
        ckvTb = sb("ckvTb", [128, 512], BF16)
        G = [sb("G%d" % i, [128, 512]) for i in range(7)]
        khat = carve(512)
        vb = carve(512)
        ohn = carve(512)
        ktil = carve(512)
        qtil = carve(512)
        qh0 = carve(512)
        qh1 = sb("qh1", [128, 512], BF16)
        ATm = sb("ATm", [128, 512], BF16)
        Sst = sb("Sst", [128, 512])
        SA = sb("SA", [128, 512])
        S0b = sb("S0b", [128, 512], BF16)
        S1b = sb("S1b", [128, 512], BF16)
        QnT = sb("QnT", [128, 1024], BF16)
        QnT3 = QnT[:, :].rearrange("p (h t) -> p h t", t=MT)
        QpeT = sb("QpeT", [128, 1024], BF16)
        QpeT3 = QpeT[:, :].rearrange("p (h t) -> p h t", t=MT)
        oT = sb("oT", [128, 2048], BF16)
        oT3 = oT[:, :].rearrange("p (c t) -> p c t", t=MT)
        xin1 = sb("xin1", [128, 1024])
        xin0 = sb("xin0", [128, 1024])
        xin = xin0
        rinv = sb("rinv", [128, 256])
        oacc = sb("oacc", [128, 256])
        PT = [sb("PT%d" % i, [128, 256], BF16) for i in range(3)]
        posi = sb("posi", [128, 256], I32)
        kfi = posi
        ssq = small[:, 0:4]
        rso = small[:, 4:8]
        rkc = small[:, 8:10]
        kmt = small[:, 10:11]

        def v3(ap, inner):
            return ap.rearrange("p (a b) -> p a b", b=inner)

        def XC(name, cs=range(8)):
            return ["%sc%d" % (name, c) for c in cs]

        def dma(eng, out, in_, reads, writes, key=None, **kw):
            return S.add(eng, lambda e: e.dma_start(out=out, in_=in_, **kw), reads, writes, dma=True, sem_key=key)

        def mm(out, lhsT, rhs, start, stop, reads, writes):
            return S.add("pe", lambda e: e.matmul(out, lhsT=lhsT, rhs=rhs, start=start, stop=stop), reads, writes)

        def tr(out, in_, ident, reads, writes):
            return S.add("pe", lambda e: e.transpose(out=out, in_=in_, identity=ident), reads, writes)

        def act(out, in_, func, reads, writes, **kw):
            return S.add("act", lambda e: e.activation(out=out, in_=in_, func=func, **kw), reads, writes)

        def tt(eng, out, in0, in1, op, reads, writes):
            return S.add(eng, lambda e: e.tensor_tensor(out=out, in0=in0, in1=in1, op=op), reads, writes)

        def ts(eng, out, in0, s1, s2, op0, op1, reads, writes):
            if op1 is None:
                return S.add(eng, lambda e: e.tensor_scalar(out=out, in0=in0, scalar1=s1, scalar2=None, op0=op0), reads, writes)
            return S.add(eng, lambda e: e.tensor_scalar(out=out, in0=in0, scalar1=s1, scalar2=s2, op0=op0, op1=op1), reads, writes)

        def stt(out, in0, scalar, in1, op0, op1, reads, writes):
            return S.add("dve", lambda e: e.scalar_tensor_tensor(out=out, in0=in0, scalar=scalar, in1=in1, op0=op0, op1=op1), reads, writes)

        def cp(eng, out, in_, reads, writes):
            return S.add(eng, lambda e: e.tensor_copy(out=out, in_=in_), reads, writes)

        def mset(eng, ap, val, writes):
            return S.add(eng, lambda e: e.memset(ap, val), (), writes)

        def recip(out, in_, reads, writes):
            return S.add("dve", lambda e: e.reciprocal(out=out, in_=in_), reads, writes)

        dbg_keys = []

        def tap(name, src, reads):
            if debug and name in dbg:
                dma("pool", dbg[name], src, reads, ["dbgout_" + name], key="dbg_" + name)
                dbg_keys.append("dbg_" + name)

        dma("sp", cst[:, :], cst_d[:, 0:512], [], ["cst"])
        dma("sp", misc[:, :], misc_d, [], ["misc"])
        dma("sp", ccol[:, :], ccol_d, [], ["ccol"])
        dma("sp", bada[:, :], bada_d, [], ["bada"])
        dma("sp", nw[:, :], nw_d, [], ["nw"])
        dma("sp", lnc[:, :], ln_d, [], ["lnc"])
        dma("sp", xT[:, 0:1024], lb_d.rearrange("a c -> (a c)").partition_broadcast(128), [], XC("xT0"))
        dma("pool", ident_b, cst_d[:, 0:128], [], ["ident_b"])
        dma("pool", amask_b, cst_d[:, 640:768], [], ["amask_b"])
        mset("dve", ones_f[:, :], 1.0, ["ones_f"])
        mset("dve", onesm_f[:, :], 1.0 / 1024.0, ["onesm_f"])
        mset("dve", ones_b[:, :], 1.0, ["ones_b"])
        mset("dve", Sst[:, :], 0.0, ["Sst"])
        mset("dve", S0b[:, :], 0.0, ["S0b"])
        mset("dve", qh0[:, :], 0.0, ["qh0"])
        mset("dve", qh1[:, :], 0.0, ["qh1"])
        mset("dve", kmax2[:, :], 0.0, ["kmax2"])
        mset("dve", QpeT[:, :], 0.0, ["QpeT"])

        act(condb[:, :], ccol[:, :], AF.Silu, ["ccol"], ["condb"])
        for ch in range(8):
            wb = wadab[ch % 2]
            wname = "wada%d" % (ch % 2)
            dma("pool", wb, wada_d[:, ch * 768:(ch + 1) * 768].rearrange("(k p) c -> p k c", p=128), [], [wname])
            for jj in range(6):
                j = ch * 6 + jj
                for k in range(8):
                    mm(bank(0, j, j + 1), wb[:, k, jj * 128:(jj + 1) * 128], condb[:, k:k + 1], k == 0, k == 7,
                       [wname, "condb"], pres(0))
        tt("dve", modT[:, :], bank(0, 0, 48), bada[:, :], ALU.add, pres(0) + ["bada"], ["modT"])
        ts("dve", mod1[:, :], modT[:, :], 1.0, None, ALU.add, None, ["modT"], ["mod1"])
        ts("dve", lnca[:, :], lnc[:, 0:16], ALPHA, None, ALU.mult, None, ["lnc"], ["lnca"])
        ts("dve", scm[:, :], mod1[:, 32:40], 1.0 / ALPHA, None, ALU.mult, None, ["mod1"], ["scm"])
        sha, sc1a, g1a = modT[:, 0:8], mod1[:, 8:16], mod1[:, 16:24]
        shm, sc1m, g1m = modT[:, 24:32], mod1[:, 32:40], mod1[:, 40:48]
        tap("modT", modT[:, :], ["modT"])

        tt("dve", G[0][:, :], xT[:, 0:512], xT[:, 512:1024], ALU.subtract, XC("xT0"), ["G0"])
        act(G[0][:, :], G[0][:, :], AF.Sigmoid, ["G0"], ["G0"])
        ts("dve", oml_bc[:, :], G[0][:, :], -1.0, 1.0, ALU.mult, ALU.add, ["G0"], ["oml_bc"])

        for k in range(8):
            dma("pool", winb[:, k, :].rearrange("p (a b) -> p a b", b=1344), win_d[k * 128:(k + 1) * 128, :].rearrange("p (a b) -> p a b", b=1344), [], ["winb%d" % k])
        for k in range(2):
            dma("pool", wqb[:, k, :], wq_d[k * 128:(k + 1) * 128, :], [], ["wqb%d" % k])
            dma("pool", wkvb[:, k, :], wkv_d[k * 128:(k + 1) * 128, :], [], ["wkvb%d" % k])
            act(wqb[:, k, :], wqb[:, k, :], AF.Identity, ["wqb%d" % k, "nw"], ["wqb%d" % k], scale=nw[:, k:k + 1])
            act(wkvb[:, k, :], wkvb[:, k, :], AF.Identity, ["wkvb%d" % k, "nw"], ["wkvb%d" % k], scale=nw[:, 2 + k:3 + k])
        wout_scale = []
        for k in range(8):
            dma("pool", woutb[:, k, :], wout_d[k * 128:(k + 1) * 128, :], [], ["woutb%d" % k])
            if k < 4:
                wout_scale.append(lambda k=k: act(woutb[:, k, :], woutb[:, k, :], AF.Identity, ["woutb%d" % k, "nw"], ["woutb%d" % k],
                                                  scale=nw[:, 4 + k:5 + k]))
        WIN = ["winb%d" % k for k in range(8)]
        WQ = ["wqb0", "wqb1"]
        WKV = ["wkvb0", "wkvb1"]
        WOUT = ["woutb%d" % k for k in range(8)]
        mset("dve", KpeT[64:65, :], 1.0, ["KpeT_ones"])

        def kres(t):
            return "K%d" % t


        for m in range(n_macro):
            t0 = 2 * m
            msl = slice(m * MT, (m + 1) * MT)
            extraK = ["wada0", "wada1"] if m == 0 else []
            XP3 = xT3 if m % 2 == 0 else xTb3
            xTn = "xT%d" % (m % 2)
            def emit_xload(m):
                t0 = 2 * m
                XP3 = xT3 if m % 2 == 0 else xTb3
                xTn = "xT%d" % (m % 2)
                evac = []
                for tt_ in range(2):
                    tsl = slice(tt_ * 128, (tt_ + 1) * 128)
                    xs_, xn_ = (xin0, "xin0") if tt_ == 0 else (xin1, "xin1")
                    for hb in range(2):
                        xb_ = 4 + 2 * tt_ + hb
                        for c4 in range(4):
                            c = hb * 4 + c4
                            tr(bank(xb_, c4 * 128, (c4 + 1) * 128), xs_[:, c * 128:(c + 1) * 128], ident_f, [xn_, "cst"], pres(xb_))
                        evac.append(lambda hb=hb, tsl=tsl, xb_=xb_: act(
                            XP3[:, hb * 4:(hb + 1) * 4, tsl], v3(bank(xb_), 128), AF.Copy, pres(xb_), XC(xTn, range(hb * 4, hb * 4 + 4)), scale=ALPHA))
                        for c4 in range(4):
                            c = hb * 4 + c4
                            evac.append(lambda c=c, c4=c4, tsl=tsl, xb_=xb_: act(
                                uT3[:, c, tsl], bank(xb_, c4 * 128, (c4 + 1) * 128), AF.Identity, pres(xb_) + ["modT", "mod1"], ["uT"],
                                scale=sc1a[:, c:c + 1], bias=sha[:, c:c + 1]))
                if m + 1 < n_macro:
                    dma("sp", xin0[:, :], x_d[(t0 + 2) * 128:(t0 + 3) * 128, :], [], ["xin0"])
                    dma("sp", xin1[:, :], x_d[(t0 + 3) * 128:(t0 + 4) * 128, :], [], ["xin1"])
                return evac

            if m == 0:
                dma("sp", xin0[:, :], x_d[0:128, :], [], ["xin0"])
                dma("sp", xin1[:, :], x_d[128:256, :], [], ["xin1"])
                for f_ in emit_xload(0):
                    f_()
            if m == 0:
                tap("uT", uT[:, :], ["uT"])
            if m == 0:
                S.mark("xload")
            def emit_rope(m):
                dma("sp", posi[:, :], pos_d[m * MT:(m + 1) * MT].partition_broadcast(128), [], ["posi"])
                ang, tmpa, tmpb = G[2][:, 0:256], G[2][:, 256:512], G[3][:, 0:256]
                Ct, Sg = G[6][:, 0:256], G[6][:, 256:512]
                cp("dve", tmpa, posi[:, :], ["posi"], ["G2b"])
                ts("dve", ang, tmpa, invf, None, ALU.mult, None, ["G2b", "misc"], ["G2a"])
                ts("dve", kfi[:, :], ang, 1.0 / (2 * PI), None, ALU.mult, None, ["G2a"], ["posi"])
                cp("dve", tmpa, kfi[:, :], ["posi"], ["G2b"])
                stt(ang, tmpa, -2.0 * PI, ang, ALU.mult, ALU.add, ["G2a", "G2b"], ["G2a"])
                ts("dve", tmpa, ang, PI, None, ALU.is_gt, None, ["G2a"], ["G2b"])
                stt(tmpa, tmpa, -2.0 * PI, ang, ALU.mult, ALU.add, ["G2a", "G2b"], ["G2b"])
                ts("dve", tmpa, tmpa, -PI_SAFE, PI_SAFE, ALU.max, ALU.min, ["G2b"], ["G2b"])
                act(Sg, tmpa, AF.Sin, ["G2b", "misc"], ["G6"], scale=sgn)
                ts("dve", tmpb, ang, PI / 2, None, ALU.add, None, ["G2a"], ["G3a"])
                ts("dve", tmpa, tmpb, PI, None, ALU.is_gt, None, ["G3a"], ["G2b"])
                stt(tmpb, tmpa, -2.0 * PI, tmpb, ALU.mult, ALU.add, ["G3a", "G2b"], ["G3a"])
                ts("dve", tmpb, tmpb, -PI_SAFE, PI_SAFE, ALU.max, ALU.min, ["G3a"], ["G3a"])
                act(Ct, tmpb, AF.Sin, ["G3a"], ["G6"])


            if m == 0:
                emit_rope(0)
            Ct, Sg = G[6][:, 0:256], G[6][:, 256:512]
            if m == 0:
                S.mark("rope")
            sq = v3(G[0][:, :], 256)
            sq2 = v3(G[1][:, :], 256)
            cq3 = v3(cqTb[:, :], 256)
            ckv3 = v3(ckvTb[:, :], 256)
            pb = 0
            for j in range(8):
                b_, hf_ = 2 + j % 2, 0
                for k in range(8):
                    mm(bank(b_, hf_ * 256, hf_ * 256 + 256), winb[:, k, j * 128:(j + 1) * 128], uT3[:, k, :], k == 0, k == 7,
                       ["uT", WIN[k]], pres(b_, hf_))
                src = bank(b_, hf_ * 256, hf_ * 256 + 256)
                if j < 4:
                    act(qTs3[:, j, :], src, AF.Copy, pres(b_, hf_), ["qTs"])
                elif j < 6:
                    act(sq[:, j - 4, :], src, AF.Square, pres(b_, hf_), ["G0"])
                    cp("dve", cq3[:, j - 4, :], src, pres(b_, hf_), ["cqTb"])
                else:
                    act(sq2[:, j - 6, :], src, AF.Square, pres(b_, hf_), ["G1"])
                    cp("dve", ckv3[:, j - 6, :], src, pres(b_, hf_), ["ckvTb"])
            if m == 0:
                S.mark("winfm")
            for jj in range(2):
                for k in range(8):
                    mm(ps[0:64, (2 + jj) * 512:(2 + jj) * 512 + 256], winb[:, k, 1024 + jj * 64:1024 + (jj + 1) * 64], uT3[:, k, :],
                       k == 0, k == 7, ["uT", WIN[k]], pres(2 + jj))
            t1, t2 = G[3][0:64, 256:512], G[4][0:64, 0:256]
            tt("dve", t1, ps[0:64, 1024:1280], Ct[0:64, :], ALU.mult, pres(2) + ["G6"], ["G3b"])
            tt("dve", t2, ps[0:64, 1536:1792], Sg[0:64, :], ALU.mult, pres(3) + ["G6"], ["G4a"])
            tt("dve", KpeT[0:64, msl], t1, t2, ALU.add, ["G3b", "G4a"], [kres(t0) + "pe", kres(t0 + 1) + "pe"] + extraK)
            if m == 0:
                S.mark("kpe")
            rq_bc, rk_bc = G[5][:, 0:256], G[5][:, 256:512]
            for i, (sqx, dst, nm) in enumerate(((sq, rq_bc, "G5a"), (sq2, rk_bc, "G5b"))):
                for k in range(2):
                    mm(bank(i, 0, 256), ones_f[:, :], sqx[:, k, :], k == 0, k == 1, ["ones_f", "G%d" % i], pres(i))
                act(dst, bank(i, 0, 256), AF.Ln, pres(i), [nm], scale=1.0 / 256.0, bias=1e-6)
                act(dst, dst, AF.Exp, [nm], [nm], scale=-0.5)
            for tt_ in range(2):
                for k in range(2):
                    mm(bank(2, tt_, tt_ + 1), sq2[:, k, tt_ * 128:(tt_ + 1) * 128], ones_f[:, 0:1], k == 0, k == 1,
                       ["G1", "ones_f"], pres(2))
            act(rkc, bank(2, 0, 2), AF.Ln, pres(2), ["rkc"], scale=1.0 / 256.0, bias=1e-6)
            act(rkc, rkc, AF.Exp, ["rkc"], ["rkc"], scale=-0.5)
            if m == 0:
                S.mark("rstd")
            for h in range(4):
                b_, hf_ = h % 2, 0
                for k in range(2):
                    mm(bank(b_, 0, 256), wkvb[:, k, h * 128:(h + 1) * 128], ckv3[:, k, :], k == 0, k == 1, ["ckvTb", WKV[k]], pres(b_, 0))
                tt("dve", KnT[:, h, msl], bank(b_, 0, 256), rk_bc, ALU.mult, pres(b_, 0) + ["G5b"],
                   [kres(t0) + "n%d" % h, kres(t0 + 1) + "n%d" % h] + extraK)
                for k in range(2):
                    mm(bank(2 + b_, 0, 256), wqb[:, k, h * 128:(h + 1) * 128], cq3[:, k, :], k == 0, k == 1, ["cqTb", WQ[k]], pres(2 + b_))
                tt("dve", QnT3[:, h, :], bank(2 + b_, 0, 256), rq_bc, ALU.mult, pres(2 + b_) + ["G5a"], ["QnT%d" % h])
            for tt_ in range(2):
                for k in range(2):
                    mm(bank(4 + tt_), ckv3[:, k, tt_ * 128:(tt_ + 1) * 128], wkvb[:, k, 512:1024], k == 0, k == 1, ["ckvTb", WKV[k]], pres(4 + tt_))
                act(Vc[:, t0 + tt_, :], bank(4 + tt_), AF.Identity, pres(4 + tt_) + ["rkc"], [kres(t0 + tt_) + "v"] + extraK, scale=rkc[:, tt_:tt_ + 1])
            g0b = G[0][:, :].bitcast(BF16).rearrange("p (h t) -> p h t", t=MT)
            g1b = G[1][:, :].bitcast(BF16).rearrange("p (h t) -> p h t", t=MT)
            KN = [kres(t0) + "n%d" % h for h in range(4)] + [kres(t0 + 1) + "n%d" % h for h in range(4)]
            QN = ["QnT%d" % h for h in range(4)]
            QP = ["QpeT%d" % h for h in range(4)]
            act(g0b, KnT[:, :, msl], AF.Square, KN, ["G0"])
            act(PT[1][0:64, :], KpeT[0:64, msl], AF.Square, [kres(t0) + "pe", kres(t0 + 1) + "pe"], ["PT1"])
            for h in range(4):
                bk = bank(4 + h // 2, (h % 2) * 256, (h % 2) * 256 + 256)
                mm(bk, ones_b[:, :], g0b[:, h, :], True, False, ["ones_b", "G0"], pres(4 + h // 2))
                mm(bk, ones_b[0:64, :], PT[1][0:64, :], False, True, ["ones_b", "PT1"], pres(4 + h // 2))
            S.add("dve", lambda e, o=small[:, 12:16], i=ps[:, 4 * 512:6 * 512].rearrange("p (h c) -> p h c", c=256):
                  e.reduce_max(out=o, in_=i, axis=mybir.AxisListType.X), pres(4) + pres(5), ["kmt4"])
            tt("dve", kmax2[:, :], kmax2[:, :], small[:, 12:16], ALU.max, ["kmt4", "kmax2"], ["kmax2"])
            act(g1b, QnT3[:, :, :], AF.Square, QN, ["G1"])
            for h in range(4):
                b_ = h % 2
                for jj in range(2):
                    c0 = 512 + jj * 256 + h * 64
                    for k in range(2):
                        mm(ps[0:64, (b_ * 2 + jj) * 512:(b_ * 2 + jj) * 512 + 256], wqb[:, k, c0:c0 + 64], cq3[:, k, :], k == 0, k == 1,
                           ["cqTb", WQ[k]], pres(b_ * 2 + jj))
                tt("dve", t1, ps[0:64, (b_ * 2) * 512:(b_ * 2) * 512 + 256], Ct[0:64, :], ALU.mult, pres(b_ * 2) + ["G6"], ["G3b"])
                tt("dve", t2, ps[0:64, (b_ * 2 + 1) * 512:(b_ * 2 + 1) * 512 + 256], Sg[0:64, :], ALU.mult, pres(b_ * 2 + 1) + ["G6"], ["G4a"])
                tt("dve", t1, t1, t2, ALU.add, ["G3b", "G4a"], ["G3b"])
                tt("dve", QpeT3[0:64, h, :], t1, rq_bc[0:64, :], ALU.mult, ["G3b", "G5a"], ["QpeT%d" % h])
            if m == 0:
                S.mark("upproj")
            act(g0b[0:64, :, :], QpeT3[0:64, :, :], AF.Square, QP + ["G0"], ["G0"])
            for h in range(4):
                bk = bank(6 + h // 2, (h % 2) * 256, (h % 2) * 256 + 256)
                mm(bk, ones_b[:, :], g1b[:, h, :], True, False, ["ones_b", "G1"], pres(6 + h // 2))
                mm(bk, ones_b[0:64, :], g0b[0:64, h, :], False, True, ["ones_b", "G0"], pres(6 + h // 2))
            for h in range(4):
                ts("dve", QpeT3[64:65, h, :], ps[64:65, (6 + h // 2) * 512 + (h % 2) * 256:(6 + h // 2) * 512 + (h % 2) * 256 + 256],
                   kmax2[64:65, h:h + 1], -0.525, ALU.add, ALU.mult, pres(6 + h // 2) + ["kmax2"], [QP[h]])
            if m == 0:
                tap("QnT", QnT[:, :], ["QnT%d" % h for h in range(4)])
                tap("QpeT", QpeT[:, :], ["QpeT%d" % h for h in range(4)])
                tap("KnT", KnT[:, 0, 0:256], [kres(0) + "n0", kres(1) + "n0"])
                tap("KpeT", KpeT[:, 0:256], [kres(0) + "pe", kres(1) + "pe", "KpeT_ones"])
                tap("V0", Vc[:, 0, :], [kres(0) + "v"])

            if m == 0:
                S.mark("stab")
            def attention_units(m=m, t0=t0):
                nk = t0 + 2
                for h in range(4):
                    obr, rbr = pres(6), pres(7)

                    def emit_st(kt, h=h):
                        lo = 128 if kt == t0 + 1 else 0
                        masked = kt >= t0
                        sb_ = 4 + kt % 2
                        sps = bank(sb_, lo, 256)
                        spr = pres(sb_)
                        kr = [kres(kt) + "n%d" % h, kres(kt) + "pe", "KpeT_ones"]
                        ksl = slice(kt * 128, (kt + 1) * 128)
                        mm(sps, KnT[:, h, ksl], QnT3[:, h, lo:256], True, False, kr + ["QnT%d" % h], spr)
                        mm(sps, KpeT[0:65, ksl], QpeT3[0:65, h, lo:256], False, not masked, kr + ["QpeT%d" % h], spr)
                        if masked:
                            mm(bank(sb_, lo, lo + 128), ident_b, amask_b, False, True, ["ident_b", "amask_b"], spr)
                        act(PT[kt % 3][:, lo:256], sps, AF.Exp, spr, ["PT%d" % (kt % 3)], scale=SCALE)

                    def emit_pv(kt, h=h):
                        lo = 128 if kt == t0 + 1 else 0
                        pt = PT[kt % 3]
                        ptn = "PT%d" % (kt % 3)
                        mm(bank(6, lo, 256), Vc[:, kt, h * 128:(h + 1) * 128], pt[:, lo:256], kt == 0, kt == nk - 1,
                           [kres(kt) + "v", ptn], obr)
                        mm(bank(7, lo, 256), ones_b[:, :], pt[:, lo:256], kt == 0, kt == nk - 1, ["ones_b", ptn], rbr)

                    emit_st(0)
                    for kt in range(nk):
                        if kt + 1 < nk:
                            emit_st(kt + 1)
                        emit_pv(kt)
                        if kt == nk - 1:
                            cp("dve", rinv[:, :], bank(7, 0, 256), rbr, ["rinv"])
                            act(oacc[:, :], bank(6, 0, 256), AF.Copy, obr, ["oacc"])
                            recip(rinv[:, :], rinv[:, :], ["rinv"], ["rinv"])
                            tt("dve", oT3[:, 4 + h, :], oacc[:, :], rinv[:, :], ALU.mult, ["oacc", "rinv"], ["oTa"])
                        yield

            att = attention_units()
            n_units = 4 * (t0 + 2)
            per_pump = -(-n_units // 8)

            def pump(n=per_pump):
                for _ in range(n):
                    try:
                        next(att)
                    except StopIteration:
                        return

            for tt_ in range(2):
                t = t0 + tt_
                tsl = slice(tt_ * 128, (tt_ + 1) * 128)
                T1, T2, T3, GT = G[0], G[1], G[2], G[3]
                F1, F2, F3 = G[4], G[5], G[6]
                for g in range(3):
                    for k in range(8):
                        mm(bank(g), uT3[:, k, tsl], winb[:, k, 1152 + g * 512:1152 + (g + 1) * 512], k == 0, k == 7,
                           ["uT", WIN[k]], pres(g))
                act(T1[:, :], bank(0), AF.Sigmoid, pres(0), ["G0"], scale=-1.0)
                act(vb[:, :], bank(1), AF.Copy, pres(1), ["vb"])
                act(GT[:, :], bank(2), AF.Sigmoid, pres(2), ["G3a", "G3b"])
                tt("dve", T1[:, :], T1[:, :], oml_bc[:, :], ALU.mult, ["G0", "oml_bc"], ["G0"])
                act(T2[:, :], T1[:, :], AF.Ln, ["G0"], ["G1"], scale=-1.0, bias=1.0)
                tt("dve", GT[:, :], GT[:, :], bank(2), ALU.mult, pres(2) + ["G3a", "G3b"], ["G3a", "G3b"])
                pump()
                mm(bank(3), bdlast, T2[:, :], True, True, ["cst", "G1"], pres(3))
                for h in range(4):
                    mm(bank(h // 2, (h % 2) * 256, (h % 2) * 256 + 256), T2[:, h * 128:(h + 1) * 128], bd12, True, True,
                       ["G1", "cst"], pres(h // 2))
                for h in range(4):
                    tr(bank(2, h * 128, (h + 1) * 128), T1[:, h * 128:(h + 1) * 128], ident_f, ["G0", "cst"], pres(2))
                act(T3[:, :], bank(3), AF.Exp, pres(3), ["G2a", "G2b"])
                e12 = ps[:, 0:1024].rearrange("p (h c) -> p h c", c=256)
                E12R = pres(0) + pres(1)
                act(v3(F2[:, :], 128), e12[:, :, 0:128], AF.Exp, E12R, ["G5a", "G5b"], scale=-1.0)
                act(v3(F1[:, :], 128), e12[:, :, 0:128], AF.Exp, E12R, ["G4a", "G4b"])
                act(v3(F3[:, :], 128), e12[:, :, 128:256], AF.Exp, E12R, ["G6"])
                tt("dve", khat[:, :], T1[:, :], T3[:, :], ALU.mult, ["G0", "G2a", "G2b"], ["khat"])
                tt("dve", ktil[:, :], bank(2), F2[:, :], ALU.mult, pres(2) + ["G5a", "G5b"], ["ktil"])
                qv = qTs3[:, :, tsl]
                tt("pool", v3(qtil[:, :], 128), qv, v3(F1[:, :], 128), ALU.mult, ["qTs", "G4a", "G4b"], ["qtil"])
                tt("pool", v3(qh0[:, :], 128)[:, :, 0:64], qTs3[:, :, tt_ * 128:tt_ * 128 + 64], v3(F3[:, :], 128)[:, :, 0:64],
                   ALU.mult, ["qTs", "G6"], ["qh0"])
                tt("pool", v3(qh1[:, :], 128)[:, :, 64:128], qTs3[:, :, tt_ * 128 + 64:tt_ * 128 + 128], v3(F3[:, :], 128)[:, :, 64:128],
                   ALU.mult, ["qTs", "G6"], ["qh1"])
                pump()
                for ci in range(2):
                    for h in range(4):
                        mm(bank(ci, h * 128, (h + 1) * 128), khat[ci * 64:(ci + 1) * 64, h * 128:(h + 1) * 128],
                           vb[ci * 64:(ci + 1) * 64, h * 128:(h + 1) * 128], True, True, ["khat", "vb"], pres(ci))
                for h in range(4):
                    mm(bank(3, h * 128, (h + 1) * 128), ktil[:, h * 128:(h + 1) * 128], qtil[:, h * 128:(h + 1) * 128], True, True,
                       ["ktil", "qtil"], pres(3))
                for h in range(4):
                    hs = slice(h * 128, (h + 1) * 128)
                    stt(SA[:, hs], Sst[:, hs], F3[:, h * 128 + 63:h * 128 + 64], bank(0, h * 128, (h + 1) * 128), ALU.mult, ALU.add,
                        ["Sst", "G6"] + pres(0), ["SA%d" % h])
                    cp("pool", S1b[:, hs], SA[:, hs], ["SA%d" % h], ["S1b%d" % h])
                for h in range(4):
                    tt("dve", ATm[:, h * 128:(h + 1) * 128], bank(3, h * 128, (h + 1) * 128), hmask, ALU.mult, pres(3) + ["cst"], ["ATm"])
                pump()
                for h in range(4):
                    hs = slice(h * 128, (h + 1) * 128)
                    mm(bank(2, h * 128, (h + 1) * 128), ATm[:, hs], vb[:, hs], True, False, ["ATm", "vb"], pres(2))
                    mm(bank(2, h * 128, (h + 1) * 128), qh0[:, hs], S0b[:, hs], False, False, ["qh0", "S0b"], pres(2))
                    mm(bank(2, h * 128, (h + 1) * 128), qh1[:, hs], S1b[:, hs], False, True, ["qh1", "S1b%d" % h], pres(2))
                for h in range(4):
                    act(T3[:, h * 128:(h + 1) * 128], bank(2, h * 128, (h + 1) * 128), AF.Square, pres(2), ["G2a", "G2b"],
                        accum_out=ssq[:, h:h + 1])
                act(rso, ssq, AF.Ln, ["G2a", "G2b"], ["rso"], scale=1.0 / 128.0, bias=1e-6)
                act(rso, rso, AF.Exp, ["rso"], ["rso"], scale=-0.5)
                for h in range(4):
                    hs = slice(h * 128, (h + 1) * 128)
                    stt(ohn[:, hs], bank(2, h * 128, (h + 1) * 128), rso[:, h:h + 1], GT[:, hs], ALU.mult, ALU.mult,
                        pres(2) + ["rso", "G3a", "G3b"], ["ohn"])
                for h in range(4):
                    hs = slice(h * 128, (h + 1) * 128)
                    stt(Sst[:, hs], SA[:, hs], F3[:, h * 128 + 127:h * 128 + 128], bank(1, h * 128, (h + 1) * 128), ALU.mult, ALU.add,
                        ["SA%d" % h, "G6"] + pres(1), ["Sst"])
                cp("pool", S0b[:, :], Sst[:, :], ["Sst"], ["S0b"])
                pump()
                pbf = bank(3).bitcast(BF16)
                for h in range(4):
                    tr(pbf[:, h * 128:(h + 1) * 128], ohn[:, h * 128:(h + 1) * 128], ident_b, ["ohn", "ident_b"], pres(3))
                act(oT3[:, 0:4, tsl], v3(pbf[:, 0:512], 128), AF.Copy, pres(3), ["oTh"])
                if m == 0 and tt_ == 0:
                    tap("khat", khat[:, :], ["khat"])
            pump(10 ** 6)
            if m == 0:
                tap("oThg", oT[:, 0:1024], ["oTh"])
                tap("oT", oT[:, :], ["oTh", "oTa"])

            if m == 0:
                S.mark("attn")
            if m + 1 < n_macro:
                emit_rope(m + 1)
                xl_evac = emit_xload(m + 1)
            else:
                xl_evac = []
            while wout_scale:
                wout_scale.pop(0)()

            def ln1_stats(c, XP3=XP3, xTn=xTn):
                mm(bank(2, 0, 256), onesm_f[:, :], XP3[:, c, :], c == 0, c == 7, ["onesm_f", xTn + "c%d" % c], pres(2, 0))
                mm(bank(3, 0, 256), onesm_f[:, :], G[c % 2][:, 0:256], c == 0, c == 7, ["onesm_f", "G%d" % (c % 2)], pres(3, 0))

            for c in range(8):
                b_, hf_ = c % 2, 0
                for k in range(8):
                    mm(bank(b_, 0, 256), woutb[:, k, c * 128:(c + 1) * 128], oT3[:, k, :], k == 0, k == 7, ["oTh", "oTa", WOUT[k]], pres(b_, 0))
                stt(XP3[:, c, :], bank(b_, 0, 256), g1a[:, c:c + 1], XP3[:, c, :], ALU.mult, ALU.add, pres(b_, 0) + ["mod1", xTn + "c%d" % c], [xTn + "c%d" % c])
                ysq = G[c % 2][:, 0:256]
                act(ysq, XP3[:, c, :], AF.Square, [xTn + "c%d" % c], ["G%d" % (c % 2)])
                for _ in range(3):
                    if xl_evac:
                        xl_evac.pop(0)()
                if c >= 1:
                    ln1_stats(c - 1)
            ln1_stats(7)
            while xl_evac:
                xl_evac.pop(0)()
            if m == 0:
                tap("y1", xT[:, :], XC(xTn))
            mean_s, rstd_s = G[5][:, 0:256], G[5][:, 256:512]
            cp("dve", mean_s, bank(2, 0, 256), pres(2, 0), ["G5a"])
            tt("dve", rstd_s, mean_s, mean_s, ALU.mult, ["G5a"], ["G5b"])
            tt("dve", rstd_s, bank(3, 0, 256), rstd_s, ALU.subtract, pres(3, 0) + ["G5b"], ["G5b"])
            act(rstd_s, rstd_s, AF.Ln, ["G5b"], ["G5b"], bias=1e-5)
            act(rstd_s, rstd_s, AF.Exp, ["G5b"], ["G5b"], scale=-0.5)
            stt(mean_s, mean_s, -1.0, rstd_s, ALU.mult, ALU.mult, ["G5a", "G5b"], ["G5a"])
            for c in range(8):
                tt("dve", XP3[:, c, :], XP3[:, c, :], rstd_s, ALU.mult, [xTn + "c%d" % c, "G5b"], [xTn + "c%d" % c])
                tt("pool", XP3[:, c, :], XP3[:, c, :], mean_s, ALU.add, [xTn + "c%d" % c, "G5a"], [xTn + "c%d" % c])
                ts("pool", XP3[:, c, :], XP3[:, c, :], lnca[:, c:c + 1], lnca[:, 8 + c:9 + c], ALU.mult, ALU.add, [xTn + "c%d" % c, "lnca"], [xTn + "c%d" % c])
            if m == 0:
                tap("x1", xT[:, :], XC(xTn))
            dma("pool", v3(x1s_d, T)[:, :, msl], XP3, XC(xTn), [("x1s", m)], key="x1st%d" % m)

        final_keys = list(dbg_keys)
        if do_phase2:
            S.barrier(lambda e: e.dma_start(out=bar_d[:, 0:32], in_=cst_d[0:1, 0:32]))
            for k in range(8):
                dma("pool", w1b[:, k, :].rearrange("p (a b) -> p a b", b=1024), w1_d[k * 128:(k + 1) * 128, :].rearrange("p (a b) -> p a b", b=1024), [], ["w1b%d" % k])
            for f4 in range(8):
                dma("pool", w2b[:, f4 * 4:(f4 + 1) * 4, :], w2_d[f4 * 512:(f4 + 1) * 512, :].rearrange("(f p) c -> p f c", p=128), [],
                    ["w2b%d" % f4])
            XB = [xT3, xTb3]
            XF = [xT, xTb]
            stg = [xin0, qTs]

            def p2_load(m):
                dma("sp", XB[m % 2], v3(x1s_d, T)[:, :, m * MT:(m + 1) * MT], [("x1s", m)], XC("X%d" % (m % 2)))

            def p2_u(m):
                X = XB[m % 2]
                xn = "X%d" % (m % 2)
                for c in range(8):
                    act(uT3[:, c, :], X[:, c, :], AF.Identity, [xn + "c%d" % c, "modT", "scm"], ["uT"], scale=scm[:, c:c + 1], bias=shm[:, c:c + 1])

            def p2_stage1(m, f0=0, f1=32):
                for f in range(f0, f1):
                    b_ = f % 4
                    for k in range(8):
                        mm(bank(b_, 0, 256), w1b[:, k, f * 128:(f + 1) * 128], uT3[:, k, :], k == 0, k == 7,
                           ["uT", "w1b%d" % k], pres(b_))
                    rl = G[f % 2][:, 0:256]
                    act(rl, bank(b_, 0, 256), AF.Relu, pres(b_), ["G%d" % (f % 2)])
                    tt("pool", hTbuf(f), rl, rl, ALU.mult, ["G%d" % (f % 2)], ["hT%d" % f])

            def p2_stats(X, xn, c):
                mm(bank(6, 0, 256), onesm_f[:, :], X[:, c, :], c == 0, c == 7, ["onesm_f", xn + "c%d" % c], pres(6))
                mm(bank(7, 0, 256), onesm_f[:, :], G[2 + c % 2][:, 0:256], c == 0, c == 7, ["onesm_f", "G%d" % (2 + c % 2)], pres(7))

            def p2_stage2(m):
                X = XB[m % 2]
                xn = "X%d" % (m % 2)
                for c in range(8):
                    b_ = 4 + c % 2
                    for f in range(32):
                        mm(bank(b_, 0, 256), w2b[:, f, c * 128:(c + 1) * 128], hTbuf(f), f == 0, f == 31,
                           ["hT%d" % f, "w2b%d" % (f // 4)], pres(b_))
                    stt(X[:, c, :], bank(b_, 0, 256), g1m[:, c:c + 1], X[:, c, :], ALU.mult, ALU.add, pres(b_) + ["mod1", xn + "c%d" % c], [xn + "c%d" % c])
                    ysq = G[2 + c % 2][:, 0:256]
                    act(ysq, X[:, c, :], AF.Square, [xn + "c%d" % c], ["G%d" % (2 + c % 2)])
                    if c >= 1:
                        p2_stats(X, xn, c - 1)
                p2_stats(X, xn, 7)

            def p2_tail(m):
                X = XB[m % 2]
                xn = "X%d" % (m % 2)
                mean_s, var_s, rstd_s, nmr_s = G[4][:, 0:256], G[4][:, 256:512], G[5][:, 0:256], G[5][:, 256:512]
                cp("dve", mean_s, bank(6, 0, 256), pres(6), ["G4a"])
                tt("dve", var_s, mean_s, mean_s, ALU.mult, ["G4a"], ["G4b"])
                tt("dve", var_s, bank(7, 0, 256), var_s, ALU.subtract, pres(7) + ["G4b"], ["G4b"])
                act(rstd_s, var_s, AF.Ln, ["G4b"], ["G5a"], bias=1e-5)
                act(rstd_s, rstd_s, AF.Exp, ["G5a"], ["G5a"], scale=-0.5)
                stt(nmr_s, mean_s, -1.0, rstd_s, ALU.mult, ALU.mult, ["G4a", "G5a"], ["G5b"])
                for c in range(8):
                    tt("dve", X[:, c, :], X[:, c, :], rstd_s, ALU.mult, [xn + "c%d" % c, "G5a"], [xn + "c%d" % c])
                    tt("dve", X[:, c, :], X[:, c, :], nmr_s, ALU.add, [xn + "c%d" % c, "G5b"], [xn + "c%d" % c])
                    ts("dve", X[:, c, :], X[:, c, :], lnc[:, 16 + c:17 + c], lnc[:, 24 + c:25 + c], ALU.mult, ALU.add, [xn + "c%d" % c, "lnc"], [xn + "c%d" % c])

            def p2_out(m):
                X = XB[m % 2]
                xn = "X%d" % (m % 2)
                for tt_ in range(2):
                    t = 2 * m + tt_
                    b0 = 4 + 2 * tt_
                    sg = stg[tt_]
                    sn = "stg%d" % tt_
                    for c in range(8):
                        tr(bank(b0 + c // 4, (c % 4) * 128, (c % 4 + 1) * 128), X[:, c, tt_ * 128:(tt_ + 1) * 128], ident_f, [xn + "c%d" % c, "cst"],
                           pres(b0 + c // 4))
                    act(sg[:, 0:512], bank(b0), AF.Copy, pres(b0), [sn])
                    cp("dve", sg[:, 512:1024], bank(b0 + 1), pres(b0 + 1), [sn])
                    dma("sp", out_d[t * 128:(t + 1) * 128, :], sg[:, :], [sn], [("out", t)], key="outst%d" % tt_)

            p2_load(0)
            p2_u(0)
            for m in range(n_macro):
                p2_stage1(m, 0, 12)
                if m >= 1:
                    p2_out(m - 1)
                if m + 1 < n_macro:
                    p2_load(m + 1)
                p2_stage1(m, 12, 32)
                if m + 1 < n_macro:
                    p2_u(m + 1)
                p2_stage2(m)
                p2_tail(m)
            p2_out(n_macro - 1)
            final_keys.append("outst0")
            final_keys.append("outst1")
        else:
            final_keys.extend("x1st%d" % m for m in range(n_macro))
        print("n_ops", len(S.ops), {e: sum(1 for o in S.ops if o.eng == e) for e in S.ENGS})
        S.emit(st, final_wait_keys=set(final_keys) | set("dbg_" + k for k in dbg))
    return nc


def _consts():
    cst = np.zeros((128, 768), np.float32)
    cst[:, 0:128] = np.eye(128, dtype=np.float32)
    s = np.arange(128)[:, None]
    c = np.arange(128)[None, :]
    same = (s // 64) == (c // 64)
    bdb = (same & (s <= c)).astype(np.float32)
    ref = (c // 64) * 64 + 31
    bdref = bdb - (same & (s <= ref)).astype(np.float32)
    bdlast = (same & (s > c)).astype(np.float32)
    cst[:, 128:256] = bdref
    cst[:, 256:384] = bdb
    cst[:, 384:512] = bdlast
    cst[:, 512:640] = bdb
    cst[:, 640:768] = np.where(s <= c, 0.0, -30000.0)
    misc = np.zeros((128, 4), np.float32)
    inv_freq = (1.0 / (10000.0 ** (np.arange(0, 64, 2, dtype=np.float32) / 64.0))).astype(np.float32)
    p = np.arange(128)
    misc[:, 0] = inv_freq[p % 32]
    misc[:, 1] = np.where((p % 64) < 32, -1.0, 1.0)
    return cst, misc


def _cols(v, n):
    return np.ascontiguousarray(np.asarray(v, np.float32).reshape(n, 128).T)


def make_in_maps(x, c, positions, w_ada, b_ada, w_in, hg_lower_bounds, hg_norm_w, mla_q_norm_w, w_q_up,
                 mla_kv_norm_w, w_kv_up, w_out, ln1_g, ln1_b, w_mlp_in, w_mlp_out, ln2_g, ln2_b):
    f = lambda a: np.asarray(a, np.float32)
    w_in0 = f(w_in)[0]
    perm = np.concatenate([np.arange(0, 512), np.arange(2048, 2304), np.arange(2304, 2560), np.arange(2560, 2624),
                           np.arange(2592, 2624), np.arange(2560, 2592),
                           np.arange(512, 1024), np.arange(1024, 1536), np.arange(1536, 2048)])
    w_in_p = np.ascontiguousarray(w_in0[:, perm])
    wq0 = f(w_q_up)[0]
    qn = np.concatenate([np.arange(h * 192, h * 192 + 128) for h in range(4)])
    qp = np.concatenate([np.arange(h * 192 + 128, h * 192 + 192) for h in range(4)])
    qps = np.concatenate([np.concatenate([np.arange(h * 192 + 160, h * 192 + 192), np.arange(h * 192 + 128, h * 192 + 160)])
                          for h in range(4)])
    w_q_p = np.ascontiguousarray(wq0[:, np.concatenate([qn, qp, qps])])
    wkv0 = f(w_kv_up)[0]
    kn = np.concatenate([np.arange(h * 256, h * 256 + 128) for h in range(4)])
    vv = np.concatenate([np.arange(h * 256 + 128, h * 256 + 256) for h in range(4)])
    w_kv_p = np.ascontiguousarray(wkv0[:, np.concatenate([kn, vv])])
    cst, misc = _consts()
    nw = np.concatenate([_cols(f(mla_q_norm_w)[0], 2), _cols(f(mla_kv_norm_w)[0], 2), _cols(f(hg_norm_w)[0], 4)], axis=1)
    lncols = np.concatenate([_cols(f(ln1_g)[0], 8), _cols(f(ln1_b)[0], 8), _cols(f(ln2_g)[0], 8), _cols(f(ln2_b)[0], 8)], axis=1)
    shared = {
        "w_ada": np.ascontiguousarray(f(w_ada)[0]), "bada": _cols(f(b_ada)[0], 48), "w_in": w_in_p,
        "lb": np.ascontiguousarray(f(hg_lower_bounds)),
        "nw": np.ascontiguousarray(nw), "w_q": w_q_p, "w_kv": w_kv_p, "w_out": np.ascontiguousarray(f(w_out)[0]),
        "lncols": np.ascontiguousarray(lncols), "w1": np.ascontiguousarray(f(w_mlp_in)[0]),
        "w2": np.ascontiguousarray(f(w_mlp_out)[0]), "cst": cst, "misc": misc,
    }
    xs = f(x)
    cs = f(c)
    ps_ = np.asarray(positions, np.int32)
    maps = []
    for b in range(8):
        mp = dict(shared)
        mp["x"] = np.ascontiguousarray(xs[b])
        mp["pos"] = np.ascontiguousarray(ps_[b])
        mp["ccol"] = _cols(cs[b], 8)
        maps.append(mp)
    return maps


_NC_CACHE = {}


def kernel(**inputs):
    maps = make_in_maps(**inputs)
    if "nc" not in _NC_CACHE:
        _NC_CACHE["nc"] = build_nc()
    res = run_bass_kernel_spmd(_NC_CACHE["nc"], maps, core_ids=list(range(8)))
    return np.stack([np.asarray(r["out"], np.float32) for r in res.results], axis=0)
```

```python
import contextlib
import math
import numpy as np
import concourse.bass as bass
import concourse.mybir as mybir
from concourse.bass_utils import run_bass_kernel_spmd

F32 = mybir.dt.float32
BF16 = mybir.dt.bfloat16
I32 = mybir.dt.int32
AF = mybir.ActivationFunctionType
ALU = mybir.AluOpType

D = 1024
T = 4096
NT = T // 128
MT = 256
NM = T // MT
DFF = 4096
ALPHA = 2.0 ** 0.25
SCALE = 192.0 ** -0.5
WIN_COLS = 2688
PI = math.pi
PI_SAFE = 3.1415925


class _Op:
    __slots__ = ("id", "eng", "fn", "reads", "writes", "deps", "is_dma", "sem_key", "signal", "count", "group")


class Sched:
    ENGS = ("pe", "act", "dve", "pool", "sp")

    def __init__(self, nc):
        self.nc = nc
        self.ops = []
        self.last_writer = {}
        self.readers = {}
        self.dma_keys = []
        self.default_writer = None
        self.limit = None

    def add(self, eng, fn, reads=(), writes=(), dma=False, sem_key=None):
        if self.limit is not None and len(self.ops) >= self.limit:
            return None
        psr = [("ps", r[1]) for r in list(reads) + list(writes) if isinstance(r, tuple) and r and r[0] == "ps"]
        reads = tuple(r for r in reads if not (isinstance(r, tuple) and r and r[0] == "ps"))
        writes = tuple(dict.fromkeys([w for w in writes if not (isinstance(w, tuple) and w and w[0] == "ps")] + psr))
        op = _Op()
        op.id = len(self.ops)
        op.eng = eng
        op.fn = fn
        op.reads = reads
        op.writes = writes
        op.is_dma = dma
        op.group = sem_key if sem_key is not None else (writes[0] if dma else None)
        op.sem_key = op.group
        op.signal = False
        op.count = 0
        deps = set()
        for r in reads:
            w = self.last_writer.get(r, self.default_writer)
            if w is not None:
                deps.add((w, 0))
        for w_ in writes:
            w = self.last_writer.get(w_, self.default_writer)
            if w is not None:
                deps.add((w, 1))
            for rd in self.readers.get(w_, ()):
                deps.add((rd, 1))
        keep = set()
        for d, kind in deps:
            if d == op.id:
                continue
            dop = self.ops[d]
            if (not dop.is_dma) and (not dma) and dop.eng == eng:
                if eng == "pe":
                    continue
            keep.add(d)
        op.deps = sorted(keep)
        for r in reads:
            self.readers.setdefault(r, []).append(op.id)
        for w_ in writes:
            self.last_writer[w_] = op.id
            self.readers[w_] = []
        if dma and op.sem_key not in self.dma_keys:
            self.dma_keys.append(op.sem_key)
        self.ops.append(op)
        return op

    def mark(self, name):
        print("MARK", name, len(self.ops))

    def barrier(self, fn):
        allres = set(self.last_writer) | set(self.readers)
        op = self.add("sp", fn, reads=(), writes=["__bar%d" % len(self.ops)] + sorted(allres, key=str), dma=True)
        self.default_writer = op.id
        self.last_writer = {}
        self.readers = {}
        return op

    def emit(self, st, final_wait_keys=()):
        nc = self.nc
        ops = self.ops
        for op in ops:
            for d in op.deps:
                ops[d].signal = True
        eng_cnt = {e: 0 for e in self.ENGS}
        dma_cnt = {k: 0 for k in self.dma_keys}
        for op in ops:
            if op.is_dma:
                dma_cnt[op.sem_key] += 16
                op.count = dma_cnt[op.sem_key]
            elif op.signal:
                eng_cnt[op.eng] += 1
                op.count = eng_cnt[op.eng]
        esem = {e: st.enter_context(nc.semaphore("s_" + e)) for e in ("pe", "act", "dve", "pool")}
        dsem = {k: st.enter_context(nc.semaphore("d%d" % i)) for i, k in enumerate(self.dma_keys)}
        block = st.enter_context(nc.Block())
        by_eng = {e: [op for op in ops if op.eng == e] for e in self.ENGS}

        def run(ename, engine):
            seen = {}
            for op in by_eng[ename]:
                need = {}
                for d in op.deps:
                    dop = ops[d]
                    if dop.is_dma:
                        key = ("d", dop.sem_key)
                        sem = dsem[dop.sem_key]
                    else:
                        key = ("e", dop.eng)
                        sem = esem[dop.eng]
                    if dop.count > seen.get(key, 0) and dop.count > need.get(key, (None, 0))[1]:
                        need[key] = (sem, dop.count)
                for key, (sem, cnt) in need.items():
                    engine.wait_ge(sem, cnt)
                    seen[key] = cnt
                ins = op.fn(engine)
                if op.is_dma:
                    ins.then_inc(dsem[op.sem_key], 16)
                elif op.signal:
                    ins.then_inc(esem[op.eng], 1)
            if ename == "sp":
                for k in final_wait_keys:
                    if k in dma_cnt:
                        engine.wait_ge(dsem[k], dma_cnt[k])

        @block.tensor
        def _(e):
            run("pe", e)

        @block.scalar
        def _(e):
            run("act", e)

        @block.vector
        def _(e):
            run("dve", e)

        @block.gpsimd
        def _(e):
            run("pool", e)

        @block.sync
        def _(e):
            run("sp", e)


def build_nc(debug=False, n_macro=NM, do_phase2=True, limit=None):
    nc = bass.Bass("TRN2", target_bir_lowering=False, dynamic_dma_scratch_size=4096)

    def din(name, shape, dt=F32):
        return nc.dram_tensor(name, list(shape), dt, kind="ExternalInput").ap()

    x_d = din("x", [T, D])
    pos_d = din("pos", [T], I32)
    ccol_d = din("ccol", [128, 8])
    wada_d = din("w_ada", [D, 6 * D])
    bada_d = din("bada", [128, 48])
    win_d = din("w_in", [D, WIN_COLS])
    lb_d = din("lb", [2, 512])
    nw_d = din("nw", [128, 8])
    wq_d = din("w_q", [256, 1024])
    wkv_d = din("w_kv", [256, 1024])
    wout_d = din("w_out", [D, D])
    ln_d = din("lncols", [128, 32])
    w1_d = din("w1", [D, DFF])
    w2_d = din("w2", [DFF, D])
    cst_d = din("cst", [128, 768])
    misc_d = din("misc", [128, 4])
    out_d = nc.dram_tensor("out", [T, D], F32, kind="ExternalOutput").ap()
    x1s_d = nc.dram_tensor("x1s", [128, 8 * T], F32, kind="Internal").ap()
    bar_d = nc.dram_tensor("bar", [1, 64], F32, kind="Internal").ap()
    dbg = {}
    if debug:
        for nm, shp in debug.items():
            dbg[nm] = nc.dram_tensor("dbg_" + nm, list(shp), F32, kind="ExternalOutput").ap()

    S = Sched(nc)
    S.limit = limit
    st = contextlib.ExitStack()
    with st:
        def sb(name, shape, dt=F32):
            return st.enter_context(nc.sbuf_tensor("sb_" + name, list(shape), dt))

        ps = st.enter_context(nc.psum_tensor("ps", [128, 4096], F32))

        def bank(b, lo=0, hi=512):
            return ps[:, b * 512 + lo:b * 512 + hi]

        def pres(b, half=None):
            if half is None:
                return [("ps", b, 0), ("ps", b, 1)]
            return [("ps", b, half)]

        arena = sb("arena", [128, 73728], BF16)
        a_off = [0]

        def carve(n):
            o = a_off[0]
            a_off[0] += n
            return arena[:, o:o + n]

        KnT = carve(4 * T).rearrange("p (h t) -> p h t", t=T)
        Vc = carve(NT * 512).rearrange("p (t c) -> p t c", c=512)
        KpeT = carve(T)
        winb = carve(8 * WIN_COLS).rearrange("p (k c) -> p k c", c=WIN_COLS)
        wqb = carve(2 * 1024).rearrange("p (k c) -> p k c", c=1024)
        wkvb = carve(2 * 1024).rearrange("p (k c) -> p k c", c=1024)
        woutb = carve(8 * 1024).rearrange("p (k c) -> p k c", c=1024)
        w1b = arena[:, 0:8 * DFF].rearrange("p (k c) -> p k c", c=DFF)
        w2b = arena[:, 8 * DFF:8 * DFF + 32 * D].rearrange("p (f c) -> p f c", c=D)
        hT3 = arena[:, 65536:73728].rearrange("p (f t) -> p f t", t=MT)

        def hTbuf(f):
            return hT3[:, f, :]
        wadab = [arena[:, i * 6144:(i + 1) * 6144].rearrange("p (k c) -> p k c", c=768) for i in range(2)]

        cst = sb("cst", [128, 512])
        ident_f = cst[:, 0:128]
        bd12 = cst[:, 128:384]
        bdlast = cst[:, 384:512]
        hmask = cst[:, 256:384]
        cstb = sb("cstb", [128, 256], BF16)
        ident_b = cstb[:, 0:128]
        amask_b = cstb[:, 128:256]
        ones_f = sb("ones_f", [128, 128])
        onesm_f = sb("onesm_f", [128, 128])
        ones_b = sb("ones_b", [128, 128], BF16)
        misc = sb("misc", [128, 4])
        invf = misc[:, 0:1]
        sgn = misc[:, 1:2]
        oml_bc = sb("oml_bc", [128, 512])
        ccol = sb("ccol", [128, 8])
        condb = sb("condb", [128, 8], BF16)
        bada = sb("bada", [128, 48])
        modT = sb("modT", [128, 48])
        mod1 = sb("mod1", [128, 48])
        nw = sb("nw", [128, 8])
        lnc = sb("lnc", [128, 32])
        kmax2 = sb("kmax2", [128, 4])
        lnca = sb("lnca", [128, 16])
        scm = sb("scm", [128, 8])
        small = sb("small", [128, 16])

        xT = sb("xT", [128, 2048])
        xT3 = xT[:, :].rearrange("p (c t) -> p c t", t=MT)
        xTb = sb("xTb", [128, 2048])
        xTb3 = xTb[:, :].rearrange("p (c t) -> p c t", t=MT)
        uT = sb("uT", [128, 2048], BF16)
        uT3 = uT[:, :].rearrange("p (c t) -> p c t", t=MT)
        qTs = sb("qTs", [128, 1024])
        qTs3 = qTs[:, :].rearrange("p (h t) -> p h t", t=MT)
        cqTb = sb("cqTb", [128, 512], BF16)
        ckvTb = sb("ckvTb", [128, 512], BF16)
        G = [sb("G%d" % i, [128, 512]) for i in range(7)]
        khat = carve(512)
        vb = carve(512)
        ohn = carve(512)
        ktil = carve(512)
        qtil = carve(512)
        qh0 = carve(512)
        qh1 = sb("qh1", [128, 512], BF16)
        ATm = sb("ATm", [128, 512], BF16)
        Sst = sb("Sst", [128, 512])
        SA = sb("SA", [128, 512])
        S0b = sb("S0b", [128, 512], BF16)
        S1b = sb("S1b", [128, 512], BF16)
        QnT = sb("QnT", [128, 1024], BF16)
        QnT3 = QnT[:, :].rearrange("p (h t) -> p h t", t=MT)
        QpeT = sb("QpeT", [128, 1024], BF16)
        QpeT3 = QpeT[:, :].rearrange("p (h t) -> p h t", t=MT)
        oT = sb("oT", [128, 2048], BF16)
        oT3 = oT[:, :].rearrange("p (c t) -> p c t", t=MT)
        xin1 = sb("xin1", [128, 1024])
        xin0 = sb("xin0", [128, 1024])
        xin = xin0
        rinv = sb("rinv", [128, 256])
        oacc = sb("oacc", [128, 256])
        PT = [sb("PT%d" % i, [128, 256], BF16) for i in range(3)]
        posi = sb("posi", [128, 256], I32)
        kfi = posi
        ssq = small[:, 0:4]
        rso = small[:, 4:8]
        rkc = small[:, 8:10]
        kmt = small[:, 10:11]

        def v3(ap, inner):
            return ap.rearrange("p (a b) -> p a b", b=inner)

        def XC(name, cs=range(8)):
            return ["%sc%d" % (name, c) for c in cs]

        def dma(eng, out, in_, reads, writes, key=None, **kw):
            return S.add(eng, lambda e: e.dma_start(out=out, in_=in_, **kw), reads, writes, dma=True, sem_key=key)

        def mm(out, lhsT, rhs, start, stop, reads, writes):
            return S.add("pe", lambda e: e.matmul(out, lhsT=lhsT, rhs=rhs, start=start, stop=stop), reads, writes)

        def tr(out, in_, ident, reads, writes):
            return S.add("pe", lambda e: e.transpose(out=out, in_=in_, identity=ident), reads, writes)

        def act(out, in_, func, reads, writes, **kw):
            return S.add("act", lambda e: e.activation(out=out, in_=in_, func=func, **kw), reads, writes)

        def tt(eng, out, in0, in1, op, reads, writes):
            return S.add(eng, lambda e: e.tensor_tensor(out=out, in0=in0, in1=in1, op=op), reads, writes)

        def ts(eng, out, in0, s1, s2, op0, op1, reads, writes):
            if op1 is None:
                return S.add(eng, lambda e: e.tensor_scalar(out=out, in0=in0, scalar1=s1, scalar2=None, op0=op0), reads, writes)
            return S.add(eng, lambda e: e.tensor_scalar(out=out, in0=in0, scalar1=s1, scalar2=s2, op0=op0, op1=op1), reads, writes)

        def stt(out, in0, scalar, in1, op0, op1, reads, writes):
            return S.add("dve", lambda e: e.scalar_tensor_tensor(out=out, in0=in0, scalar=scalar, in1=in1, op0=op0, op1=op1), reads, writes)

        def cp(eng, out, in_, reads, writes):
            return S.add(eng, lambda e: e.tensor_copy(out=out, in_=in_), reads, writes)

        def mset(eng, ap, val, writes):
            return S.add(eng, lambda e: e.memset(ap, val), (), writes)

        def recip(out, in_, reads, writes):
            return S.add("dve", lambda e: e.reciprocal(out=out, in_=in_), reads, writes)

        dbg_keys = []

        def tap(name, src, reads):
            if debug and name in dbg:
                dma("pool", dbg[name], src, reads, ["dbgout_" + name], key="dbg_" + name)
                dbg_keys.append("dbg_" + name)

        dma("sp", cst[:, :], cst_d[:, 0:512], [], ["cst"])
        dma("sp", misc[:, :], misc_d, [], ["misc"])
        dma("sp", ccol[:, :], ccol_d, [], ["ccol"])
        dma("sp", bada[:, :], bada_d, [], ["bada"])
        dma("sp", nw[:, :], nw_d, [], ["nw"])
        dma("sp", lnc[:, :], ln_d, [], ["lnc"])
        dma("sp", xT[:, 0:1024], lb_d.rearrange("a c -> (a c)").partition_broadcast(128), [], XC("xT0"))
        dma("pool", ident_b, cst_d[:, 0:128], [], ["ident_b"])
        dma("pool", amask_b, cst_d[:, 640:768], [], ["amask_b"])
        mset("dve", ones_f[:, :], 1.0, ["ones_f"])
        mset("dve", onesm_f[:, :], 1.0 / 1024.0, ["onesm_f"])
        mset("dve", ones_b[:, :], 1.0, ["ones_b"])
        mset("dve", Sst[:, :], 0.0, ["Sst"])
        mset("dve", S0b[:, :], 0.0, ["S0b"])
        mset("dve", qh0[:, :], 0.0, ["qh0"])
        mset("dve", qh1[:, :], 0.0, ["qh1"])
        mset("dve", kmax2[:, :], 0.0, ["kmax2"])
        mset("dve", QpeT[:, :], 0.0, ["QpeT"])

        act(condb[:, :], ccol[:, :], AF.Silu, ["ccol"], ["condb"])
        for ch in range(8):
            wb = wadab[ch % 2]
            wname = "wada%d" % (ch % 2)
            dma("pool", wb, wada_d[:, ch * 768:(ch + 1) * 768].rearrange("(k p) c -> p k c", p=128), [], [wname])
            for jj in range(6):
                j = ch * 6 + jj
                for k in range(8):
                    mm(bank(0, j, j + 1), wb[:, k, jj * 128:(jj + 1) * 128], condb[:, k:k + 1], k == 0, k == 7,
                       [wname, "condb"], pres(0))
        tt("dve", modT[:, :], bank(0, 0, 48), bada[:, :], ALU.add, pres(0) + ["bada"], ["modT"])
        ts("dve", mod1[:, :], modT[:, :], 1.0, None, ALU.add, None, ["modT"], ["mod1"])
        ts("dve", lnca[:, :], lnc[:, 0:16], ALPHA, None, ALU.mult, None, ["lnc"], ["lnca"])
        ts("dve", scm[:, :], mod1[:, 32:40], 1.0 / ALPHA, None, ALU.mult, None, ["mod1"], ["scm"])
        sha, sc1a, g1a = modT[:, 0:8], mod1[:, 8:16], mod1[:, 16:24]
        shm, sc1m, g1m = modT[:, 24:32], mod1[:, 32:40], mod1[:, 40:48]
        tap("modT", modT[:, :], ["modT"])

        tt("dve", G[0][:, :], xT[:, 0:512], xT[:, 512:1024], ALU.subtract, XC("xT0"), ["G0"])
        act(G[0][:, :], G[0][:, :], AF.Sigmoid, ["G0"], ["G0"])
        ts("dve", oml_bc[:, :], G[0][:, :], -1.0, 1.0, ALU.mult, ALU.add, ["G0"], ["oml_bc"])

        for k in range(8):
            dma("pool", winb[:, k, :].rearrange("p (a b) -> p a b", b=1344), win_d[k * 128:(k + 1) * 128, :].rearrange("p (a b) -> p a b", b=1344), [], ["winb%d" % k])
        for k in range(2):
            dma("pool", wqb[:, k, :], wq_d[k * 128:(k + 1) * 128, :], [], ["wqb%d" % k])
            dma("pool", wkvb[:, k, :], wkv_d[k * 128:(k + 1) * 128, :], [], ["wkvb%d" % k])
            act(wqb[:, k, :], wqb[:, k, :], AF.Identity, ["wqb%d" % k, "nw"], ["wqb%d" % k], scale=nw[:, k:k + 1])
            act(wkvb[:, k, :], wkvb[:, k, :], AF.Identity, ["wkvb%d" % k, "nw"], ["wkvb%d" % k], scale=nw[:, 2 + k:3 + k])
        wout_scale = []
        for k in range(8):
            dma("pool", woutb[:, k, :], wout_d[k * 128:(k + 1) * 128, :], [], ["woutb%d" % k])
            if k < 4:
                wout_scale.append(lambda k=k: act(woutb[:, k, :], woutb[:, k, :], AF.Identity, ["woutb%d" % k, "nw"], ["woutb%d" % k],
                                                  scale=nw[:, 4 + k:5 + k]))
        WIN = ["winb%d" % k for k in range(8)]
        WQ = ["wqb0", "wqb1"]
        WKV = ["wkvb0", "wkvb1"]
        WOUT = ["woutb%d" % k for k in range(8)]
        mset("dve", KpeT[64:65, :], 1.0, ["KpeT_ones"])

        def kres(t):
            return "K%d" % t


        for m in range(n_macro):
            t0 = 2 * m
            msl = slice(m * MT, (m + 1) * MT)
            extraK = ["wada0", "wada1"] if m == 0 else []
            XP3 = xT3 if m % 2 == 0 else xTb3
            xTn = "xT%d" % (m % 2)
            def emit_xload(m):
                t0 = 2 * m
                XP3 = xT3 if m % 2 == 0 else xTb3
                xTn = "xT%d" % (m % 2)
                evac = []
                for tt_ in range(2):
                    tsl = slice(tt_ * 128, (tt_ + 1) * 128)
                    xs_, xn_ = (xin0, "xin0") if tt_ == 0 else (xin1, "xin1")
                    for hb in range(2):
                        xb_ = 4 + 2 * tt_ + hb
                        for c4 in range(4):
                            c = hb * 4 + c4
                            tr(bank(xb_, c4 * 128, (c4 + 1) * 128), xs_[:, c * 128:(c + 1) * 128], ident_f, [xn_, "cst"], pres(xb_))
                        evac.append(lambda hb=hb, tsl=tsl, xb_=xb_: act(
                            XP3[:, hb * 4:(hb + 1) * 4, tsl], v3(bank(xb_), 128), AF.Copy, pres(xb_), XC(xTn, range(hb * 4, hb * 4 + 4)), scale=ALPHA))
                        for c4 in range(4):
                            c = hb * 4 + c4
                            evac.append(lambda c=c, c4=c4, tsl=tsl, xb_=xb_: act(
                                uT3[:, c, tsl], bank(xb_, c4 * 128, (c4 + 1) * 128), AF.Identity, pres(xb_) + ["modT", "mod1"], ["uT"],
                                scale=sc1a[:, c:c + 1], bias=sha[:, c:c + 1]))
                if m + 1 < n_macro:
                    dma("sp", xin0[:, :], x_d[(t0 + 2) * 128:(t0 + 3) * 128, :], [], ["xin0"])
                    dma("sp", xin1[:, :], x_d[(t0 + 3) * 128:(t0 + 4) * 128, :], [], ["xin1"])
                return evac

            if m == 0:
                dma("sp", xin0[:, :], x_d[0:128, :], [], ["xin0"])
                dma("sp", xin1[:, :], x_d[128:256, :], [], ["xin1"])
                for f_ in emit_xload(0):
                    f_()
            if m == 0:
                tap("uT", uT[:, :], ["uT"])
            if m == 0:
                S.mark("xload")
            def emit_rope(m):
                dma("sp", posi[:, :], pos_d[m * MT:(m + 1) * MT].partition_broadcast(128), [], ["posi"])
                ang, tmpa, tmpb = G[2][:, 0:256], G[2][:, 256:512], G[3][:, 0:256]
                Ct, Sg = G[6][:, 0:256], G[6][:, 256:512]
                cp("dve", tmpa, posi[:, :], ["posi"], ["G2b"])
                ts("dve", ang, tmpa, invf, None, ALU.mult, None, ["G2b", "misc"], ["G2a"])
                ts("dve", kfi[:, :], ang, 1.0 / (2 * PI), None, ALU.mult, None, ["G2a"], ["posi"])
                cp("dve", tmpa, kfi[:, :], ["posi"], ["G2b"])
                stt(ang, tmpa, -2.0 * PI, ang, ALU.mult, ALU.add, ["G2a", "G2b"], ["G2a"])
                ts("dve", tmpa, ang, PI, None, ALU.is_gt, None, ["G2a"], ["G2b"])
                stt(tmpa, tmpa, -2.0 * PI, ang, ALU.mult, ALU.add, ["G2a", "G2b"], ["G2b"])
                ts("dve", tmpa, tmpa, -PI_SAFE, PI_SAFE, ALU.max, ALU.min, ["G2b"], ["G2b"])
                act(Sg, tmpa, AF.Sin, ["G2b", "misc"], ["G6"], scale=sgn)
                ts("dve", tmpb, ang, PI / 2, None, ALU.add, None, ["G2a"], ["G3a"])
                ts("dve", tmpa, tmpb, PI, None, ALU.is_gt, None, ["G3a"], ["G2b"])
                stt(tmpb, tmpa, -2.0 * PI, tmpb, ALU.mult, ALU.add, ["G3a", "G2b"], ["G3a"])
                ts("dve", tmpb, tmpb, -PI_SAFE, PI_SAFE, ALU.max, ALU.min, ["G3a"], ["G3a"])
                act(Ct, tmpb, AF.Sin, ["G3a"], ["G6"])


            if m == 0:
                emit_rope(0)
            Ct, Sg = G[6][:, 0:256], G[6][:, 256:512]
            if m == 0:
                S.mark("rope")
            sq = v3(G[0][:, :], 256)
            sq2 = v3(G[1][:, :], 256)
            cq3 = v3(cqTb[:, :], 256)
            ckv3 = v3(ckvTb[:, :], 256)
            pb = 0
            for j in range(8):
                b_, hf_ = 2 + j % 2, 0
                for k in range(8):
                    mm(bank(b_, hf_ * 256, hf_ * 256 + 256), winb[:, k, j * 128:(j + 1) * 128], uT3[:, k, :], k == 0, k == 7,
                       ["uT", WIN[k]], pres(b_, hf_))
                src = bank(b_, hf_ * 256, hf_ * 256 + 256)
                if j < 4:
                    act(qTs3[:, j, :], src, AF.Copy, pres(b_, hf_), ["qTs"])
                elif j < 6:
                    act(sq[:, j - 4, :], src, AF.Square, pres(b_, hf_), ["G0"])
                    cp("dve", cq3[:, j - 4, :], src, pres(b_, hf_), ["cqTb"])
                else:
                    act(sq2[:, j - 6, :], src, AF.Square, pres(b_, hf_), ["G1"])
                    cp("dve", ckv3[:, j - 6, :], src, pres(b_, hf_), ["ckvTb"])
            if m == 0:
                S.mark("winfm")
            for jj in range(2):
                for k in range(8):
                    mm(ps[0:64, (2 + jj) * 512:(2 + jj) * 512 + 256], winb[:, k, 1024 + jj * 64:1024 + (jj + 1) * 64], uT3[:, k, :],
                       k == 0, k == 7, ["uT", WIN[k]], pres(2 + jj))
            t1, t2 = G[3][0:64, 256:512], G[4][0:64, 0:256]
            tt("dve", t1, ps[0:64, 1024:1280], Ct[0:64, :], ALU.mult, pres(2) + ["G6"], ["G3b"])
            tt("dve", t2, ps[0:64, 1536:1792], Sg[0:64, :], ALU.mult, pres(3) + ["G6"], ["G4a"])
            tt("dve", KpeT[0:64, msl], t1, t2, ALU.add, ["G3b", "G4a"], [kres(t0) + "pe", kres(t0 + 1) + "pe"] + extraK)
            if m == 0:
                S.mark("kpe")
            rq_bc, rk_bc = G[5][:, 0:256], G[5][:, 256:512]
            for i, (sqx, dst, nm) in enumerate(((sq, rq_bc, "G5a"), (sq2, rk_bc, "G5b"))):
                for k in range(2):
                    mm(bank(i, 0, 256), ones_f[:, :], sqx[:, k, :], k == 0, k == 1, ["ones_f", "G%d" % i], pres(i))
                act(dst, bank(i, 0, 256), AF.Ln, pres(i), [nm], scale=1.0 / 256.0, bias=1e-6)
                act(dst, dst, AF.Exp, [nm], [nm], scale=-0.5)
            for tt_ in range(2):
                for k in range(2):
                    mm(bank(2, tt_, tt_ + 1), sq2[:, k, tt_ * 128:(tt_ + 1) * 128], ones_f[:, 0:1], k == 0, k == 1,
                       ["G1", "ones_f"], pres(2))
            act(rkc, bank(2, 0, 2), AF.Ln, pres(2), ["rkc"], scale=1.0 / 256.0, bias=1e-6)
            act(rkc, rkc, AF.Exp, ["rkc"], ["rkc"], scale=-0.5)
            if m == 0:
                S.mark("rstd")
            for h in range(4):
                b_, hf_ = h % 2, 0
                for k in range(2):
                    mm(bank(b_, 0, 256), wkvb[:, k, h * 128:(h + 1) * 128], ckv3[:, k, :], k == 0, k == 1, ["ckvTb", WKV[k]], pres(b_, 0))
                tt("dve", KnT[:, h, msl], bank(b_, 0, 256), rk_bc, ALU.mult, pres(b_, 0) + ["G5b"],
                   [kres(t0) + "n%d" % h, kres(t0 + 1) + "n%d" % h] + extraK)
                for k in range(2):
                    mm(bank(2 + b_, 0, 256), wqb[:, k, h * 128:(h + 1) * 128], cq3[:, k, :], k == 0, k == 1, ["cqTb", WQ[k]], pres(2 + b_))
                tt("dve", QnT3[:, h, :], bank(2 + b_, 0, 256), rq_bc, ALU.mult, pres(2 + b_) + ["G5a"], ["QnT%d" % h])
            for tt_ in range(2):
                for k in range(2):
                    mm(bank(4 + tt_), ckv3[:, k, tt_ * 128:(tt_ + 1) * 128], wkvb[:, k, 512:1024], k == 0, k == 1, ["ckvTb", WKV[k]], pres(4 + tt_))
                act(Vc[:, t0 + tt_, :], bank(4 + tt_), AF.Identity, pres(4 + tt_) + ["rkc"], [kres(t0 + tt_) + "v"] + extraK, scale=rkc[:, tt_:tt_ + 1])
            for h in range(4):
                b_ = h % 2
                for jj in range(2):
                    c0 = 512 + jj * 256 + h * 64
                    for k in range(2):
                        mm(ps[0:64, (b_ * 2 + jj) * 512:(b_ * 2 + jj) * 512 + 256], wqb[:, k, c0:c0 + 64], cq3[:, k, :], k == 0, k == 1,
                           ["cqTb", WQ[k]], pres(b_ * 2 + jj))
                tt("dve", t1, ps[0:64, (b_ * 2) * 512:(b_ * 2) * 512 + 256], Ct[0:64, :], ALU.mult, pres(b_ * 2) + ["G6"], ["G3b"])
                tt("dve", t2, ps[0:64, (b_ * 2 + 1) * 512:(b_ * 2 + 1) * 512 + 256], Sg[0:64, :], ALU.mult, pres(b_ * 2 + 1) + ["G6"], ["G4a"])
                tt("dve", t1, t1, t2, ALU.add, ["G3b", "G4a"], ["G3b"])
                tt("dve", QpeT3[0:64, h, :], t1, rq_bc[0:64, :], ALU.mult, ["G3b", "G5a"], ["QpeT%d" % h])
            if m == 0:
                S.mark("upproj")
            g0b = G[0][:, :].bitcast(BF16).rearrange("p (h t) -> p h t", t=MT)
            g1b = G[1][:, :].bitcast(BF16).rearrange("p (h t) -> p h t", t=MT)
            KN = [kres(t0) + "n%d" % h for h in range(4)] + [kres(t0 + 1) + "n%d" % h for h in range(4)]
            QN = ["QnT%d" % h for h in range(4)]
            QP = ["QpeT%d" % h for h in range(4)]
            act(g0b, KnT[:, :, msl], AF.Square, KN, ["G0"])
            act(PT[1][0:64, :], KpeT[0:64, msl], AF.Square, [kres(t0) + "pe", kres(t0 + 1) + "pe"], ["PT1"])
            for h in range(4):
                bk = bank(4 + h // 2, (h % 2) * 256, (h % 2) * 256 + 256)
                mm(bk, ones_b[:, :], g0b[:, h, :], True, False, ["ones_b", "G0"], pres(4 + h // 2))
                mm(bk, ones_b[0:64, :], PT[1][0:64, :], False, True, ["ones_b", "PT1"], pres(4 + h // 2))
            S.add("dve", lambda e, o=small[:, 12:16], i=ps[:, 4 * 512:6 * 512].rearrange("p (h c) -> p h c", c=256):
                  e.reduce_max(out=o, in_=i, axis=mybir.AxisListType.X), pres(4) + pres(5), ["kmt4"])
            tt("dve", kmax2[:, :], kmax2[:, :], small[:, 12:16], ALU.max, ["kmt4", "kmax2"], ["kmax2"])
            act(g1b, QnT3[:, :, :], AF.Square, QN, ["G1"])
            act(g0b[0:64, :, :], QpeT3[0:64, :, :], AF.Square, QP + ["G0"], ["G0"])
            for h in range(4):
                bk = bank(6 + h // 2, (h % 2) * 256, (h % 2) * 256 + 256)
                mm(bk, ones_b[:, :], g1b[:, h, :], True, False, ["ones_b", "G1"], pres(6 + h // 2))
                mm(bk, ones_b[0:64, :], g0b[0:64, h, :], False, True, ["ones_b", "G0"], pres(6 + h // 2))
            for h in range(4):
                ts("dve", QpeT3[64:65, h, :], ps[64:65, (6 + h // 2) * 512 + (h % 2) * 256:(6 + h // 2) * 512 + (h % 2) * 256 + 256],
                   kmax2[64:65, h:h + 1], -0.525, ALU.add, ALU.mult, pres(6 + h // 2) + ["kmax2"], [QP[h]])
            if m == 0:
                tap("QnT", QnT[:, :], ["QnT%d" % h for h in range(4)])
                tap("QpeT", QpeT[:, :], ["QpeT%d" % h for h in range(4)])
                tap("KnT", KnT[:, 0, 0:256], [kres(0) + "n0", kres(1) + "n0"])
                tap("KpeT", KpeT[:, 0:256], [kres(0) + "pe", kres(1) + "pe", "KpeT_ones"])
                tap("V0", Vc[:, 0, :], [kres(0) + "v"])

            if m == 0:
                S.mark("stab")
            def attention_units(m=m, t0=t0):
                nk = t0 + 2
                for h in range(4):
                    obr, rbr = pres(6), pres(7)

                    def emit_st(kt, h=h):
                        lo = 128 if kt == t0 + 1 else 0
                        masked = kt >= t0
                        sb_ = 4 + kt % 2
                        sps = bank(sb_, lo, 256)
                        spr = pres(sb_)
                        kr = [kres(kt) + "n%d" % h, kres(kt) + "pe", "KpeT_ones"]
                        ksl = slice(kt * 128, (kt + 1) * 128)
                        mm(sps, KnT[:, h, ksl], QnT3[:, h, lo:256], True, False, kr + ["QnT%d" % h], spr)
                        mm(sps, KpeT[0:65, ksl], QpeT3[0:65, h, lo:256], False, not masked, kr + ["QpeT%d" % h], spr)
                        if masked:
                            mm(bank(sb_, lo, lo + 128), ident_b, amask_b, False, True, ["ident_b", "amask_b"], spr)
                        act(PT[kt % 3][:, lo:256], sps, AF.Exp, spr, ["PT%d" % (kt % 3)], scale=SCALE)

                    def emit_pv(kt, h=h):
                        lo = 128 if kt == t0 + 1 else 0
                        pt = PT[kt % 3]
                        ptn = "PT%d" % (kt % 3)
                        mm(bank(6, lo, 256), Vc[:, kt, h * 128:(h + 1) * 128], pt[:, lo:256], kt == 0, kt == nk - 1,
                           [kres(kt) + "v", ptn], obr)
                        mm(bank(7, lo, 256), ones_b[:, :], pt[:, lo:256], kt == 0, kt == nk - 1, ["ones_b", ptn], rbr)

                    emit_st(0)
                    for kt in range(nk):
                        if kt + 1 < nk:
                            emit_st(kt + 1)
                        emit_pv(kt)
                        if kt == nk - 1:
                            cp("dve", rinv[:, :], bank(7, 0, 256), rbr, ["rinv"])
                            act(oacc[:, :], bank(6, 0, 256), AF.Copy, obr, ["oacc"])
                            recip(rinv[:, :], rinv[:, :], ["rinv"], ["rinv"])
                            tt("dve", oT3[:, 4 + h, :], oacc[:, :], rinv[:, :], ALU.mult, ["oacc", "rinv"], ["oTa"])
                        yield

            att = attention_units()
            n_units = 4 * (t0 + 2)
            per_pump = -(-n_units // 8)

            def pump(n=per_pump):
                for _ in range(n):
                    try:
                        next(att)
                    except StopIteration:
                        return

            for tt_ in range(2):
                t = t0 + tt_
                tsl = slice(tt_ * 128, (tt_ + 1) * 128)
                T1, T2, T3, GT = G[0], G[1], G[2], G[3]
                F1, F2, F3 = G[4], G[5], G[6]
                for g in range(3):
                    for k in range(8):
                        mm(bank(g), uT3[:, k, tsl], winb[:, k, 1152 + g * 512:1152 + (g + 1) * 512], k == 0, k == 7,
                           ["uT", WIN[k]], pres(g))
                act(T1[:, :], bank(0), AF.Sigmoid, pres(0), ["G0"], scale=-1.0)
                act(vb[:, :], bank(1), AF.Copy, pres(1), ["vb"])
                act(GT[:, :], bank(2), AF.Sigmoid, pres(2), ["G3a", "G3b"])
                tt("dve", T1[:, :], T1[:, :], oml_bc[:, :], ALU.mult, ["G0", "oml_bc"], ["G0"])
                act(T2[:, :], T1[:, :], AF.Ln, ["G0"], ["G1"], scale=-1.0, bias=1.0)
                tt("dve", GT[:, :], GT[:, :], bank(2), ALU.mult, pres(2) + ["G3a", "G3b"], ["G3a", "G3b"])
                pump()
                mm(bank(3), bdlast, T2[:, :], True, True, ["cst", "G1"], pres(3))
                for h in range(4):
                    mm(bank(h // 2, (h % 2) * 256, (h % 2) * 256 + 256), T2[:, h * 128:(h + 1) * 128], bd12, True, True,
                       ["G1", "cst"], pres(h // 2))
                for h in range(4):
                    tr(bank(2, h * 128, (h + 1) * 128), T1[:, h * 128:(h + 1) * 128], ident_f, ["G0", "cst"], pres(2))
                act(T3[:, :], bank(3), AF.Exp, pres(3), ["G2a", "G2b"])
                e12 = ps[:, 0:1024].rearrange("p (h c) -> p h c", c=256)
                E12R = pres(0) + pres(1)
                act(v3(F2[:, :], 128), e12[:, :, 0:128], AF.Exp, E12R, ["G5a", "G5b"], scale=-1.0)
                act(v3(F1[:, :], 128), e12[:, :, 0:128], AF.Exp, E12R, ["G4a", "G4b"])
                act(v3(F3[:, :], 128), e12[:, :, 128:256], AF.Exp, E12R, ["G6"])
                tt("dve", khat[:, :], T1[:, :], T3[:, :], ALU.mult, ["G0", "G2a", "G2b"], ["khat"])
                tt("dve", ktil[:, :], bank(2), F2[:, :], ALU.mult, pres(2) + ["G5a", "G5b"], ["ktil"])
                qv = qTs3[:, :, tsl]
                tt("pool", v3(qtil[:, :], 128), qv, v3(F1[:, :], 128), ALU.mult, ["qTs", "G4a", "G4b"], ["qtil"])
                tt("pool", v3(qh0[:, :], 128)[:, :, 0:64], qTs3[:, :, tt_ * 128:tt_ * 128 + 64], v3(F3[:, :], 128)[:, :, 0:64],
                   ALU.mult, ["qTs", "G6"], ["qh0"])
                tt("pool", v3(qh1[:, :], 128)[:, :, 64:128], qTs3[:, :, tt_ * 128 + 64:tt_ * 128 + 128], v3(F3[:, :], 128)[:, :, 64:128],
                   ALU.mult, ["qTs", "G6"], ["qh1"])
                pump()
                for ci in range(2):
                    for h in range(4):
                        mm(bank(ci, h * 128, (h + 1) * 128), khat[ci * 64:(ci + 1) * 64, h * 128:(h + 1) * 128],
                           vb[ci * 64:(ci + 1) * 64, h * 128:(h + 1) * 128], True, True, ["khat", "vb"], pres(ci))
                for h in range(4):
                    mm(bank(3, h * 128, (h + 1) * 128), ktil[:, h * 128:(h + 1) * 128], qtil[:, h * 128:(h + 1) * 128], True, True,
                       ["ktil", "qtil"], pres(3))
                for h in range(4):
                    hs = slice(h * 128, (h + 1) * 128)
                    stt(SA[:, hs], Sst[:, hs], F3[:, h * 128 + 63:h * 128 + 64], bank(0, h * 128, (h + 1) * 128), ALU.mult, ALU.add,
                        ["Sst", "G6"] + pres(0), ["SA%d" % h])
                    cp("pool", S1b[:, hs], SA[:, hs], ["SA%d" % h], ["S1b%d" % h])
                for h in range(4):
                    tt("dve", ATm[:, h * 128:(h + 1) * 128], bank(3, h * 128, (h + 1) * 128), hmask, ALU.mult, pres(3) + ["cst"], ["ATm"])
                pump()
                for h in range(4):
                    hs = slice(h * 128, (h + 1) * 128)
                    mm(bank(2, h * 128, (h + 1) * 128), ATm[:, hs], vb[:, hs], True, False, ["ATm", "vb"], pres(2))
                    mm(bank(2, h * 128, (h + 1) * 128), qh0[:, hs], S0b[:, hs], False, False, ["qh0", "S0b"], pres(2))
                    mm(bank(2, h * 128, (h + 1) * 128), qh1[:, hs], S1b[:, hs], False, True, ["qh1", "S1b%d" % h], pres(2))
                for h in range(4):
                    act(T3[:, h * 128:(h + 1) * 128], bank(2, h * 128, (h + 1) * 128), AF.Square, pres(2), ["G2a", "G2b"],
                        accum_out=ssq[:, h:h + 1])
                act(rso, ssq, AF.Ln, ["G2a", "G2b"], ["rso"], scale=1.0 / 128.0, bias=1e-6)
                act(rso, rso, AF.Exp, ["rso"], ["rso"], scale=-0.5)
                for h in range(4):
                    hs = slice(h * 128, (h + 1) * 128)
                    stt(ohn[:, hs], bank(2, h * 128, (h + 1) * 128), rso[:, h:h + 1], GT[:, hs], ALU.mult, ALU.mult,
                        pres(2) + ["rso", "G3a", "G3b"], ["ohn"])
                for h in range(4):
                    hs = slice(h * 128, (h + 1) * 128)
                    stt(Sst[:, hs], SA[:, hs], F3[:, h * 128 + 127:h * 128 + 128], bank(1, h * 128, (h + 1) * 128), ALU.mult, ALU.add,
                        ["SA%d" % h, "G6"] + pres(1), ["Sst"])
                cp("pool", S0b[:, :], Sst[:, :], ["Sst"], ["S0b"])
                pump()
                pbf = bank(3).bitcast(BF16)
                for h in range(4):
                    tr(pbf[:, h * 128:(h + 1) * 128], ohn[:, h * 128:(h + 1) * 128], ident_b, ["ohn", "ident_b"], pres(3))
                act(oT3[:, 0:4, tsl], v3(pbf[:, 0:512], 128), AF.Copy, pres(3), ["oTh"])
                if m == 0 and tt_ == 0:
                    tap("khat", khat[:, :], ["khat"])
            pump(10 ** 6)
            if m == 0:
                tap("oThg", oT[:, 0:1024], ["oTh"])
                tap("oT", oT[:, :], ["oTh", "oTa"])

            if m == 0:
                S.mark("attn")
            if m + 1 < n_macro:
                emit_rope(m + 1)
                xl_evac = emit_xload(m + 1)
            else:
                xl_evac = []
            while wout_scale:
                wout_scale.pop(0)()

            def ln1_stats(c, XP3=XP3, xTn=xTn):
                mm(bank(2, 0, 256), onesm_f[:, :], XP3[:, c, :], c == 0, c == 7, ["onesm_f", xTn + "c%d" % c], pres(2, 0))
                mm(bank(3, 0, 256), onesm_f[:, :], G[c % 2][:, 0:256], c == 0, c == 7, ["onesm_f", "G%d" % (c % 2)], pres(3, 0))

            for c in range(8):
                b_, hf_ = c % 2, 0
                for k in range(8):
                    mm(bank(b_, 0, 256), woutb[:, k, c * 128:(c + 1) * 128], oT3[:, k, :], k == 0, k == 7, ["oTh", "oTa", WOUT[k]], pres(b_, 0))
                stt(XP3[:, c, :], bank(b_, 0, 256), g1a[:, c:c + 1], XP3[:, c, :], ALU.mult, ALU.add, pres(b_, 0) + ["mod1", xTn + "c%d" % c], [xTn + "c%d" % c])
                ysq = G[c % 2][:, 0:256]
                act(ysq, XP3[:, c, :], AF.Square, [xTn + "c%d" % c], ["G%d" % (c % 2)])
                for _ in range(3):
                    if xl_evac:
                        xl_evac.pop(0)()
                if c >= 1:
                    ln1_stats(c - 1)
            ln1_stats(7)
            while xl_evac:
                xl_evac.pop(0)()
            if m == 0:
                tap("y1", xT[:, :], XC(xTn))
            mean_s, rstd_s = G[5][:, 0:256], G[5][:, 256:512]
            cp("dve", mean_s, bank(2, 0, 256), pres(2, 0), ["G5a"])
            tt("dve", rstd_s, mean_s, mean_s, ALU.mult, ["G5a"], ["G5b"])
            tt("dve", rstd_s, bank(3, 0, 256), rstd_s, ALU.subtract, pres(3, 0) + ["G5b"], ["G5b"])
            act(rstd_s, rstd_s, AF.Ln, ["G5b"], ["G5b"], bias=1e-5)
            act(rstd_s, rstd_s, AF.Exp, ["G5b"], ["G5b"], scale=-0.5)
            stt(mean_s, mean_s, -1.0, rstd_s, ALU.mult, ALU.mult, ["G5a", "G5b"], ["G5a"])
            for c in range(8):
                tt("dve", XP3[:, c, :], XP3[:, c, :], rstd_s, ALU.mult, [xTn + "c%d" % c, "G5b"], [xTn + "c%d" % c])
                tt("pool", XP3[:, c, :], XP3[:, c, :], mean_s, ALU.add, [xTn + "c%d" % c, "G5a"], [xTn + "c%d" % c])
                ts("pool", XP3[:, c, :], XP3[:, c, :], lnca[:, c:c + 1], lnca[:, 8 + c:9 + c], ALU.mult, ALU.add, [xTn + "c%d" % c, "lnca"], [xTn + "c%d" % c])
            if m == 0:
                tap("x1", xT[:, :], XC(xTn))
            dma("pool", v3(x1s_d, T)[:, :, msl], XP3, XC(xTn), [("x1s", m)], key="x1st%d" % m)

        final_keys = list(dbg_keys)
        if do_phase2:
            S.barrier(lambda e: e.dma_start(out=bar_d[:, 0:32], in_=cst_d[0:1, 0:32]))
            for k in range(8):
                dma("pool", w1b[:, k, :].rearrange("p (a b) -> p a b", b=1024), w1_d[k * 128:(k + 1) * 128, :].rearrange("p (a b) -> p a b", b=1024), [], ["w1b%d" % k])
            for f4 in range(8):
                dma("pool", w2b[:, f4 * 4:(f4 + 1) * 4, :], w2_d[f4 * 512:(f4 + 1) * 512, :].rearrange("(f p) c -> p f c", p=128), [],
                    ["w2b%d" % f4])
            XB = [xT3, xTb3]
            XF = [xT, xTb]
            stg = [xin0, qTs]

            def p2_load(m):
                dma("sp", XB[m % 2], v3(x1s_d, T)[:, :, m * MT:(m + 1) * MT], [("x1s", m)], XC("X%d" % (m % 2)))

            def p2_u(m):
                X = XB[m % 2]
                xn = "X%d" % (m % 2)
                for c in range(8):
                    act(uT3[:, c, :], X[:, c, :], AF.Identity, [xn + "c%d" % c, "modT", "scm"], ["uT"], scale=scm[:, c:c + 1], bias=shm[:, c:c + 1])

            def p2_stage1(m, f0=0, f1=32):
                for f in range(f0, f1):
                    b_ = f % 4
                    for k in range(8):
                        mm(bank(b_, 0, 256), w1b[:, k, f * 128:(f + 1) * 128], uT3[:, k, :], k == 0, k == 7,
                           ["uT", "w1b%d" % k], pres(b_))
                    rl = G[f % 2][:, 0:256]
                    act(rl, bank(b_, 0, 256), AF.Relu, pres(b_), ["G%d" % (f % 2)])
                    tt("pool", hTbuf(f), rl, rl, ALU.mult, ["G%d" % (f % 2)], ["hT%d" % f])

            def p2_stats(X, xn, c):
                mm(bank(6, 0, 256), onesm_f[:, :], X[:, c, :], c == 0, c == 7, ["onesm_f", xn + "c%d" % c], pres(6))
                mm(bank(7, 0, 256), onesm_f[:, :], G[2 + c % 2][:, 0:256], c == 0, c == 7, ["onesm_f", "G%d" % (2 + c % 2)], pres(7))

            def p2_stage2(m):
                X = XB[m % 2]
                xn = "X%d" % (m % 2)
                for c in range(8):
                    b_ = 4 + c % 2
                    for f in range(32):
                        mm(bank(b_, 0, 256), w2b[:, f, c * 128:(c + 1) * 128], hTbuf(f), f == 0, f == 31,
                           ["hT%d" % f, "w2b%d" % (f // 4)], pres(b_))
                    stt(X[:, c, :], bank(b_, 0, 256), g1m[:, c:c + 1], X[:, c, :], ALU.mult, ALU.add, pres(b_) + ["mod1", xn + "c%d" % c], [xn + "c%d" % c])
                    ysq = G[2 + c % 2][:, 0:256]
                    act(ysq, X[:, c, :], AF.Square, [xn + "c%d" % c], ["G%d" % (2 + c % 2)])
                    if c >= 1:
                        p2_stats(X, xn, c - 1)
                p2_stats(X, xn, 7)

            def p2_tail(m):
                X = XB[m % 2]
                xn = "X%d" % (m % 2)
                mean_s, var_s, rstd_s, nmr_s = G[4][:, 0:256], G[4][:, 256:512], G[5][:, 0:256], G[5][:, 256:512]
                cp("dve", mean_s, bank(6, 0, 256), pres(6), ["G4a"])
                tt("dve", var_s, mean_s, mean_s, ALU.mult, ["G4a"], ["G4b"])
                tt("dve", var_s, bank(7, 0, 256), var_s, ALU.subtract, pres(7) + ["G4b"], ["G4b"])
                act(rstd_s, var_s, AF.Ln, ["G4b"], ["G5a"], bias=1e-5)
                act(rstd_s, rstd_s, AF.Exp, ["G5a"], ["G5a"], scale=-0.5)
                stt(nmr_s, mean_s, -1.0, rstd_s, ALU.mult, ALU.mult, ["G4a", "G5a"], ["G5b"])
                for c in range(8):
                    tt("dve", X[:, c, :], X[:, c, :], rstd_s, ALU.mult, [xn + "c%d" % c, "G5a"], [xn + "c%d" % c])
                    tt("dve", X[:, c, :], X[:, c, :], nmr_s, ALU.add, [xn + "c%d" % c, "G5b"], [xn + "c%d" % c])
                    ts("dve", X[:, c, :], X[:, c, :], lnc[:, 16 + c:17 + c], lnc[:, 24 + c:25 + c], ALU.mult, ALU.add, [xn + "c%d" % c, "lnc"], [xn + "c%d" % c])

            def p2_out(m):
                X = XB[m % 2]
                xn = "X%d" % (m % 2)
                for tt_ in range(2):
                    t = 2 * m + tt_
                    b0 = 4 + 2 * tt_
                    sg = stg[tt_]
                    sn = "stg%d" % tt_
                    for c in range(8):
                        tr(bank(b0 + c // 4, (c % 4) * 128, (c % 4 + 1) * 128), X[:, c, tt_ * 128:(tt_ + 1) * 128], ident_f, [xn + "c%d" % c, "cst"],
                           pres(b0 + c // 4))
                    act(sg[:, 0:512], bank(b0), AF.Copy, pres(b0), [sn])
                    cp("dve", sg[:, 512:1024], bank(b0 + 1), pres(b0 + 1), [sn])
                    dma("sp", out_d[t * 128:(t + 1) * 128, :], sg[:, :], [sn], [("out", t)], key="outst%d" % tt_)

            p2_load(0)
            p2_u(0)
            for m in range(n_macro):
                p2_stage1(m, 0, 12)
                if m >= 1:
                    p2_out(m - 1)
                if m + 1 < n_macro:
                    p2_load(m + 1)
                p2_stage1(m, 12, 32)
                if m + 1 < n_macro:
                    p2_u(m + 1)
                p2_stage2(m)
                p2_tail(m)
            p2_out(n_macro - 1)
            final_keys.append("outst0")
            final_keys.append("outst1")
        else:
            final_keys.extend("x1st%d" % m for m in range(n_macro))
        print("n_ops", len(S.ops), {e: sum(1 for o in S.ops if o.eng == e) for e in S.ENGS})
        S.emit(st, final_wait_keys=set(final_keys) | set("dbg_" + k for k in dbg))
    return nc


def _consts():
    cst = np.zeros((128, 768), np.float32)
    cst[:, 0:128] = np.eye(128, dtype=np.float32)
    s = np.arange(128)[:, None]
    c = np.arange(128)[None, :]
    same = (s // 64) == (c // 64)
    bdb = (same & (s <= c)).astype(np.float32)
    ref = (c // 64) * 64 + 31
    bdref = bdb - (same & (s <= ref)).astype(np.float32)
    bdlast = (same & (s > c)).astype(np.float32)
    cst[:, 128:256] = bdref
    cst[:, 256:384] = bdb
    cst[:, 384:512] = bdlast
    cst[:, 512:640] = bdb
    cst[:, 640:768] = np.where(s <= c, 0.0, -30000.0)
    misc = np.zeros((128, 4), np.float32)
    inv_freq = (1.0 / (10000.0 ** (np.arange(0, 64, 2, dtype=np.float32) / 64.0))).astype(np.float32)
    p = np.arange(128)
    misc[:, 0] = inv_freq[p % 32]
    misc[:, 1] = np.where((p % 64) < 32, -1.0, 1.0)
    return cst, misc


def _cols(v, n):
    return np.ascontiguousarray(np.asarray(v, np.float32).reshape(n, 128).T)


def make_in_maps(x, c, positions, w_ada, b_ada, w_in, hg_lower_bounds, hg_norm_w, mla_q_norm_w, w_q_up,
                 mla_kv_norm_w, w_kv_up, w_out, ln1_g, ln1_b, w_mlp_in, w_mlp_out, ln2_g, ln2_b):
    f = lambda a: np.asarray(a, np.float32)
    w_in0 = f(w_in)[0]
    perm = np.concatenate([np.arange(0, 512), np.arange(2048, 2304), np.arange(2304, 2560), np.arange(2560, 2624),
                           np.arange(2592, 2624), np.arange(2560, 2592),
                           np.arange(512, 1024), np.arange(1024, 1536), np.arange(1536, 2048)])
    w_in_p = np.ascontiguousarray(w_in0[:, perm])
    wq0 = f(w_q_up)[0]
    qn = np.concatenate([np.arange(h * 192, h * 192 + 128) for h in range(4)])
    qp = np.concatenate([np.arange(h * 192 + 128, h * 192 + 192) for h in range(4)])
    qps = np.concatenate([np.concatenate([np.arange(h * 192 + 160, h * 192 + 192), np.arange(h * 192 + 128, h * 192 + 160)])
                          for h in range(4)])
    w_q_p = np.ascontiguousarray(wq0[:, np.concatenate([qn, qp, qps])])
    wkv0 = f(w_kv_up)[0]
    kn = np.concatenate([np.arange(h * 256, h * 256 + 128) for h in range(4)])
    vv = np.concatenate([np.arange(h * 256 + 128, h * 256 + 256) for h in range(4)])
    w_kv_p = np.ascontiguousarray(wkv0[:, np.concatenate([kn, vv])])
    cst, misc = _consts()
    nw = np.concatenate([_cols(f(mla_q_norm_w)[0], 2), _cols(f(mla_kv_norm_w)[0], 2), _cols(f(hg_norm_w)[0], 4)], axis=1)
    lncols = np.concatenate([_cols(f(ln1_g)[0], 8), _cols(f(ln1_b)[0], 8), _cols(f(ln2_g)[0], 8), _cols(f(ln2_b)[0], 8)], axis=1)
    shared = {
        "w_ada": np.ascontiguousarray(f(w_ada)[0]), "bada": _cols(f(b_ada)[0], 48), "w_in": w_in_p,
        "lb": np.ascontiguousarray(f(hg_lower_bounds)),
        "nw": np.ascontiguousarray(nw), "w_q": w_q_p, "w_kv": w_kv_p, "w_out": np.ascontiguousarray(f(w_out)[0]),
        "lncols": np.ascontiguousarray(lncols), "w1": np.ascontiguousarray(f(w_mlp_in)[0]),
        "w2": np.ascontiguousarray(f(w_mlp_out)[0]), "cst": cst, "misc": misc,
    }
    xs = f(x)
    cs = f(c)
    ps_ = np.asarray(positions, np.int32)
    maps = []
    for b in range(8):
        mp = dict(shared)
        mp["x"] = np.ascontiguousarray(xs[b])
        mp["pos"] = np.ascontiguousarray(ps_[b])
        mp["ccol"] = _cols(cs[b], 8)
        maps.append(mp)
    return maps


_NC_CACHE = {}


def kernel(**inputs):
    maps = make_in_maps(**inputs)
    if "nc" not in _NC_CACHE:
        _NC_CACHE["nc"] = build_nc()
    res = run_bass_kernel_spmd(_NC_CACHE["nc"], maps, core_ids=list(range(8)))
    return np.stack([np.asarray(r["out"], np.float32) for r in res.results], axis=0)
```

```python
import contextlib
import math
import numpy as np
import concourse.bass as bass
import concourse.mybir as mybir
from concourse.bass_utils import run_bass_kernel_spmd

F32 = mybir.dt.float32
BF16 = mybir.dt.bfloat16
I32 = mybir.dt.int32
AF = mybir.ActivationFunctionType
ALU = mybir.AluOpType

D = 1024
T = 4096
NT = T // 128
MT = 256
NM = T // MT
DFF = 4096
ALPHA = 2.0 ** 0.25
SCALE = 192.0 ** -0.5
WIN_COLS = 2688
PI = math.pi
PI_SAFE = 3.1415925


class _Op:
    __slots__ = ("id", "eng", "fn", "reads", "writes", "deps", "is_dma", "sem_key", "signal", "count", "group")


class Sched:
    ENGS = ("pe", "act", "dve", "pool", "sp")

    def __init__(self, nc):
        self.nc = nc
        self.ops = []
        self.last_writer = {}
        self.readers = {}
        self.dma_keys = []
        self.default_writer = None
        self.limit = None

    def add(self, eng, fn, reads=(), writes=(), dma=False, sem_key=None):
        if self.limit is not None and len(self.ops) >= self.limit:
            return None
        psr = [("ps", r[1]) for r in list(reads) + list(writes) if isinstance(r, tuple) and r and r[0] == "ps"]
        reads = tuple(r for r in reads if not (isinstance(r, tuple) and r and r[0] == "ps"))
        writes = tuple(dict.fromkeys([w for w in writes if not (isinstance(w, tuple) and w and w[0] == "ps")] + psr))
        op = _Op()
        op.id = len(self.ops)
        op.eng = eng
        op.fn = fn
        op.reads = reads
        op.writes = writes
        op.is_dma = dma
        op.group = sem_key if sem_key is not None else (writes[0] if dma else None)
        op.sem_key = op.group
        op.signal = False
        op.count = 0
        deps = set()
        for r in reads:
            w = self.last_writer.get(r, self.default_writer)
            if w is not None:
                deps.add((w, 0))
        for w_ in writes:
            w = self.last_writer.get(w_, self.default_writer)
            if w is not None:
                deps.add((w, 1))
            for rd in self.readers.get(w_, ()):
                deps.add((rd, 1))
        keep = set()
        for d, kind in deps:
            if d == op.id:
                continue
            dop = self.ops[d]
            if (not dop.is_dma) and (not dma) and dop.eng == eng:
                if eng == "pe":
                    continue
            keep.add(d)
        op.deps = sorted(keep)
        for r in reads:
            self.readers.setdefault(r, []).append(op.id)
        for w_ in writes:
            self.last_writer[w_] = op.id
            self.readers[w_] = []
        if dma and op.sem_key not in self.dma_keys:
            self.dma_keys.append(op.sem_key)
        self.ops.append(op)
        return op

    def mark(self, name):
        print("MARK", name, len(self.ops))

    def barrier(self, fn):
        allres = set(self.last_writer) | set(self.readers)
        op = self.add("sp", fn, reads=(), writes=["__bar%d" % len(self.ops)] + sorted(allres, key=str), dma=True)
        self.default_writer = op.id
        self.last_writer = {}
        self.readers = {}
        return op

    def emit(self, st, final_wait_keys=()):
        nc = self.nc
        ops = self.ops
        for op in ops:
            for d in op.deps:
                ops[d].signal = True
        eng_cnt = {e: 0 for e in self.ENGS}
        dma_cnt = {k: 0 for k in self.dma_keys}
        for op in ops:
            if op.is_dma:
                dma_cnt[op.sem_key] += 16
                op.count = dma_cnt[op.sem_key]
            elif op.signal:
                eng_cnt[op.eng] += 1
                op.count = eng_cnt[op.eng]
        esem = {e: st.enter_context(nc.semaphore("s_" + e)) for e in ("pe", "act", "dve", "pool")}
        dsem = {k: st.enter_context(nc.semaphore("d%d" % i)) for i, k in enumerate(self.dma_keys)}
        block = st.enter_context(nc.Block())
        by_eng = {e: [op for op in ops if op.eng == e] for e in self.ENGS}

        def run(ename, engine):
            seen = {}
            for op in by_eng[ename]:
                need = {}
                for d in op.deps:
                    dop = ops[d]
                    if dop.is_dma:
                        key = ("d", dop.sem_key)
                        sem = dsem[dop.sem_key]
                    else:
                        key = ("e", dop.eng)
                        sem = esem[dop.eng]
                    if dop.count > seen.get(key, 0) and dop.count > need.get(key, (None, 0))[1]:
                        need[key] = (sem, dop.count)
                for key, (sem, cnt) in need.items():
                    engine.wait_ge(sem, cnt)
                    seen[key] = cnt
                ins = op.fn(engine)
                if op.is_dma:
                    ins.then_inc(dsem[op.sem_key], 16)
                elif op.signal:
                    ins.then_inc(esem[op.eng], 1)
            if ename == "sp":
                for k in final_wait_keys:
                    if k in dma_cnt:
                        engine.wait_ge(dsem[k], dma_cnt[k])

        @block.tensor
        def _(e):
            run("pe", e)

        @block.scalar
        def _(e):
            run("act", e)

        @block.vector
        def _(e):
            run("dve", e)

        @block.gpsimd
        def _(e):
            run("pool", e)

        @block.sync
        def _(e):
            run("sp", e)


def build_nc(debug=False, n_macro=NM, do_phase2=True, limit=None):
    nc = bass.Bass("TRN2", target_bir_lowering=False, dynamic_dma_scratch_size=4096)

    def din(name, shape, dt=F32):
        return nc.dram_tensor(name, list(shape), dt, kind="ExternalInput").ap()

    x_d = din("x", [T, D])
    pos_d = din("pos", [T], I32)
    ccol_d = din("ccol", [128, 8])
    wada_d = din("w_ada", [D, 6 * D])
    bada_d = din("bada", [128, 48])
    win_d = din("w_in", [D, WIN_COLS])
    lb_d = din("lb", [2, 512])
    nw_d = din("nw", [128, 8])
    wq_d = din("w_q", [256, 1024])
    wkv_d = din("w_kv", [256, 1024])
    wout_d = din("w_out", [D, D])
    ln_d = din("lncols", [128, 32])
    w1_d = din("w1", [D, DFF])
    w2_d = din("w2", [DFF, D])
    cst_d = din("cst", [128, 768])
    misc_d = din("misc", [128, 4])
    out_d = nc.dram_tensor("out", [T, D], F32, kind="ExternalOutput").ap()
    x1s_d = nc.dram_tensor("x1s", [128, 8 * T], F32, kind="Internal").ap()
    bar_d = nc.dram_tensor("bar", [1, 64], F32, kind="Internal").ap()
    dbg = {}
    if debug:
        for nm, shp in debug.items():
            dbg[nm] = nc.dram_tensor("dbg_" + nm, list(shp), F32, kind="ExternalOutput").ap()

    S = Sched(nc)
    S.limit = limit
    st = contextlib.ExitStack()
    with st:
        def sb(name, shape, dt=F32):
            return st.enter_context(nc.sbuf_tensor("sb_" + name, list(shape), dt))

        ps = st.enter_context(nc.psum_tensor("ps", [128, 4096], F32))

        def bank(b, lo=0, hi=512):
            return ps[:, b * 512 + lo:b * 512 + hi]

        def pres(b, half=None):
            if half is None:
                return [("ps", b, 0), ("ps", b, 1)]
            return [("ps", b, half)]

        arena = sb("arena", [128, 73728], BF16)
        a_off = [0]

        def carve(n):
            o = a_off[0]
            a_off[0] += n
            return arena[:, o:o + n]

        KnT = carve(4 * T).rearrange("p (h t) -> p h t", t=T)
        Vc = carve(NT * 512).rearrange("p (t c) -> p t c", c=512)
        KpeT = carve(T)
        winb = carve(8 * WIN_COLS).rearrange("p (k c) -> p k c", c=WIN_COLS)
        wqb = carve(2 * 1024).rearrange("p (k c) -> p k c", c=1024)
        wkvb = carve(2 * 1024).rearrange("p (k c) -> p k c", c=1024)
        woutb = carve(8 * 1024).rearrange("p (k c) -> p k c", c=1024)
        w1b = arena[:, 0:8 * DFF].rearrange("p (k c) -> p k c", c=DFF)
        w2b = arena[:, 8 * DFF:8 * DFF + 32 * D].rearrange("p (f c) -> p f c", c=D)
        hT3 = arena[:, 65536:73728].rearrange("p (f t) -> p f t", t=MT)

        def hTbuf(f):
            return hT3[:, f, :]
        wadab = [arena[:, i * 6144:(i + 1) * 6144].rearrange("p (k c) -> p k c", c=768) for i in range(2)]

        cst = sb("cst", [128, 512])
        ident_f = cst[:, 0:128]
        bd12 = cst[:, 128:384]
        bdlast = cst[:, 384:512]
        hmask = cst[:, 256:384]
        cstb = sb("cstb", [128, 256], BF16)
        ident_b = cstb[:, 0:128]
        amask_b = cstb[:, 128:256]
        ones_f = sb("ones_f", [128, 128])
        onesm_f = sb("onesm_f", [128, 128])
        ones_b = sb("ones_b", [128, 128], BF16)
        misc = sb("misc", [128, 4])
        invf = misc[:, 0:1]
        sgn = misc[:, 1:2]
        oml_bc = sb("oml_bc", [128, 512])
        ccol = sb("ccol", [128, 8])
        condb = sb("condb", [128, 8], BF16)
        bada = sb("bada", [128, 48])
        modT = sb("modT", [128, 48])
        mod1 = sb("mod1", [128, 48])
        nw = sb("nw", [128, 8])
        lnc = sb("lnc", [128, 32])
        kmax2 = sb("kmax2", [128, 4])
        lnca = sb("lnca", [128, 16])
        scm = sb("scm", [128, 8])
        small = sb("small", [128, 16])

        xT = sb("xT", [128, 2048])
        xT3 = xT[:, :].rearrange("p (c t) -> p c t", t=MT)
        xTb = sb("xTb", [128, 2048])
        xTb3 = xTb[:, :].rearrange("p (c t) -> p c t", t=MT)
        uT = sb("uT", [128, 2048], BF16)
        uT3 = uT[:, :].rearrange("p (c t) -> p c t", t=MT)
        qTs = sb("qTs", [128, 1024])
        qTs3 = qTs[:, :].rearrange("p (h t) -> p h t", t=MT)
        cqTb = sb("cqTb", [128, 512], BF16)
        ckvTb = sb("ckvTb", [128, 512], BF16)
        G = [sb("G%d" % i, [128, 512]) for i in range(7)]
        khat = carve(512)
        vb = carve(512)
        ohn = carve(512)
        ktil = carve(512)
        qtil = carve(512)
        qh0 = carve(512)
        qh1 = sb("qh1", [128, 512], BF16)
        ATm = sb("ATm", [128, 512], BF16)
        Sst = sb("Sst", [128, 512])
        SA = sb("SA", [128, 512])
        S0b = sb("S0b", [128, 512], BF16)
        S1b = sb("S1b", [128, 512], BF16)
        QnT = sb("QnT", [128, 1024], BF16)
        QnT3 = QnT[:, :].rearrange("p (h t) -> p h t", t=MT)
        QpeT = sb("QpeT", [128, 1024], BF16)
        QpeT3 = QpeT[:, :].rearrange("p (h t) -> p h t", t=MT)
        oT = sb("oT", [128, 2048], BF16)
        oT3 = oT[:, :].rearrange("p (c t) -> p c t", t=MT)
        xin1 = sb("xin1", [128, 1024])
        xin0 = sb("xin0", [128, 1024])
        xin = xin0
        rinv = sb("rinv", [128, 256])
        oacc = sb("oacc", [128, 256])
        PT = [sb("PT%d" % i, [128, 256], BF16) for i in range(3)]
        posi = sb("posi", [128, 256], I32)
        kfi = posi
        ssq = small[:, 0:4]
        rso = small[:, 4:8]
        rkc = small[:, 8:10]
        kmt = small[:, 10:11]

        def v3(ap, inner):
            return ap.rearrange("p (a b) -> p a b", b=inner)

        def XC(name, cs=range(8)):
            return ["%sc%d" % (name, c) for c in cs]

        def dma(eng, out, in_, reads, writes, key=None, **kw):
            return S.add(eng, lambda e: e.dma_start(out=out, in_=in_, **kw), reads, writes, dma=True, sem_key=key)

        def mm(out, lhsT, rhs, start, stop, reads, writes):
            return S.add("pe", lambda e: e.matmul(out, lhsT=lhsT, rhs=rhs, start=start, stop=stop), reads, writes)

        def tr(out, in_, ident, reads, writes):
            return S.add("pe", lambda e: e.transpose(out=out, in_=in_, identity=ident), reads, writes)

        def act(out, in_, func, reads, writes, **kw):
            return S.add("act", lambda e: e.activation(out=out, in_=in_, func=func, **kw), reads, writes)

        def tt(eng, out, in0, in1, op, reads, writes):
            return S.add(eng, lambda e: e.tensor_tensor(out=out, in0=in0, in1=in1, op=op), reads, writes)

        def ts(eng, out, in0, s1, s2, op0, op1, reads, writes):
            if op1 is None:
                return S.add(eng, lambda e: e.tensor_scalar(out=out, in0=in0, scalar1=s1, scalar2=None, op0=op0), reads, writes)
            return S.add(eng, lambda e: e.tensor_scalar(out=out, in0=in0, scalar1=s1, scalar2=s2, op0=op0, op1=op1), reads, writes)

        def stt(out, in0, scalar, in1, op0, op1, reads, writes):
            return S.add("dve", lambda e: e.scalar_tensor_tensor(out=out, in0=in0, scalar=scalar, in1=in1, op0=op0, op1=op1), reads, writes)

        def cp(eng, out, in_, reads, writes):
            return S.add(eng, lambda e: e.tensor_copy(out=out, in_=in_), reads, writes)

        def mset(eng, ap, val, writes):
            return S.add(eng, lambda e: e.memset(ap, val), (), writes)

        def recip(out, in_, reads, writes):
            return S.add("dve", lambda e: e.reciprocal(out=out, in_=in_), reads, writes)

        dbg_keys = []

        def tap(name, src, reads):
            if debug and name in dbg:
                dma("pool", dbg[name], src, reads, ["dbgout_" + name], key="dbg_" + name)
                dbg_keys.append("dbg_" + name)

        dma("sp", cst[:, :], cst_d[:, 0:512], [], ["cst"])
        dma("sp", misc[:, :], misc_d, [], ["misc"])
        dma("sp", ccol[:, :], ccol_d, [], ["ccol"])
        dma("sp", bada[:, :], bada_d, [], ["bada"])
        dma("sp", nw[:, :], nw_d, [], ["nw"])
        dma("sp", lnc[:, :], ln_d, [], ["lnc"])
        dma("sp", xT[:, 0:1024], lb_d.rearrange("a c -> (a c)").partition_broadcast(128), [], XC("xT0"))
        dma("pool", ident_b, cst_d[:, 0:128], [], ["ident_b"])
        dma("pool", amask_b, cst_d[:, 640:768], [], ["amask_b"])
        mset("dve", ones_f[:, :], 1.0, ["ones_f"])
        mset("dve", onesm_f[:, :], 1.0 / 1024.0, ["onesm_f"])
        mset("dve", ones_b[:, :], 1.0, ["ones_b"])
        mset("dve", Sst[:, :], 0.0, ["Sst"])
        mset("dve", S0b[:, :], 0.0, ["S0b"])
        mset("dve", qh0[:, :], 0.0, ["qh0"])
        mset("dve", qh1[:, :], 0.0, ["qh1"])
        mset("dve", kmax2[:, :], 0.0, ["kmax2"])
        mset("dve", QpeT[:, :], 0.0, ["QpeT"])

        act(condb[:, :], ccol[:, :], AF.Silu, ["ccol"], ["condb"])
        for ch in range(8):
            wb = wadab[ch % 2]
            wname = "wada%d" % (ch % 2)
            dma("pool", wb, wada_d[:, ch * 768:(ch + 1) * 768].rearrange("(k p) c -> p k c", p=128), [], [wname])
            for jj in range(6):
                j = ch * 6 + jj
                for k in range(8):
                    mm(bank(0, j, j + 1), wb[:, k, jj * 128:(jj + 1) * 128], condb[:, k:k + 1], k == 0, k == 7,
                       [wname, "condb"], pres(0))
        tt("dve", modT[:, :], bank(0, 0, 48), bada[:, :], ALU.add, pres(0) + ["bada"], ["modT"])
        ts("dve", mod1[:, :], modT[:, :], 1.0, None, ALU.add, None, ["modT"], ["mod1"])
        ts("dve", lnca[:, :], lnc[:, 0:16], ALPHA, None, ALU.mult, None, ["lnc"], ["lnca"])
        ts("dve", scm[:, :], mod1[:, 32:40], 1.0 / ALPHA, None, ALU.mult, None, ["mod1"], ["scm"])
        sha, sc1a, g1a = modT[:, 0:8], mod1[:, 8:16], mod1[:, 16:24]
        shm, sc1m, g1m = modT[:, 24:32], mod1[:, 32:40], mod1[:, 40:48]
        tap("modT", modT[:, :], ["modT"])

        tt("dve", G[0][:, :], xT[:, 0:512], xT[:, 512:1024], ALU.subtract, XC("xT0"), ["G0"])
        act(G[0][:, :], G[0][:, :], AF.Sigmoid, ["G0"], ["G0"])
        ts("dve", oml_bc[:, :], G[0][:, :], -1.0, 1.0, ALU.mult, ALU.add, ["G0"], ["oml_bc"])

        for k in range(8):
            dma("pool", winb[:, k, :].rearrange("p (a b) -> p a b", b=1344), win_d[k * 128:(k + 1) * 128, :].rearrange("p (a b) -> p a b", b=1344), [], ["winb%d" % k])
        for k in range(2):
            dma("pool", wqb[:, k, :], wq_d[k * 128:(k + 1) * 128, :], [], ["wqb%d" % k])
            dma("pool", wkvb[:, k, :], wkv_d[k * 128:(k + 1) * 128, :], [], ["wkvb%d" % k])
            act(wqb[:, k, :], wqb[:, k, :], AF.Identity, ["wqb%d" % k, "nw"], ["wqb%d" % k], scale=nw[:, k:k + 1])
            act(wkvb[:, k, :], wkvb[:, k, :], AF.Identity, ["wkvb%d" % k, "nw"], ["wkvb%d" % k], scale=nw[:, 2 + k:3 + k])
        wout_scale = []
        for k in range(8):
            dma("pool", woutb[:, k, :], wout_d[k * 128:(k + 1) * 128, :], [], ["woutb%d" % k])
            if k < 4:
                wout_scale.append(lambda k=k: act(woutb[:, k, :], woutb[:, k, :], AF.Identity, ["woutb%d" % k, "nw"], ["woutb%d" % k],
                                                  scale=nw[:, 4 + k:5 + k]))
        WIN = ["winb%d" % k for k in range(8)]
        WQ = ["wqb0", "wqb1"]
        WKV = ["wkvb0", "wkvb1"]
        WOUT = ["woutb%d" % k for k in range(8)]
        mset("dve", KpeT[64:65, :], 1.0, ["KpeT_ones"])

        def kres(t):
            return "K%d" % t


        for m in range(n_macro):
            t0 = 2 * m
            msl = slice(m * MT, (m + 1) * MT)
            extraK = ["wada0", "wada1"] if m == 0 else []
            XP3 = xT3 if m % 2 == 0 else xTb3
            xTn = "xT%d" % (m % 2)
            def emit_xload(m):
                t0 = 2 * m
                XP3 = xT3 if m % 2 == 0 else xTb3
                xTn = "xT%d" % (m % 2)
                evac = []
                for tt_ in range(2):
                    tsl = slice(tt_ * 128, (tt_ + 1) * 128)
                    xs_, xn_ = (xin0, "xin0") if tt_ == 0 else (xin1, "xin1")
                    for hb in range(2):
                        xb_ = 4 + 2 * tt_ + hb
                        for c4 in range(4):
                            c = hb * 4 + c4
                            tr(bank(xb_, c4 * 128, (c4 + 1) * 128), xs_[:, c * 128:(c + 1) * 128], ident_f, [xn_, "cst"], pres(xb_))
                        evac.append(lambda hb=hb, tsl=tsl, xb_=xb_: act(
                            XP3[:, hb * 4:(hb + 1) * 4, tsl], v3(bank(xb_), 128), AF.Copy, pres(xb_), XC(xTn, range(hb * 4, hb * 4 + 4)), scale=ALPHA))
                        for c4 in range(4):
                            c = hb * 4 + c4
                            evac.append(lambda c=c, c4=c4, tsl=tsl, xb_=xb_: act(
                                uT3[:, c, tsl], bank(xb_, c4 * 128, (c4 + 1) * 128), AF.Identity, pres(xb_) + ["modT", "mod1"], ["uT"],
                                scale=sc1a[:, c:c + 1], bias=sha[:, c:c + 1]))
                if m + 1 < n_macro:
                    dma("sp", xin0[:, :], x_d[(t0 + 2) * 128:(t0 + 3) * 128, :], [], ["xin0"])
                    dma("sp", xin1[:, :], x_d[(t0 + 3) * 128:(t0 + 4) * 128, :], [], ["xin1"])
                return evac

            if m == 0:
                dma("sp", xin0[:, :], x_d[0:128, :], [], ["xin0"])
                dma("sp", xin1[:, :], x_d[128:256, :], [], ["xin1"])
                for f_ in emit_xload(0):
                    f_()
            if m == 0:
                tap("uT", uT[:, :], ["uT"])
            if m == 0:
                S.mark("xload")
            def emit_rope(m):
                dma("sp", posi[:, :], pos_d[m * MT:(m + 1) * MT].partition_broadcast(128), [], ["posi"])
                ang, tmpa, tmpb = G[2][:, 0:256], G[2][:, 256:512], G[3][:, 0:256]
                Ct, Sg = G[6][:, 0:256], G[6][:, 256:512]
                cp("dve", tmpa, posi[:, :], ["posi"], ["G2b"])
                ts("dve", ang, tmpa, invf, None, ALU.mult, None, ["G2b", "misc"], ["G2a"])
                ts("dve", kfi[:, :], ang, 1.0 / (2 * PI), None, ALU.mult, None, ["G2a"], ["posi"])
                cp("dve", tmpa, kfi[:, :], ["posi"], ["G2b"])
                stt(ang, tmpa, -2.0 * PI, ang, ALU.mult, ALU.add, ["G2a", "G2b"], ["G2a"])
                ts("dve", tmpa, ang, PI, None, ALU.is_gt, None, ["G2a"], ["G2b"])
                stt(tmpa, tmpa, -2.0 * PI, ang, ALU.mult, ALU.add, ["G2a", "G2b"], ["G2b"])
                ts("dve", tmpa, tmpa, -PI_SAFE, PI_SAFE, ALU.max, ALU.min, ["G2b"], ["G2b"])
                act(Sg, tmpa, AF.Sin, ["G2b", "misc"], ["G6"], scale=sgn)
                ts("dve", tmpb, ang, PI / 2, None, ALU.add, None, ["G2a"], ["G3a"])
                ts("dve", tmpa, tmpb, PI, None, ALU.is_gt, None, ["G3a"], ["G2b"])
                stt(tmpb, tmpa, -2.0 * PI, tmpb, ALU.mult, ALU.add, ["G3a", "G2b"], ["G3a"])
                ts("dve", tmpb, tmpb, -PI_SAFE, PI_SAFE, ALU.max, ALU.min, ["G3a"], ["G3a"])
                act(Ct, tmpb, AF.Sin, ["G3a"], ["G6"])


            if m == 0:
                emit_rope(0)
            Ct, Sg = G[6][:, 0:256], G[6][:, 256:512]
            if m == 0:
                S.mark("rope")
            sq = v3(G[0][:, :], 256)
            sq2 = v3(G[1][:, :], 256)
            cq3 = v3(cqTb[:, :], 256)
            ckv3 = v3(ckvTb[:, :], 256)
            pb = 0
            for j in range(8):
                b_, hf_ = 2 + j % 2, 0
                for k in range(8):
                    mm(bank(b_, hf_ * 256, hf_ * 256 + 256), winb[:, k, j * 128:(j + 1) * 128], uT3[:, k, :], k == 0, k == 7,
                       ["uT", WIN[k]], pres(b_, hf_))
                src = bank(b_, hf_ * 256, hf_ * 256 + 256)
                if j < 4:
                    act(qTs3[:, j, :], src, AF.Copy, pres(b_, hf_), ["qTs"])
                elif j < 6:
                    act(sq[:, j - 4, :], src, AF.Square, pres(b_, hf_), ["G0"])
                    cp("dve", cq3[:, j - 4, :], src, pres(b_, hf_), ["cqTb"])
                else:
                    act(sq2[:, j - 6, :], src, AF.Square, pres(b_, hf_), ["G1"])
                    cp("dve", ckv3[:, j - 6, :], src, pres(b_, hf_), ["ckvTb"])
            if m == 0:
                S.mark("winfm")
            for jj in range(2):
                for k in range(8):
                    mm(ps[0:64, (2 + jj) * 512:(2 + jj) * 512 + 256], winb[:, k, 1024 + jj * 64:1024 + (jj + 1) * 64], uT3[:, k, :],
                       k == 0, k == 7, ["uT", WIN[k]], pres(2 + jj))
            t1, t2 = G[3][0:64, 256:512], G[4][0:64, 0:256]
            tt("dve", t1, ps[0:64, 1024:1280], Ct[0:64, :], ALU.mult, pres(2) + ["G6"], ["G3b"])
            tt("dve", t2, ps[0:64, 1536:1792], Sg[0:64, :], ALU.mult, pres(3) + ["G6"], ["G4a"])
            tt("dve", KpeT[0:64, msl], t1, t2, ALU.add, ["G3b", "G4a"], [kres(t0) + "pe", kres(t0 + 1) + "pe"] + extraK)
            if m == 0:
                S.mark("kpe")
            rq_bc, rk_bc = G[5][:, 0:256], G[5][:, 256:512]
            for i, (sqx, dst, nm) in enumerate(((sq, rq_bc, "G5a"), (sq2, rk_bc, "G5b"))):
                for k in range(2):
                    mm(bank(i, 0, 256), ones_f[:, :], sqx[:, k, :], k == 0, k == 1, ["ones_f", "G%d" % i], pres(i))
                act(dst, bank(i, 0, 256), AF.Ln, pres(i), [nm], scale=1.0 / 256.0, bias=1e-6)
                act(dst, dst, AF.Exp, [nm], [nm], scale=-0.5)
            for tt_ in range(2):
                for k in range(2):
                    mm(bank(2, tt_, tt_ + 1), sq2[:, k, tt_ * 128:(tt_ + 1) * 128], ones_f[:, 0:1], k == 0, k == 1,
                       ["G1", "ones_f"], pres(2))
            act(rkc, bank(2, 0, 2), AF.Ln, pres(2), ["rkc"], scale=1.0 / 256.0, bias=1e-6)
            act(rkc, rkc, AF.Exp, ["rkc"], ["rkc"], scale=-0.5)
            if m == 0:
                S.mark("rstd")
            for h in range(4):
                b_, hf_ = h % 2, 0
                for k in range(2):
                    mm(bank(b_, 0, 256), wkvb[:, k, h * 128:(h + 1) * 128], ckv3[:, k, :], k == 0, k == 1, ["ckvTb", WKV[k]], pres(b_, 0))
                tt("dve", KnT[:, h, msl], bank(b_, 0, 256), rk_bc, ALU.mult, pres(b_, 0) + ["G5b"],
                   [kres(t0) + "n%d" % h, kres(t0 + 1) + "n%d" % h] + extraK)
                for k in range(2):
                    mm(bank(2 + b_, 0, 256), wqb[:, k, h * 128:(h + 1) * 128], cq3[:, k, :], k == 0, k == 1, ["cqTb", WQ[k]], pres(2 + b_))
                tt("dve", QnT3[:, h, :], bank(2 + b_, 0, 256), rq_bc, ALU.mult, pres(2 + b_) + ["G5a"], ["QnT%d" % h])
            for tt_ in range(2):
                for k in range(2):
                    mm(bank(4 + tt_), ckv3[:, k, tt_ * 128:(tt_ + 1) * 128], wkvb[:, k, 512:1024], k == 0, k == 1, ["ckvTb", WKV[k]], pres(4 + tt_))
                act(Vc[:, t0 + tt_, :], bank(4 + tt_), AF.Identity, pres(4 + tt_) + ["rkc"], [kres(t0 + tt_) + "v"] + extraK, scale=rkc[:, tt_:tt_ + 1])
            for h in range(4):
                b_ = h % 2
                for jj in range(2):
                    c0 = 512 + jj * 256 + h * 64
                    for k in range(2):
                        mm(ps[0:64, (b_ * 2 + jj) * 512:(b_ * 2 + jj) * 512 + 256], wqb[:, k, c0:c0 + 64], cq3[:, k, :], k == 0, k == 1,
                           ["cqTb", WQ[k]], pres(b_ * 2 + jj))
                tt("dve", t1, ps[0:64, (b_ * 2) * 512:(b_ * 2) * 512 + 256], Ct[0:64, :], ALU.mult, pres(b_ * 2) + ["G6"], ["G3b"])
                tt("dve", t2, ps[0:64, (b_ * 2 + 1) * 512:(b_ * 2 + 1) * 512 + 256], Sg[0:64, :], ALU.mult, pres(b_ * 2 + 1) + ["G6"], ["G4a"])
                tt("dve", t1, t1, t2, ALU.add, ["G3b", "G4a"], ["G3b"])
                tt("dve", QpeT3[0:64, h, :], t1, rq_bc[0:64, :], ALU.mult, ["G3b", "G5a"], ["QpeT%d" % h])
            if m == 0:
                S.mark("upproj")
            g0b = G[0][:, :].bitcast(BF16).rearrange("p (h t) -> p h t", t=MT)
            g1b = G[1][:, :].bitcast(BF16).rearrange("p (h t) -> p h t", t=MT)
            KN = [kres(t0) + "n%d" % h for h in range(4)] + [kres(t0 + 1) + "n%d" % h for h in range(4)]
            QN = ["QnT%d" % h for h in range(4)]
            QP = ["QpeT%d" % h for h in range(4)]
            act(g0b, KnT[:, :, msl], AF.Square, KN, ["G0"])
            act(PT[1][0:64, :], KpeT[0:64, msl], AF.Square, [kres(t0) + "pe", kres(t0 + 1) + "pe"], ["PT1"])
            for h in range(4):
                bk = bank(4 + h // 2, (h % 2) * 256, (h % 2) * 256 + 256)
                mm(bk, ones_b[:, :], g0b[:, h, :], True, False, ["ones_b", "G0"], pres(4 + h // 2))
                mm(bk, ones_b[0:64, :], PT[1][0:64, :], False, True, ["ones_b", "PT1"], pres(4 + h // 2))
            S.add("dve", lambda e, o=small[:, 12:16], i=ps[:, 4 * 512:6 * 512].rearrange("p (h c) -> p h c", c=256):
                  e.reduce_max(out=o, in_=i, axis=mybir.AxisListType.X), pres(4) + pres(5), ["kmt4"])
            tt("dve", kmax2[:, :], kmax2[:, :], small[:, 12:16], ALU.max, ["kmt4", "kmax2"], ["kmax2"])
            act(g1b, QnT3[:, :, :], AF.Square, QN, ["G1"])
            act(g0b[0:64, :, :], QpeT3[0:64, :, :], AF.Square, QP + ["G0"], ["G0"])
            for h in range(4):
                bk = bank(6 + h // 2, (h % 2) * 256, (h % 2) * 256 + 256)
                mm(bk, ones_b[:, :], g1b[:, h, :], True, False, ["ones_b", "G1"], pres(6 + h // 2))
                mm(bk, ones_b[0:64, :], g0b[0:64, h, :], False, True, ["ones_b", "G0"], pres(6 + h // 2))
            shr = G[2][64:65, :].rearrange("p (a b) -> p a b", b=512)[:, 0, :]
            sh4 = [G[2][64:65, 0:256], G[2][64:65, 256:512], G[3][64:65, 0:256], G[3][64:65, 256:512]]
            for h in range(4):
                ts("dve", sh4[h], ps[64:65, (6 + h // 2) * 512 + (h % 2) * 256:(6 + h // 2) * 512 + (h % 2) * 256 + 256],
                   kmax2[64:65, h:h + 1], 1.1, ALU.mult, ALU.mult, pres(6 + h // 2) + ["kmax2"], ["G2a" if h < 2 else "G3a", "G2b" if h < 2 else "G3b"])
            for gi in (2, 3):
                gn = ["G%da" % gi, "G%db" % gi]
                act(G[gi][64:65, :], G[gi][64:65, :], AF.Ln, gn, gn)
                act(G[gi][64:65, :], G[gi][64:65, :], AF.Exp, gn, gn, scale=0.5)
                ts("dve", QpeT3[64:65, (gi - 2) * 2:(gi - 2) * 2 + 2, :], G[gi][64:65, :].rearrange("p (a b) -> p a b", b=256), -1.0, None,
                   ALU.mult, None, gn, QP[(gi - 2) * 2:(gi - 2) * 2 + 2])
            if m == 0:
                tap("QnT", QnT[:, :], ["QnT%d" % h for h in range(4)])
                tap("QpeT", QpeT[:, :], ["QpeT%d" % h for h in range(4)])
                tap("KnT", KnT[:, 0, 0:256], [kres(0) + "n0", kres(1) + "n0"])
                tap("KpeT", KpeT[:, 0:256], [kres(0) + "pe", kres(1) + "pe", "KpeT_ones"])
                tap("V0", Vc[:, 0, :], [kres(0) + "v"])

            if m == 0:
                S.mark("stab")
            def attention_units(m=m, t0=t0):
                nk = t0 + 2
                for h in range(4):
                    obr, rbr = pres(6), pres(7)

                    def emit_st(kt, h=h):
                        lo = 128 if kt == t0 + 1 else 0
                        masked = kt >= t0
                        sb_ = 4 + kt % 2
                        sps = bank(sb_, lo, 256)
                        spr = pres(sb_)
                        kr = [kres(kt) + "n%d" % h, kres(kt) + "pe", "KpeT_ones"]
                        ksl = slice(kt * 128, (kt + 1) * 128)
                        mm(sps, KnT[:, h, ksl], QnT3[:, h, lo:256], True, False, kr + ["QnT%d" % h], spr)
                        mm(sps, KpeT[0:65, ksl], QpeT3[0:65, h, lo:256], False, not masked, kr + ["QpeT%d" % h], spr)
                        if masked:
                            mm(bank(sb_, lo, lo + 128), ident_b, amask_b, False, True, ["ident_b", "amask_b"], spr)
                        act(PT[kt % 3][:, lo:256], sps, AF.Exp, spr, ["PT%d" % (kt % 3)], scale=SCALE)

                    def emit_pv(kt, h=h):
                        lo = 128 if kt == t0 + 1 else 0
                        pt = PT[kt % 3]
                        ptn = "PT%d" % (kt % 3)
                        mm(bank(6, lo, 256), Vc[:, kt, h * 128:(h + 1) * 128], pt[:, lo:256], kt == 0, kt == nk - 1,
                           [kres(kt) + "v", ptn], obr)
                        mm(bank(7, lo, 256), ones_b[:, :], pt[:, lo:256], kt == 0, kt == nk - 1, ["ones_b", ptn], rbr)

                    emit_st(0)
                    for kt in range(nk):
                        if kt + 1 < nk:
                            emit_st(kt + 1)
                        emit_pv(kt)
                        if kt == nk - 1:
                            cp("dve", rinv[:, :], bank(7, 0, 256), rbr, ["rinv"])
                            act(oacc[:, :], bank(6, 0, 256), AF.Copy, obr, ["oacc"])
                            recip(rinv[:, :], rinv[:, :], ["rinv"], ["rinv"])
                            tt("dve", oT3[:, 4 + h, :], oacc[:, :], rinv[:, :], ALU.mult, ["oacc", "rinv"], ["oTa"])
                        yield

            att = attention_units()
            n_units = 4 * (t0 + 2)
            per_pump = -(-n_units // 8)

            def pump(n=per_pump):
                for _ in range(n):
                    try:
                        next(att)
                    except StopIteration:
                        return

            for tt_ in range(2):
                t = t0 + tt_
                tsl = slice(tt_ * 128, (tt_ + 1) * 128)
                T1, T2, T3, GT = G[0], G[1], G[2], G[3]
                F1, F2, F3 = G[4], G[5], G[6]
                for g in range(3):
                    for k in range(8):
                        mm(bank(g), uT3[:, k, tsl], winb[:, k, 1152 + g * 512:1152 + (g + 1) * 512], k == 0, k == 7,
                           ["uT", WIN[k]], pres(g))
                act(T1[:, :], bank(0), AF.Sigmoid, pres(0), ["G0"], scale=-1.0)
                act(vb[:, :], bank(1), AF.Copy, pres(1), ["vb"])
                act(GT[:, :], bank(2), AF.Sigmoid, pres(2), ["G3a", "G3b"])
                tt("dve", T1[:, :], T1[:, :], oml_bc[:, :], ALU.mult, ["G0", "oml_bc"], ["G0"])
                act(T2[:, :], T1[:, :], AF.Ln, ["G0"], ["G1"], scale=-1.0, bias=1.0)
                tt("dve", GT[:, :], GT[:, :], bank(2), ALU.mult, pres(2) + ["G3a", "G3b"], ["G3a", "G3b"])
                pump()
                mm(bank(3), bdlast, T2[:, :], True, True, ["cst", "G1"], pres(3))
                for h in range(4):
                    mm(bank(h // 2, (h % 2) * 256, (h % 2) * 256 + 256), T2[:, h * 128:(h + 1) * 128], bd12, True, True,
                       ["G1", "cst"], pres(h // 2))
                for h in range(4):
                    tr(bank(2, h * 128, (h + 1) * 128), T1[:, h * 128:(h + 1) * 128], ident_f, ["G0", "cst"], pres(2))
                act(T3[:, :], bank(3), AF.Exp, pres(3), ["G2a", "G2b"])
                e12 = ps[:, 0:1024].rearrange("p (h c) -> p h c", c=256)
                E12R = pres(0) + pres(1)
                act(v3(F2[:, :], 128), e12[:, :, 0:128], AF.Exp, E12R, ["G5a", "G5b"], scale=-1.0)
                act(v3(F1[:, :], 128), e12[:, :, 0:128], AF.Exp, E12R, ["G4a", "G4b"])
                act(v3(F3[:, :], 128), e12[:, :, 128:256], AF.Exp, E12R, ["G6"])
                tt("dve", khat[:, :], T1[:, :], T3[:, :], ALU.mult, ["G0", "G2a", "G2b"], ["khat"])
                tt("dve", F2[:, :], bank(2), F2[:, :], ALU.mult, pres(2) + ["G5a", "G5b"], ["G5a", "G5b"])
                qv = qTs3[:, :, tsl]
                tt("pool", v3(F1[:, :], 128), qv, v3(F1[:, :], 128), ALU.mult, ["qTs", "G4a", "G4b"], ["G4a", "G4b"])
                tt("pool", v3(qh0[:, :], 128)[:, :, 0:64], qTs3[:, :, tt_ * 128:tt_ * 128 + 64], v3(F3[:, :], 128)[:, :, 0:64],
                   ALU.mult, ["qTs", "G6"], ["qh0"])
                tt("pool", v3(qh1[:, :], 128)[:, :, 64:128], qTs3[:, :, tt_ * 128 + 64:tt_ * 128 + 128], v3(F3[:, :], 128)[:, :, 64:128],
                   ALU.mult, ["qTs", "G6"], ["qh1"])
                pump()
                for ci in range(2):
                    for h in range(4):
                        mm(bank(ci, h * 128, (h + 1) * 128), khat[ci * 64:(ci + 1) * 64, h * 128:(h + 1) * 128],
                           vb[ci * 64:(ci + 1) * 64, h * 128:(h + 1) * 128], True, True, ["khat", "vb"], pres(ci))
                for h in range(4):
                    mm(bank(3, h * 128, (h + 1) * 128), F2[:, h * 128:(h + 1) * 128], F1[:, h * 128:(h + 1) * 128], True, True,
                       ["G4a", "G4b", "G5a", "G5b"], pres(3))
                for h in range(4):
                    hs = slice(h * 128, (h + 1) * 128)
                    stt(SA[:, hs], Sst[:, hs], F3[:, h * 128 + 63:h * 128 + 64], bank(0, h * 128, (h + 1) * 128), ALU.mult, ALU.add,
                        ["Sst", "G6"] + pres(0), ["SA%d" % h])
                    cp("pool", S1b[:, hs], SA[:, hs], ["SA%d" % h], ["S1b%d" % h])
                for h in range(4):
                    tt("dve", ATm[:, h * 128:(h + 1) * 128], bank(3, h * 128, (h + 1) * 128), hmask, ALU.mult, pres(3) + ["cst"], ["ATm"])
                pump()
                for h in range(4):
                    hs = slice(h * 128, (h + 1) * 128)
                    mm(bank(2, h * 128, (h + 1) * 128), ATm[:, hs], vb[:, hs], True, False, ["ATm", "vb"], pres(2))
                    mm(bank(2, h * 128, (h + 1) * 128), qh0[:, hs], S0b[:, hs], False, False, ["qh0", "S0b"], pres(2))
                    mm(bank(2, h * 128, (h + 1) * 128), qh1[:, hs], S1b[:, hs], False, True, ["qh1", "S1b%d" % h], pres(2))
                for h in range(4):
                    act(T3[:, h * 128:(h + 1) * 128], bank(2, h * 128, (h + 1) * 128), AF.Square, pres(2), ["G2a", "G2b"],
                        accum_out=ssq[:, h:h + 1])
                act(rso, ssq, AF.Ln, ["G2a", "G2b"], ["rso"], scale=1.0 / 128.0, bias=1e-6)
                act(rso, rso, AF.Exp, ["rso"], ["rso"], scale=-0.5)
                for h in range(4):
                    hs = slice(h * 128, (h + 1) * 128)
                    stt(ohn[:, hs], bank(2, h * 128, (h + 1) * 128), rso[:, h:h + 1], GT[:, hs], ALU.mult, ALU.mult,
                        pres(2) + ["rso", "G3a", "G3b"], ["ohn"])
                for h in range(4):
                    hs = slice(h * 128, (h + 1) * 128)
                    stt(Sst[:, hs], SA[:, hs], F3[:, h * 128 + 127:h * 128 + 128], bank(1, h * 128, (h + 1) * 128), ALU.mult, ALU.add,
                        ["SA%d" % h, "G6"] + pres(1), ["Sst"])
                cp("pool", S0b[:, :], Sst[:, :], ["Sst"], ["S0b"])
                pump()
                pbf = bank(3).bitcast(BF16)
                for h in range(4):
                    tr(pbf[:, h * 128:(h + 1) * 128], ohn[:, h * 128:(h + 1) * 128], ident_b, ["ohn", "ident_b"], pres(3))
                act(oT3[:, 0:4, tsl], v3(pbf[:, 0:512], 128), AF.Copy, pres(3), ["oTh"])
                if m == 0 and tt_ == 0:
                    tap("khat", khat[:, :], ["khat"])
            pump(10 ** 6)
            if m == 0:
                tap("oThg", oT[:, 0:1024], ["oTh"])
                tap("oT", oT[:, :], ["oTh", "oTa"])

            if m == 0:
                S.mark("attn")
            if m + 1 < n_macro:
                emit_rope(m + 1)
                xl_evac = emit_xload(m + 1)
            else:
                xl_evac = []
            while wout_scale:
                wout_scale.pop(0)()

            def ln1_stats(c, XP3=XP3, xTn=xTn):
                mm(bank(2, 0, 256), onesm_f[:, :], XP3[:, c, :], c == 0, c == 7, ["onesm_f", xTn + "c%d" % c], pres(2, 0))
                mm(bank(3, 0, 256), onesm_f[:, :], G[c % 2][:, 0:256], c == 0, c == 7, ["onesm_f", "G%d" % (c % 2)], pres(3, 0))

            for c in range(8):
                b_, hf_ = c % 2, 0
                for k in range(8):
                    mm(bank(b_, 0, 256), woutb[:, k, c * 128:(c + 1) * 128], oT3[:, k, :], k == 0, k == 7, ["oTh", "oTa", WOUT[k]], pres(b_, 0))
                stt(XP3[:, c, :], bank(b_, 0, 256), g1a[:, c:c + 1], XP3[:, c, :], ALU.mult, ALU.add, pres(b_, 0) + ["mod1", xTn + "c%d" % c], [xTn + "c%d" % c])
                ysq = G[c % 2][:, 0:256]
                act(ysq, XP3[:, c, :], AF.Square, [xTn + "c%d" % c], ["G%d" % (c % 2)])
                for _ in range(3):
                    if xl_evac:
                        xl_evac.pop(0)()
                if c >= 1:
                    ln1_stats(c - 1)
            ln1_stats(7)
            while xl_evac:
                xl_evac.pop(0)()
            if m == 0:
                tap("y1", xT[:, :], XC(xTn))
            mean_s, rstd_s = G[5][:, 0:256], G[5][:, 256:512]
            cp("dve", mean_s, bank(2, 0, 256), pres(2, 0), ["G5a"])
            tt("dve", rstd_s, mean_s, mean_s, ALU.mult, ["G5a"], ["G5b"])
            tt("dve", rstd_s, bank(3, 0, 256), rstd_s, ALU.subtract, pres(3, 0) + ["G5b"], ["G5b"])
            act(rstd_s, rstd_s, AF.Ln, ["G5b"], ["G5b"], bias=1e-5)
            act(rstd_s, rstd_s, AF.Exp, ["G5b"], ["G5b"], scale=-0.5)
            stt(mean_s, mean_s, -1.0, rstd_s, ALU.mult, ALU.mult, ["G5a", "G5b"], ["G5a"])
            for c in range(8):
                tt("dve", XP3[:, c, :], XP3[:, c, :], rstd_s, ALU.mult, [xTn + "c%d" % c, "G5b"], [xTn + "c%d" % c])
                tt("pool", XP3[:, c, :], XP3[:, c, :], mean_s, ALU.add, [xTn + "c%d" % c, "G5a"], [xTn + "c%d" % c])
                ts("pool", XP3[:, c, :], XP3[:, c, :], lnca[:, c:c + 1], lnca[:, 8 + c:9 + c], ALU.mult, ALU.add, [xTn + "c%d" % c, "lnca"], [xTn + "c%d" % c])
            if m == 0:
                tap("x1", xT[:, :], XC(xTn))
            dma("pool", v3(x1s_d, T)[:, :, msl], XP3, XC(xTn), [("x1s", m)], key="x1st%d" % m)

        final_keys = list(dbg_keys)
        if do_phase2:
            S.barrier(lambda e: e.dma_start(out=bar_d[:, 0:32], in_=cst_d[0:1, 0:32]))
            for k in range(8):
                dma("pool", w1b[:, k, :].rearrange("p (a b) -> p a b", b=1024), w1_d[k * 128:(k + 1) * 128, :].rearrange("p (a b) -> p a b", b=1024), [], ["w1b%d" % k])
            for f4 in range(8):
                dma("pool", w2b[:, f4 * 4:(f4 + 1) * 4, :], w2_d[f4 * 512:(f4 + 1) * 512, :].rearrange("(f p) c -> p f c", p=128), [],
                    ["w2b%d" % f4])
            XB = [xT3, xTb3]
            XF = [xT, xTb]
            stg = [xin0, qTs]

            def p2_load(m):
                dma("sp", XB[m % 2], v3(x1s_d, T)[:, :, m * MT:(m + 1) * MT], [("x1s", m)], XC("X%d" % (m % 2)))

            def p2_u(m):
                X = XB[m % 2]
                xn = "X%d" % (m % 2)
                for c in range(8):
                    act(uT3[:, c, :], X[:, c, :], AF.Identity, [xn + "c%d" % c, "modT", "scm"], ["uT"], scale=scm[:, c:c + 1], bias=shm[:, c:c + 1])

            def p2_stage1(m, f0=0, f1=32):
                for f in range(f0, f1):
                    b_ = f % 4
                    for k in range(8):
                        mm(bank(b_, 0, 256), w1b[:, k, f * 128:(f + 1) * 128], uT3[:, k, :], k == 0, k == 7,
                           ["uT", "w1b%d" % k], pres(b_))
                    rl = G[f % 2][:, 0:256]
                    act(rl, bank(b_, 0, 256), AF.Relu, pres(b_), ["G%d" % (f % 2)])
                    tt("pool", hTbuf(f), rl, rl, ALU.mult, ["G%d" % (f % 2)], ["hT%d" % f])

            def p2_stats(X, xn, c):
                mm(bank(6, 0, 256), onesm_f[:, :], X[:, c, :], c == 0, c == 7, ["onesm_f", xn + "c%d" % c], pres(6))
                mm(bank(7, 0, 256), onesm_f[:, :], G[2 + c % 2][:, 0:256], c == 0, c == 7, ["onesm_f", "G%d" % (2 + c % 2)], pres(7))

            def p2_stage2(m):
                X = XB[m % 2]
                xn = "X%d" % (m % 2)
                for c in range(8):
                    b_ = 4 + c % 2
                    for f in range(32):
                        mm(bank(b_, 0, 256), w2b[:, f, c * 128:(c + 1) * 128], hTbuf(f), f == 0, f == 31,
                           ["hT%d" % f, "w2b%d" % (f // 4)], pres(b_))
                    stt(X[:, c, :], bank(b_, 0, 256), g1m[:, c:c + 1], X[:, c, :], ALU.mult, ALU.add, pres(b_) + ["mod1", xn + "c%d" % c], [xn + "c%d" % c])
                    ysq = G[2 + c % 2][:, 0:256]
                    act(ysq, X[:, c, :], AF.Square, [xn + "c%d" % c], ["G%d" % (2 + c % 2)])
                    if c >= 1:
                        p2_stats(X, xn, c - 1)
                p2_stats(X, xn, 7)

            def p2_tail(m):
                X = XB[m % 2]
                xn = "X%d" % (m % 2)
                mean_s, var_s, rstd_s, nmr_s = G[4][:, 0:256], G[4][:, 256:512], G[5][:, 0:256], G[5][:, 256:512]
                cp("dve", mean_s, bank(6, 0, 256), pres(6), ["G4a"])
                tt("dve", var_s, mean_s, mean_s, ALU.mult, ["G4a"], ["G4b"])
                tt("dve", var_s, bank(7, 0, 256), var_s, ALU.subtract, pres(7) + ["G4b"], ["G4b"])
                act(rstd_s, var_s, AF.Ln, ["G4b"], ["G5a"], bias=1e-5)
                act(rstd_s, rstd_s, AF.Exp, ["G5a"], ["G5a"], scale=-0.5)
                stt(nmr_s, mean_s, -1.0, rstd_s, ALU.mult, ALU.mult, ["G4a", "G5a"], ["G5b"])
                for c in range(8):
                    tt("dve", X[:, c, :], X[:, c, :], rstd_s, ALU.mult, [xn + "c%d" % c, "G5a"], [xn + "c%d" % c])
                    tt("dve", X[:, c, :], X[:, c, :], nmr_s, ALU.add, [xn + "c%d" % c, "G5b"], [xn + "c%d" % c])
                    ts("dve", X[:, c, :], X[:, c, :], lnc[:, 16 + c:17 + c], lnc[:, 24 + c:25 + c], ALU.mult, ALU.add, [xn + "c%d" % c, "lnc"], [xn + "c%d" % c])

            def p2_out(m):
                X = XB[m % 2]
                xn = "X%d" % (m % 2)
                for tt_ in range(2):
                    t = 2 * m + tt_
                    b0 = 4 + 2 * tt_
                    sg = stg[tt_]
                    sn = "stg%d" % tt_
                    for c in range(8):
                        tr(bank(b0 + c // 4, (c % 4) * 128, (c % 4 + 1) * 128), X[:, c, tt_ * 128:(tt_ + 1) * 128], ident_f, [xn + "c%d" % c, "cst"],
                           pres(b0 + c // 4))
                    act(sg[:, 0:512], bank(b0), AF.Copy, pres(b0), [sn])
                    cp("dve", sg[:, 512:1024], bank(b0 + 1), pres(b0 + 1), [sn])
                    dma("sp", out_d[t * 128:(t + 1) * 128, :], sg[:, :], [sn], [("out", t)], key="outst%d" % tt_)

            p2_load(0)
            p2_u(0)
            for m in range(n_macro):
                p2_stage1(m, 0, 12)
                if m >= 1:
                    p2_out(m - 1)
                if m + 1 < n_macro:
                    p2_load(m + 1)
                p2_stage1(m, 12, 32)
                if m + 1 < n_macro:
                    p2_u(m + 1)
                p2_stage2(m)
                p2_tail(m)
            p2_out(n_macro - 1)
            final_keys.append("outst0")
            final_keys.append("outst1")
        else:
            final_keys.extend("x1st%d" % m for m in range(n_macro))
        print("n_ops", len(S.ops), {e: sum(1 for o in S.ops if o.eng == e) for e in S.ENGS})
        S.emit(st, final_wait_keys=set(final_keys) | set("dbg_" + k for k in dbg))
    return nc


def _consts():
    cst = np.zeros((128, 768), np.float32)
    cst[:, 0:128] = np.eye(128, dtype=np.float32)
    s = np.arange(128)[:, None]
    c = np.arange(128)[None, :]
    same = (s // 64) == (c // 64)
    bdb = (same & (s <= c)).astype(np.float32)
    ref = (c // 64) * 64 + 31
    bdref = bdb - (same & (s <= ref)).astype(np.float32)
    bdlast = (same & (s > c)).astype(np.float32)
    cst[:, 128:256] = bdref
    cst[:, 256:384] = bdb
    cst[:, 384:512] = bdlast
    cst[:, 512:640] = bdb
    cst[:, 640:768] = np.where(s <= c, 0.0, -30000.0)
    misc = np.zeros((128, 4), np.float32)
    inv_freq = (1.0 / (10000.0 ** (np.arange(0, 64, 2, dtype=np.float32) / 64.0))).astype(np.float32)
    p = np.arange(128)
    misc[:, 0] = inv_freq[p % 32]
    misc[:, 1] = np.where((p % 64) < 32, -1.0, 1.0)
    return cst, misc


def _cols(v, n):
    return np.ascontiguousarray(np.asarray(v, np.float32).reshape(n, 128).T)


def make_in_maps(x, c, positions, w_ada, b_ada, w_in, hg_lower_bounds, hg_norm_w, mla_q_norm_w, w_q_up,
                 mla_kv_norm_w, w_kv_up, w_out, ln1_g, ln1_b, w_mlp_in, w_mlp_out, ln2_g, ln2_b):
    f = lambda a: np.asarray(a, np.float32)
    w_in0 = f(w_in)[0]
    perm = np.concatenate([np.arange(0, 512), np.arange(2048, 2304), np.arange(2304, 2560), np.arange(2560, 2624),
                           np.arange(2592, 2624), np.arange(2560, 2592),
                           np.arange(512, 1024), np.arange(1024, 1536), np.arange(1536, 2048)])
    w_in_p = np.ascontiguousarray(w_in0[:, perm])
    wq0 = f(w_q_up)[0]
    qn = np.concatenate([np.arange(h * 192, h * 192 + 128) for h in range(4)])
    qp = np.concatenate([np.arange(h * 192 + 128, h * 192 + 192) for h in range(4)])
    qps = np.concatenate([np.concatenate([np.arange(h * 192 + 160, h * 192 + 192), np.arange(h * 192 + 128, h * 192 + 160)])
                          for h in range(4)])
    w_q_p = np.ascontiguousarray(wq0[:, np.concatenate([qn, qp, qps])])
    wkv0 = f(w_kv_up)[0]
    kn = np.concatenate([np.arange(h * 256, h * 256 + 128) for h in range(4)])
    vv = np.concatenate([np.arange(h * 256 + 128, h * 256 + 256) for h in range(4)])
    w_kv_p = np.ascontiguousarray(wkv0[:, np.concatenate([kn, vv])])
    cst, misc = _consts()
    nw = np.concatenate([_cols(f(mla_q_norm_w)[0], 2), _cols(f(mla_kv_norm_w)[0], 2), _cols(f(hg_norm_w)[0], 4)], axis=1)
    lncols = np.concatenate([_cols(f(ln1_g)[0], 8), _cols(f(ln1_b)[0], 8), _cols(f(ln2_g)[0], 8), _cols(f(ln2_b)[0], 8)], axis=1)
    shared = {
        "w_ada": np.ascontiguousarray(f(w_ada)[0]), "bada": _cols(f(b_ada)[0], 48), "w_in": w_in_p,
        "lb": np.ascontiguousarray(f(hg_lower_bounds)),
        "nw": np.ascontiguousarray(nw), "w_q": w_q_p, "w_kv": w_kv_p, "w_out": np.ascontiguousarray(f(w_out)[0]),
        "lncols": np.ascontiguousarray(lncols), "w1": np.ascontiguousarray(f(w_mlp_in)[0]),
        "w2": np.ascontiguousarray(f(w_mlp_out)[0]), "cst": cst, "misc": misc,
    }
    xs = f(x)
    cs = f(c)
    ps_ = np.asarray(positions, np.int32)
    maps = []
    for b in range(8):
        mp = dict(shared)
        mp["x"] = np.ascontiguousarray(xs[b])
        mp["pos"] = np.ascontiguousarray(ps_[b])
        mp["ccol"] = _cols(cs[b], 8)
        maps.append(mp)
    return maps


_NC_CACHE = {}


def kernel(**inputs):
    maps = make_in_maps(**inputs)
    if "nc" not in _NC_CACHE:
        _NC_CACHE["nc"] = build_nc()
    res = run_bass_kernel_spmd(_NC_CACHE["nc"], maps, core_ids=list(range(8)))
    return np.stack([np.asarray(r["out"], np.float32) for r in res.results], axis=0)
```

```python
import contextlib
import math
import numpy as np
import concourse.bass as bass
import concourse.mybir as mybir
from concourse.bass_utils import run_bass_kernel_spmd

F32 = mybir.dt.float32
BF16 = mybir.dt.bfloat16
I32 = mybir.dt.int32
AF = mybir.ActivationFunctionType
ALU = mybir.AluOpType

D = 1024
T = 4096
NT = T // 128
MT = 256
NM = T // MT
DFF = 4096
ALPHA = 2.0 ** 0.25
SCALE = 192.0 ** -0.5
WIN_COLS = 2688
PI = math.pi
PI_SAFE = 3.1415925


class _Op:
    __slots__ = ("id", "eng", "fn", "reads", "writes", "deps", "is_dma", "sem_key", "signal", "count", "group")


class Sched:
    ENGS = ("pe", "act", "dve", "pool", "sp")

    def __init__(self, nc):
        self.nc = nc
        self.ops = []
        self.last_writer = {}
        self.readers = {}
        self.dma_keys = []
        self.default_writer = None
        self.limit = None

    def add(self, eng, fn, reads=(), writes=(), dma=False, sem_key=None):
        if self.limit is not None and len(self.ops) >= self.limit:
            return None
        psr = [("ps", r[1]) for r in list(reads) + list(writes) if isinstance(r, tuple) and r and r[0] == "ps"]
        reads = tuple(r for r in reads if not (isinstance(r, tuple) and r and r[0] == "ps"))
        writes = tuple(dict.fromkeys([w for w in writes if not (isinstance(w, tuple) and w and w[0] == "ps")] + psr))
        op = _Op()
        op.id = len(self.ops)
        op.eng = eng
        op.fn = fn
        op.reads = reads
        op.writes = writes
        op.is_dma = dma
        op.group = sem_key if sem_key is not None else (writes[0] if dma else None)
        op.sem_key = op.group
        op.signal = False
        op.count = 0
        deps = set()
        for r in reads:
            w = self.last_writer.get(r, self.default_writer)
            if w is not None:
                deps.add((w, 0))
        for w_ in writes:
            w = self.last_writer.get(w_, self.default_writer)
            if w is not None:
                deps.add((w, 1))
            for rd in self.readers.get(w_, ()):
                deps.add((rd, 1))
        keep = set()
        for d, kind in deps:
            if d == op.id:
                continue
            dop = self.ops[d]
            if (not dop.is_dma) and (not dma) and dop.eng == eng:
                if eng == "pe":
                    continue
            keep.add(d)
        op.deps = sorted(keep)
        for r in reads:
            self.readers.setdefault(r, []).append(op.id)
        for w_ in writes:
            self.last_writer[w_] = op.id
            self.readers[w_] = []
        if dma and op.sem_key not in self.dma_keys:
            self.dma_keys.append(op.sem_key)
        self.ops.append(op)
        return op

    def mark(self, name):
        print("MARK", name, len(self.ops))

    def barrier(self, fn):
        allres = set(self.last_writer) | set(self.readers)
        op = self.add("sp", fn, reads=(), writes=["__bar%d" % len(self.ops)] + sorted(allres, key=str), dma=True)
        self.default_writer = op.id
        self.last_writer = {}
        self.readers = {}
        return op

    def emit(self, st, final_wait_keys=()):
        nc = self.nc
        ops = self.ops
        for op in ops:
            for d in op.deps:
                ops[d].signal = True
        eng_cnt = {e: 0 for e in self.ENGS}
        dma_cnt = {k: 0 for k in self.dma_keys}
        for op in ops:
            if op.is_dma:
                dma_cnt[op.sem_key] += 16
                op.count = dma_cnt[op.sem_key]
            elif op.signal:
                eng_cnt[op.eng] += 1
                op.count = eng_cnt[op.eng]
        esem = {e: st.enter_context(nc.semaphore("s_" + e)) for e in ("pe", "act", "dve", "pool")}
        dsem = {k: st.enter_context(nc.semaphore("d%d" % i)) for i, k in enumerate(self.dma_keys)}
        block = st.enter_context(nc.Block())
        by_eng = {e: [op for op in ops if op.eng == e] for e in self.ENGS}

        def run(ename, engine):
            seen = {}
            for op in by_eng[ename]:
                need = {}
                for d in op.deps:
                    dop = ops[d]
                    if dop.is_dma:
                        key = ("d", dop.sem_key)
                        sem = dsem[dop.sem_key]
                    else:
                        key = ("e", dop.eng)
                        sem = esem[dop.eng]
                    if dop.count > seen.get(key, 0) and dop.count > need.get(key, (None, 0))[1]:
                        need[key] = (sem, dop.count)
                for key, (sem, cnt) in need.items():
                    engine.wait_ge(sem, cnt)
                    seen[key] = cnt
                ins = op.fn(engine)
                if op.is_dma:
                    ins.then_inc(dsem[op.sem_key], 16)
                elif op.signal:
                    ins.then_inc(esem[op.eng], 1)
            if ename == "sp":
                for k in final_wait_keys:
                    if k in dma_cnt:
                        engine.wait_ge(dsem[k], dma_cnt[k])

        @block.tensor
        def _(e):
            run("pe", e)

        @block.scalar
        def _(e):
            run("act", e)

        @block.vector
        def _(e):
            run("dve", e)

        @block.gpsimd
        def _(e):
            run("pool", e)

        @block.sync
        def _(e):
            run("sp", e)


def build_nc(debug=False, n_macro=NM, do_phase2=True, limit=None):
    nc = bass.Bass("TRN2", target_bir_lowering=False, dynamic_dma_scratch_size=4096)

    def din(name, shape, dt=F32):
        return nc.dram_tensor(name, list(shape), dt, kind="ExternalInput").ap()

    x_d = din("x", [T, D])
    pos_d = din("pos", [T], I32)
    ccol_d = din("ccol", [128, 8])
    wada_d = din("w_ada", [D, 6 * D])
    bada_d = din("bada", [128, 48])
    win_d = din("w_in", [D, WIN_COLS])
    lb_d = din("lb", [2, 512])
    nw_d = din("nw", [128, 8])
    wq_d = din("w_q", [256, 1024])
    wkv_d = din("w_kv", [256, 1024])
    wout_d = din("w_out", [D, D])
    ln_d = din("lncols", [128, 32])
    w1_d = din("w1", [D, DFF])
    w2_d = din("w2", [DFF, D])
    cst_d = din("cst", [128, 768])
    misc_d = din("misc", [128, 4])
    out_d = nc.dram_tensor("out", [T, D], F32, kind="ExternalOutput").ap()
    x1s_d = nc.dram_tensor("x1s", [128, 8 * T], F32, kind="Internal").ap()
    bar_d = nc.dram_tensor("bar", [1, 64], F32, kind="Internal").ap()
    dbg = {}
    if debug:
        for nm, shp in debug.items():
            dbg[nm] = nc.dram_tensor("dbg_" + nm, list(shp), F32, kind="ExternalOutput").ap()

    S = Sched(nc)
    S.limit = limit
    st = contextlib.ExitStack()
    with st:
        def sb(name, shape, dt=F32):
            return st.enter_context(nc.sbuf_tensor("sb_" + name, list(shape), dt))

        ps = st.enter_context(nc.psum_tensor("ps", [128, 4096], F32))

        def bank(b, lo=0, hi=512):
            return ps[:, b * 512 + lo:b * 512 + hi]

        def pres(b, half=None):
            if half is None:
                return [("ps", b, 0), ("ps", b, 1)]
            return [("ps", b, half)]

        arena = sb("arena", [128, 73728], BF16)
        a_off = [0]

        def carve(n):
            o = a_off[0]
            a_off[0] += n
            return arena[:, o:o + n]

        KnT = carve(4 * T).rearrange("p (h t) -> p h t", t=T)
        Vc = carve(NT * 512).rearrange("p (t c) -> p t c", c=512)
        KpeT = carve(T)
        winb = carve(8 * WIN_COLS).rearrange("p (k c) -> p k c", c=WIN_COLS)
        wqb = carve(2 * 1024).rearrange("p (k c) -> p k c", c=1024)
        wkvb = carve(2 * 1024).rearrange("p (k c) -> p k c", c=1024)
        woutb = carve(8 * 1024).rearrange("p (k c) -> p k c", c=1024)
        w1b = arena[:, 0:8 * DFF].rearrange("p (k c) -> p k c", c=DFF)
        w2b = arena[:, 8 * DFF:8 * DFF + 32 * D].rearrange("p (f c) -> p f c", c=D)
        hT3 = arena[:, 65536:73728].rearrange("p (f t) -> p f t", t=MT)

        def hTbuf(f):
            return hT3[:, f, :]
        wadab = [arena[:, i * 6144:(i + 1) * 6144].rearrange("p (k c) -> p k c", c=768) for i in range(2)]

        cst = sb("cst", [128, 512])
        ident_f = cst[:, 0:128]
        bd12 = cst[:, 128:384]
        bdlast = cst[:, 384:512]
        hmask = cst[:, 256:384]
        cstb = sb("cstb", [128, 256], BF16)
        ident_b = cstb[:, 0:128]
        amask_b = cstb[:, 128:256]
        ones_f = sb("ones_f", [128, 128])
        onesm_f = sb("onesm_f", [128, 128])
        ones_b = sb("ones_b", [128, 128], BF16)
        misc = sb("misc", [128, 4])
        invf = misc[:, 0:1]
        sgn = misc[:, 1:2]
        oml_bc = sb("oml_bc", [128, 512])
        ccol = sb("ccol", [128, 8])
        condb = sb("condb", [128, 8], BF16)
        bada = sb("bada", [128, 48])
        modT = sb("modT", [128, 48])
        mod1 = sb("mod1", [128, 48])
        nw = sb("nw", [128, 8])
        lnc = sb("lnc", [128, 32])
        kmax2 = sb("kmax2", [128, 4])
        lnca = sb("lnca", [128, 16])
        scm = sb("scm", [128, 8])
        small = sb("small", [128, 16])

        xT = sb("xT", [128, 2048])
        xT3 = xT[:, :].rearrange("p (c t) -> p c t", t=MT)
        xTb = sb("xTb", [128, 2048])
        xTb3 = xTb[:, :].rearrange("p (c t) -> p c t", t=MT)
        uT = sb("uT", [128, 2048], BF16)
        uT3 = uT[:, :].rearrange("p (c t) -> p c t", t=MT)
        qTs = sb("qTs", [128, 1024])
        qTs3 = qTs[:, :].rearrange("p (h t) -> p h t", t=MT)
        cqTb = sb("cqTb", [128, 512], BF16)
        ckvTb = sb("ckvTb", [128, 512], BF16)
        G = [sb("G%d" % i, [128, 512]) for i in range(7)]
        khat = carve(512)
        vb = carve(512)
        ohn = carve(512)
        ktil = carve(512)
        qtil = carve(512)
        qh0 = carve(512)
        qh1 = sb("qh1", [128, 512], BF16)
        ATm = sb("ATm", [128, 512], BF16)
        Sst = sb("Sst", [128, 512])
        SA = sb("SA", [128, 512])
        S0b = sb("S0b", [128, 512], BF16)
        S1b = sb("S1b", [128, 512], BF16)
        QnT = sb("QnT", [128, 1024], BF16)
        QnT3 = QnT[:, :].rearrange("p (h t) -> p h t", t=MT)
        QpeT = sb("QpeT", [128, 1024], BF16)
        QpeT3 = QpeT[:, :].rearrange("p (h t) -> p h t", t=MT)
        oT = sb("oT", [128, 2048], BF16)
        oT3 = oT[:, :].rearrange("p (c t) -> p c t", t=MT)
        xin1 = sb("xin1", [128, 1024])
        xin0 = sb("xin0", [128, 1024])
        xin = xin0
        rinv = sb("rinv", [128, 256])
        oacc = sb("oacc", [128, 256])
        PT = [sb("PT%d" % i, [128, 256], BF16) for i in range(3)]
        posi = sb("posi", [128, 256], I32)
        kfi = posi
        ssq = small[:, 0:4]
        rso = small[:, 4:8]
        rkc = small[:, 8:10]
        kmt = small[:, 10:11]

        def v3(ap, inner):
            return ap.rearrange("p (a b) -> p a b", b=inner)

        def XC(name, cs=range(8)):
            return ["%sc%d" % (name, c) for c in cs]

        def dma(eng, out, in_, reads, writes, key=None, **kw):
            return S.add(eng, lambda e: e.dma_start(out=out, in_=in_, **kw), reads, writes, dma=True, sem_key=key)

        def mm(out, lhsT, rhs, start, stop, reads, writes):
            return S.add("pe", lambda e: e.matmul(out, lhsT=lhsT, rhs=rhs, start=start, stop=stop), reads, writes)

        def tr(out, in_, ident, reads, writes):
            return S.add("pe", lambda e: e.transpose(out=out, in_=in_, identity=ident), reads, writes)

        def act(out, in_, func, reads, writes, **kw):
            return S.add("act", lambda e: e.activation(out=out, in_=in_, func=func, **kw), reads, writes)

        def tt(eng, out, in0, in1, op, reads, writes):
            return S.add(eng, lambda e: e.tensor_tensor(out=out, in0=in0, in1=in1, op=op), reads, writes)

        def ts(eng, out, in0, s1, s2, op0, op1, reads, writes):
            if op1 is None:
                return S.add(eng, lambda e: e.tensor_scalar(out=out, in0=in0, scalar1=s1, scalar2=None, op0=op0), reads, writes)
            return S.add(eng, lambda e: e.tensor_scalar(out=out, in0=in0, scalar1=s1, scalar2=s2, op0=op0, op1=op1), reads, writes)

        def stt(out, in0, scalar, in1, op0, op1, reads, writes):
            return S.add("dve", lambda e: e.scalar_tensor_tensor(out=out, in0=in0, scalar=scalar, in1=in1, op0=op0, op1=op1), reads, writes)

        def cp(eng, out, in_, reads, writes):
            return S.add(eng, lambda e: e.tensor_copy(out=out, in_=in_), reads, writes)

        def mset(eng, ap, val, writes):
            return S.add(eng, lambda e: e.memset(ap, val), (), writes)

        def recip(out, in_, reads, writes):
            return S.add("dve", lambda e: e.reciprocal(out=out, in_=in_), reads, writes)

        dbg_keys = []

        def tap(name, src, reads):
            if debug and name in dbg:
                dma("pool", dbg[name], src, reads, ["dbgout_" + name], key="dbg_" + name)
                dbg_keys.append("dbg_" + name)

        dma("sp", cst[:, :], cst_d[:, 0:512], [], ["cst"])
        dma("sp", misc[:, :], misc_d, [], ["misc"])
        dma("sp", ccol[:, :], ccol_d, [], ["ccol"])
        dma("sp", bada[:, :], bada_d, [], ["bada"])
        dma("sp", nw[:, :], nw_d, [], ["nw"])
        dma("sp", lnc[:, :], ln_d, [], ["lnc"])
        dma("sp", xT[:, 0:1024], lb_d.rearrange("a c -> (a c)").partition_broadcast(128), [], XC("xT0"))
        dma("pool", ident_b, cst_d[:, 0:128], [], ["ident_b"])
        dma("pool", amask_b, cst_d[:, 640:768], [], ["amask_b"])
        mset("dve", ones_f[:, :], 1.0, ["ones_f"])
        mset("dve", onesm_f[:, :], 1.0 / 1024.0, ["onesm_f"])
        mset("dve", ones_b[:, :], 1.0, ["ones_b"])
        mset("dve", Sst[:, :], 0.0, ["Sst"])
        mset("dve", S0b[:, :], 0.0, ["S0b"])
        mset("dve", qh0[:, :], 0.0, ["qh0"])
        mset("dve", qh1[:, :], 0.0, ["qh1"])
        mset("dve", kmax2[:, :], 0.0, ["kmax2"])
        mset("dve", QpeT[:, :], 0.0, ["QpeT"])

        act(condb[:, :], ccol[:, :], AF.Silu, ["ccol"], ["condb"])
        for ch in range(8):
            wb = wadab[ch % 2]
            wname = "wada%d" % (ch % 2)
            dma("pool", wb, wada_d[:, ch * 768:(ch + 1) * 768].rearrange("(k p) c -> p k c", p=128), [], [wname])
            for jj in range(6):
                j = ch * 6 + jj
                for k in range(8):
                    mm(bank(0, j, j + 1), wb[:, k, jj * 128:(jj + 1) * 128], condb[:, k:k + 1], k == 0, k == 7,
                       [wname, "condb"], pres(0))
        tt("dve", modT[:, :], bank(0, 0, 48), bada[:, :], ALU.add, pres(0) + ["bada"], ["modT"])
        ts("dve", mod1[:, :], modT[:, :], 1.0, None, ALU.add, None, ["modT"], ["mod1"])
        ts("dve", lnca[:, :], lnc[:, 0:16], ALPHA, None, ALU.mult, None, ["lnc"], ["lnca"])
        ts("dve", scm[:, :], mod1[:, 32:40], 1.0 / ALPHA, None, ALU.mult, None, ["mod1"], ["scm"])
        sha, sc1a, g1a = modT[:, 0:8], mod1[:, 8:16], mod1[:, 16:24]
        shm, sc1m, g1m = modT[:, 24:32], mod1[:, 32:40], mod1[:, 40:48]
        tap("modT", modT[:, :], ["modT"])

        tt("dve", G[0][:, :], xT[:, 0:512], xT[:, 512:1024], ALU.subtract, XC("xT0"), ["G0"])
        act(G[0][:, :], G[0][:, :], AF.Sigmoid, ["G0"], ["G0"])
        ts("dve", oml_bc[:, :], G[0][:, :], -1.0, 1.0, ALU.mult, ALU.add, ["G0"], ["oml_bc"])

        for k in range(8):
            dma("pool", winb[:, k, :].rearrange("p (a b) -> p a b", b=1344), win_d[k * 128:(k + 1) * 128, :].rearrange("p (a b) -> p a b", b=1344), [], ["winb%d" % k])
        for k in range(2):
            dma("pool", wqb[:, k, :], wq_d[k * 128:(k + 1) * 128, :], [], ["wqb%d" % k])
            dma("pool", wkvb[:, k, :], wkv_d[k * 128:(k + 1) * 128, :], [], ["wkvb%d" % k])
            act(wqb[:, k, :], wqb[:, k, :], AF.Identity, ["wqb%d" % k, "nw"], ["wqb%d" % k], scale=nw[:, k:k + 1])
            act(wkvb[:, k, :], wkvb[:, k, :], AF.Identity, ["wkvb%d" % k, "nw"], ["wkvb%d" % k], scale=nw[:, 2 + k:3 + k])
        wout_scale = []
        for k in range(8):
            dma("pool", woutb[:, k, :], wout_d[k * 128:(k + 1) * 128, :], [], ["woutb%d" % k])
            if k < 4:
                wout_scale.append(lambda k=k: act(woutb[:, k, :], woutb[:, k, :], AF.Identity, ["woutb%d" % k, "nw"], ["woutb%d" % k],
                                                  scale=nw[:, 4 + k:5 + k]))
        WIN = ["winb%d" % k for k in range(8)]
        WQ = ["wqb0", "wqb1"]
        WKV = ["wkvb0", "wkvb1"]
        WOUT = ["woutb%d" % k for k in range(8)]
        mset("dve", KpeT[64:65, :], 1.0, ["KpeT_ones"])

        def kres(t):
            return "K%d" % t


        for m in range(n_macro):
            t0 = 2 * m
            msl = slice(m * MT, (m + 1) * MT)
            extraK = ["wada0", "wada1"] if m == 0 else []
            XP3 = xT3 if m % 2 == 0 else xTb3
            xTn = "xT%d" % (m % 2)
            def emit_xload(m):
                t0 = 2 * m
                XP3 = xT3 if m % 2 == 0 else xTb3
                xTn = "xT%d" % (m % 2)
                evac = []
                for tt_ in range(2):
                    tsl = slice(tt_ * 128, (tt_ + 1) * 128)
                    xs_, xn_ = (xin0, "xin0") if tt_ == 0 else (xin1, "xin1")
                    for hb in range(2):
                        xb_ = 4 + 2 * tt_ + hb
                        for c4 in range(4):
                            c = hb * 4 + c4
                            tr(bank(xb_, c4 * 128, (c4 + 1) * 128), xs_[:, c * 128:(c + 1) * 128], ident_f, [xn_, "cst"], pres(xb_))
                        evac.append(lambda hb=hb, tsl=tsl, xb_=xb_: act(
                            XP3[:, hb * 4:(hb + 1) * 4, tsl], v3(bank(xb_), 128), AF.Copy, pres(xb_), XC(xTn, range(hb * 4, hb * 4 + 4)), scale=ALPHA))
                        for c4 in range(4):
                            c = hb * 4 + c4
                            evac.append(lambda c=c, c4=c4, tsl=tsl, xb_=xb_: act(
                                uT3[:, c, tsl], bank(xb_, c4 * 128, (c4 + 1) * 128), AF.Identity, pres(xb_) + ["modT", "mod1"], ["uT"],
                                scale=sc1a[:, c:c + 1], bias=sha[:, c:c + 1]))
                if m + 1 < n_macro:
                    dma("sp", xin0[:, :], x_d[(t0 + 2) * 128:(t0 + 3) * 128, :], [], ["xin0"])
                    dma("sp", xin1[:, :], x_d[(t0 + 3) * 128:(t0 + 4) * 128, :], [], ["xin1"])
                return evac

            if m == 0:
                dma("sp", xin0[:, :], x_d[0:128, :], [], ["xin0"])
                dma("sp", xin1[:, :], x_d[128:256, :], [], ["xin1"])
                for f_ in emit_xload(0):
                    f_()
            if m == 0:
                tap("uT", uT[:, :], ["uT"])
            if m == 0:
                S.mark("xload")
            def emit_rope(m):
                dma("sp", posi[:, :], pos_d[m * MT:(m + 1) * MT].partition_broadcast(128), [], ["posi"])
                ang, tmpa, tmpb = G[2][:, 0:256], G[2][:, 256:512], G[3][:, 0:256]
                Ct, Sg = G[6][:, 0:256], G[6][:, 256:512]
                cp("dve", tmpa, posi[:, :], ["posi"], ["G2b"])
                ts("dve", ang, tmpa, invf, None, ALU.mult, None, ["G2b", "misc"], ["G2a"])
                ts("dve", kfi[:, :], ang, 1.0 / (2 * PI), None, ALU.mult, None, ["G2a"], ["posi"])
                cp("dve", tmpa, kfi[:, :], ["posi"], ["G2b"])
                stt(ang, tmpa, -2.0 * PI, ang, ALU.mult, ALU.add, ["G2a", "G2b"], ["G2a"])
                ts("dve", tmpa, ang, PI, None, ALU.is_gt, None, ["G2a"], ["G2b"])
                stt(tmpa, tmpa, -2.0 * PI, ang, ALU.mult, ALU.add, ["G2a", "G2b"], ["G2b"])
                ts("dve", tmpa, tmpa, -PI_SAFE, PI_SAFE, ALU.max, ALU.min, ["G2b"], ["G2b"])
                act(Sg, tmpa, AF.Sin, ["G2b", "misc"], ["G6"], scale=sgn)
                ts("dve", tmpb, ang, PI / 2, None, ALU.add, None, ["G2a"], ["G3a"])
                ts("dve", tmpa, tmpb, PI, None, ALU.is_gt, None, ["G3a"], ["G2b"])
                stt(tmpb, tmpa, -2.0 * PI, tmpb, ALU.mult, ALU.add, ["G3a", "G2b"], ["G3a"])
                ts("dve", tmpb, tmpb, -PI_SAFE, PI_SAFE, ALU.max, ALU.min, ["G3a"], ["G3a"])
                act(Ct, tmpb, AF.Sin, ["G3a"], ["G6"])


            if m == 0:
                emit_rope(0)
            Ct, Sg = G[6][:, 0:256], G[6][:, 256:512]
            if m == 0:
                S.mark("rope")
            sq = v3(G[0][:, :], 256)
            sq2 = v3(G[1][:, :], 256)
            cq3 = v3(cqTb[:, :], 256)
            ckv3 = v3(ckvTb[:, :], 256)
            pb = 0
            for j in range(8):
                b_, hf_ = 2 + j % 2, 0
                for k in range(8):
                    mm(bank(b_, hf_ * 256, hf_ * 256 + 256), winb[:, k, j * 128:(j + 1) * 128], uT3[:, k, :], k == 0, k == 7,
                       ["uT", WIN[k]], pres(b_, hf_))
                src = bank(b_, hf_ * 256, hf_ * 256 + 256)
                if j < 4:
                    act(qTs3[:, j, :], src, AF.Copy, pres(b_, hf_), ["qTs"])
                elif j < 6:
                    act(sq[:, j - 4, :], src, AF.Square, pres(b_, hf_), ["G0"])
                    cp("dve", cq3[:, j - 4, :], src, pres(b_, hf_), ["cqTb"])
                else:
                    act(sq2[:, j - 6, :], src, AF.Square, pres(b_, hf_), ["G1"])
                    cp("dve", ckv3[:, j - 6, :], src, pres(b_, hf_), ["ckvTb"])
            if m == 0:
                S.mark("winfm")
            for jj in range(2):
                for k in range(8):
                    mm(ps[0:64, (2 + jj) * 512:(2 + jj) * 512 + 256], winb[:, k, 1024 + jj * 64:1024 + (jj + 1) * 64], uT3[:, k, :],
                       k == 0, k == 7, ["uT", WIN[k]], pres(2 + jj))
            t1, t2 = G[3][0:64, 256:512], G[4][0:64, 0:256]
            tt("dve", t1, ps[0:64, 1024:1280], Ct[0:64, :], ALU.mult, pres(2) + ["G6"], ["G3b"])
            tt("dve", t2, ps[0:64, 1536:1792], Sg[0:64, :], ALU.mult, pres(3) + ["G6"], ["G4a"])
            tt("dve", KpeT[0:64, msl], t1, t2, ALU.add, ["G3b", "G4a"], [kres(t0) + "pe", kres(t0 + 1) + "pe"] + extraK)
            if m == 0:
                S.mark("kpe")
            rq_bc, rk_bc = G[5][:, 0:256], G[5][:, 256:512]
            for i, (sqx, dst, nm) in enumerate(((sq, rq_bc, "G5a"), (sq2, rk_bc, "G5b"))):
                for k in range(2):
                    mm(bank(i, 0, 256), ones_f[:, :], sqx[:, k, :], k == 0, k == 1, ["ones_f", "G%d" % i], pres(i))
                act(dst, bank(i, 0, 256), AF.Ln, pres(i), [nm], scale=1.0 / 256.0, bias=1e-6)
                act(dst, dst, AF.Exp, [nm], [nm], scale=-0.5)
            for tt_ in range(2):
                for k in range(2):
                    mm(bank(2, tt_, tt_ + 1), sq2[:, k, tt_ * 128:(tt_ + 1) * 128], ones_f[:, 0:1], k == 0, k == 1,
                       ["G1", "ones_f"], pres(2))
            act(rkc, bank(2, 0, 2), AF.Ln, pres(2), ["rkc"], scale=1.0 / 256.0, bias=1e-6)
            act(rkc, rkc, AF.Exp, ["rkc"], ["rkc"], scale=-0.5)
            if m == 0:
                S.mark("rstd")
            for h in range(4):
                b_, hf_ = h % 2, 0
                for k in range(2):
                    mm(bank(b_, 0, 256), wkvb[:, k, h * 128:(h + 1) * 128], ckv3[:, k, :], k == 0, k == 1, ["ckvTb", WKV[k]], pres(b_, 0))
                tt("dve", KnT[:, h, msl], bank(b_, 0, 256), rk_bc, ALU.mult, pres(b_, 0) + ["G5b"],
                   [kres(t0) + "n%d" % h, kres(t0 + 1) + "n%d" % h] + extraK)
                for k in range(2):
                    mm(bank(2 + b_, 0, 256), wqb[:, k, h * 128:(h + 1) * 128], cq3[:, k, :], k == 0, k == 1, ["cqTb", WQ[k]], pres(2 + b_))
                tt("dve", QnT3[:, h, :], bank(2 + b_, 0, 256), rq_bc, ALU.mult, pres(2 + b_) + ["G5a"], ["QnT%d" % h])
            for tt_ in range(2):
                for k in range(2):
                    mm(bank(4 + tt_), ckv3[:, k, tt_ * 128:(tt_ + 1) * 128], wkvb[:, k, 512:1024], k == 0, k == 1, ["ckvTb", WKV[k]], pres(4 + tt_))
                act(Vc[:, t0 + tt_, :], bank(4 + tt_), AF.Identity, pres(4 + tt_) + ["rkc"], [kres(t0 + tt_) + "v"] + extraK, scale=rkc[:, tt_:tt_ + 1])
            for h in range(4):
                b_ = h % 2
                for jj in range(2):
                    c0 = 512 + jj * 256 + h * 64
                    for k in range(2):
                        mm(ps[0:64, (b_ * 2 + jj) * 512:(b_ * 2 + jj) * 512 + 256], wqb[:, k, c0:c0 + 64], cq3[:, k, :], k == 0, k == 1,
                           ["cqTb", WQ[k]], pres(b_ * 2 + jj))
                tt("dve", t1, ps[0:64, (b_ * 2) * 512:(b_ * 2) * 512 + 256], Ct[0:64, :], ALU.mult, pres(b_ * 2) + ["G6"], ["G3b"])
                tt("dve", t2, ps[0:64, (b_ * 2 + 1) * 512:(b_ * 2 + 1) * 512 + 256], Sg[0:64, :], ALU.mult, pres(b_ * 2 + 1) + ["G6"], ["G4a"])
                tt("dve", t1, t1, t2, ALU.add, ["G3b", "G4a"], ["G3b"])
                tt("dve", QpeT3[0:64, h, :], t1, rq_bc[0:64, :], ALU.mult, ["G3b", "G5a"], ["QpeT%d" % h])
            if m == 0:
                S.mark("upproj")
            g0b = G[0][:, :].bitcast(BF16).rearrange("p (h t) -> p h t", t=MT)
            g1b = G[1][:, :].bitcast(BF16).rearrange("p (h t) -> p h t", t=MT)
            KN = [kres(t0) + "n%d" % h for h in range(4)] + [kres(t0 + 1) + "n%d" % h for h in range(4)]
            QN = ["QnT%d" % h for h in range(4)]
            QP = ["QpeT%d" % h for h in range(4)]
            act(g0b, KnT[:, :, msl], AF.Square, KN, ["G0"])
            act(PT[1][0:64, :], KpeT[0:64, msl], AF.Square, [kres(t0) + "pe", kres(t0 + 1) + "pe"], ["PT1"])
            for h in range(4):
                bk = bank(4 + h // 2, (h % 2) * 256, (h % 2) * 256 + 256)
                mm(bk, ones_b[:, :], g0b[:, h, :], True, False, ["ones_b", "G0"], pres(4 + h // 2))
                mm(bk, ones_b[0:64, :], PT[1][0:64, :], False, True, ["ones_b", "PT1"], pres(4 + h // 2))
            S.add("dve", lambda e, o=small[:, 12:16], i=ps[:, 4 * 512:6 * 512].rearrange("p (h c) -> p h c", c=256):
                  e.reduce_max(out=o, in_=i, axis=mybir.AxisListType.X), pres(4) + pres(5), ["kmt4"])
            tt("dve", kmax2[:, :], kmax2[:, :], small[:, 12:16], ALU.max, ["kmt4", "kmax2"], ["kmax2"])
            act(g1b, QnT3[:, :, :], AF.Square, QN, ["G1"])
            act(g0b[0:64, :, :], QpeT3[0:64, :, :], AF.Square, QP + ["G0"], ["G0"])
            for h in range(4):
                bk = bank(6 + h // 2, (h % 2) * 256, (h % 2) * 256 + 256)
                mm(bk, ones_b[:, :], g1b[:, h, :], True, False, ["ones_b", "G1"], pres(6 + h // 2))
                mm(bk, ones_b[0:64, :], g0b[0:64, h, :], False, True, ["ones_b", "G0"], pres(6 + h // 2))
            shr = G[2][64:65, :].rearrange("p (a b) -> p a b", b=512)[:, 0, :]
            sh4 = [G[2][64:65, 0:256], G[2][64:65, 256:512], G[3][64:65, 0:256], G[3][64:65, 256:512]]
            for h in range(4):
                ts("dve", sh4[h], ps[64:65, (6 + h // 2) * 512 + (h % 2) * 256:(6 + h // 2) * 512 + (h % 2) * 256 + 256],
                   kmax2[64:65, h:h + 1], 1.1, ALU.mult, ALU.mult, pres(6 + h // 2) + ["kmax2"], ["G2a" if h < 2 else "G3a", "G2b" if h < 2 else "G3b"])
            for gi in (2, 3):
                gn = ["G%da" % gi, "G%db" % gi]
                act(G[gi][64:65, :], G[gi][64:65, :], AF.Ln, gn, gn)
                act(G[gi][64:65, :], G[gi][64:65, :], AF.Exp, gn, gn, scale=0.5)
                ts("dve", QpeT3[64:65, (gi - 2) * 2:(gi - 2) * 2 + 2, :], G[gi][64:65, :].rearrange("p (a b) -> p a b", b=256), -1.0, None,
                   ALU.mult, None, gn, QP[(gi - 2) * 2:(gi - 2) * 2 + 2])
            if m == 0:
                tap("QnT", QnT[:, :], ["QnT%d" % h for h in range(4)])
                tap("QpeT", QpeT[:, :], ["QpeT%d" % h for h in range(4)])
                tap("KnT", KnT[:, 0, 0:256], [kres(0) + "n0", kres(1) + "n0"])
                tap("KpeT", KpeT[:, 0:256], [kres(0) + "pe", kres(1) + "pe", "KpeT_ones"])
                tap("V0", Vc[:, 0, :], [kres(0) + "v"])

            if m == 0:
                S.mark("stab")
            def attention_units(m=m, t0=t0):
                nk = t0 + 2
                for h in range(4):
                    obr, rbr = pres(6), pres(7)

                    def emit_st(kt, h=h):
                        lo = 128 if kt == t0 + 1 else 0
                        masked = kt >= t0
                        sb_ = 4 + kt % 2
                        sps = bank(sb_, lo, 256)
                        spr = pres(sb_)
                        kr = [kres(kt) + "n%d" % h, kres(kt) + "pe", "KpeT_ones"]
                        ksl = slice(kt * 128, (kt + 1) * 128)
                        mm(sps, KnT[:, h, ksl], QnT3[:, h, lo:256], True, False, kr + ["QnT%d" % h], spr)
                        mm(sps, KpeT[0:65, ksl], QpeT3[0:65, h, lo:256], False, not masked, kr + ["QpeT%d" % h], spr)
                        if masked:
                            mm(bank(sb_, lo, lo + 128), ident_b, amask_b, False, True, ["ident_b", "amask_b"], spr)
                        act(PT[kt % 3][:, lo:256], sps, AF.Exp, spr, ["PT%d" % (kt % 3)], scale=SCALE)

                    def emit_pv(kt, h=h):
                        lo = 128 if kt == t0 + 1 else 0
                        pt = PT[kt % 3]
                        ptn = "PT%d" % (kt % 3)
                        mm(bank(6, lo, 256), Vc[:, kt, h * 128:(h + 1) * 128], pt[:, lo:256], kt == 0, kt == nk - 1,
                           [kres(kt) + "v", ptn], obr)
                        mm(bank(7, lo, 256), ones_b[:, :], pt[:, lo:256], kt == 0, kt == nk - 1, ["ones_b", ptn], rbr)

                    emit_st(0)
                    for kt in range(nk):
                        if kt + 1 < nk:
                            emit_st(kt + 1)
                        emit_pv(kt)
                        if kt == nk - 1:
                            cp("dve", rinv[:, :], bank(7, 0, 256), rbr, ["rinv"])
                            act(oacc[:, :], bank(6, 0, 256), AF.Copy, obr, ["oacc"])
                            act(rinv[:, :], rinv[:, :], AF.Ln, ["rinv"], ["rinv"])
                            act(rinv[:, :], rinv[:, :], AF.Exp, ["rinv"], ["rinv"], scale=-1.0)
                            tt("dve", oT3[:, 4 + h, :], oacc[:, :], rinv[:, :], ALU.mult, ["oacc", "rinv"], ["oTa"])
                        yield

            att = attention_units()
            n_units = 4 * (t0 + 2)
            per_pump = -(-n_units // 8)

            def pump(n=per_pump):
                for _ in range(n):
                    try:
                        next(att)
                    except StopIteration:
                        return

            for tt_ in range(2):
                t = t0 + tt_
                tsl = slice(tt_ * 128, (tt_ + 1) * 128)
                T1, T2, T3, GT = G[0], G[1], G[2], G[3]
                F1, F2, F3 = G[4], G[5], G[6]
                for g in range(3):
                    for k in range(8):
                        mm(bank(g), uT3[:, k, tsl], winb[:, k, 1152 + g * 512:1152 + (g + 1) * 512], k == 0, k == 7,
                           ["uT", WIN[k]], pres(g))
                act(T1[:, :], bank(0), AF.Sigmoid, pres(0), ["G0"], scale=-1.0)
                act(vb[:, :], bank(1), AF.Copy, pres(1), ["vb"])
                act(GT[:, :], bank(2), AF.Sigmoid, pres(2), ["G3a", "G3b"])
                tt("dve", T1[:, :], T1[:, :], oml_bc[:, :], ALU.mult, ["G0", "oml_bc"], ["G0"])
                act(T2[:, :], T1[:, :], AF.Ln, ["G0"], ["G1"], scale=-1.0, bias=1.0)
                tt("dve", GT[:, :], GT[:, :], bank(2), ALU.mult, pres(2) + ["G3a", "G3b"], ["G3a", "G3b"])
                pump()
                mm(bank(3), bdlast, T2[:, :], True, True, ["cst", "G1"], pres(3))
                for h in range(4):
                    mm(bank(h // 2, (h % 2) * 256, (h % 2) * 256 + 256), T2[:, h * 128:(h + 1) * 128], bd12, True, True,
                       ["G1", "cst"], pres(h // 2))
                for h in range(4):
                    tr(bank(2, h * 128, (h + 1) * 128), T1[:, h * 128:(h + 1) * 128], ident_f, ["G0", "cst"], pres(2))
                act(T3[:, :], bank(3), AF.Exp, pres(3), ["G2a", "G2b"])
                e12 = ps[:, 0:1024].rearrange("p (h c) -> p h c", c=256)
                E12R = pres(0) + pres(1)
                act(v3(F2[:, :], 128), e12[:, :, 0:128], AF.Exp, E12R, ["G5a", "G5b"], scale=-1.0)
                act(v3(F1[:, :], 128), e12[:, :, 0:128], AF.Exp, E12R, ["G4a", "G4b"])
                act(v3(F3[:, :], 128), e12[:, :, 128:256], AF.Exp, E12R, ["G6"])
                tt("dve", khat[:, :], T1[:, :], T3[:, :], ALU.mult, ["G0", "G2a", "G2b"], ["khat"])
                tt("dve", F2[:, :], bank(2), F2[:, :], ALU.mult, pres(2) + ["G5a", "G5b"], ["G5a", "G5b"])
                qv = qTs3[:, :, tsl]
                tt("pool", v3(F1[:, :], 128), qv, v3(F1[:, :], 128), ALU.mult, ["qTs", "G4a", "G4b"], ["G4a", "G4b"])
                tt("pool", v3(qh0[:, :], 128)[:, :, 0:64], qTs3[:, :, tt_ * 128:tt_ * 128 + 64], v3(F3[:, :], 128)[:, :, 0:64],
                   ALU.mult, ["qTs", "G6"], ["qh0"])
                tt("pool", v3(qh1[:, :], 128)[:, :, 64:128], qTs3[:, :, tt_ * 128 + 64:tt_ * 128 + 128], v3(F3[:, :], 128)[:, :, 64:128],
                   ALU.mult, ["qTs", "G6"], ["qh1"])
                pump()
                for ci in range(2):
                    for h in range(4):
                        mm(bank(ci, h * 128, (h + 1) * 128), khat[ci * 64:(ci + 1) * 64, h * 128:(h + 1) * 128],
                           vb[ci * 64:(ci + 1) * 64, h * 128:(h + 1) * 128], True, True, ["khat", "vb"], pres(ci))
                for h in range(4):
                    mm(bank(3, h * 128, (h + 1) * 128), F2[:, h * 128:(h + 1) * 128], F1[:, h * 128:(h + 1) * 128], True, True,
                       ["G4a", "G4b", "G5a", "G5b"], pres(3))
                for h in range(4):
                    hs = slice(h * 128, (h + 1) * 128)
                    stt(SA[:, hs], Sst[:, hs], F3[:, h * 128 + 63:h * 128 + 64], bank(0, h * 128, (h + 1) * 128), ALU.mult, ALU.add,
                        ["Sst", "G6"] + pres(0), ["SA%d" % h])
                    cp("pool", S1b[:, hs], SA[:, hs], ["SA%d" % h], ["S1b%d" % h])
                for h in range(4):
                    tt("dve", ATm[:, h * 128:(h + 1) * 128], bank(3, h * 128, (h + 1) * 128), hmask, ALU.mult, pres(3) + ["cst"], ["ATm"])
                pump()
                for h in range(4):
                    hs = slice(h * 128, (h + 1) * 128)
                    mm(bank(2, h * 128, (h + 1) * 128), ATm[:, hs], vb[:, hs], True, False, ["ATm", "vb"], pres(2))
                    mm(bank(2, h * 128, (h + 1) * 128), qh0[:, hs], S0b[:, hs], False, False, ["qh0", "S0b"], pres(2))
                    mm(bank(2, h * 128, (h + 1) * 128), qh1[:, hs], S1b[:, hs], False, True, ["qh1", "S1b%d" % h], pres(2))
                for h in range(4):
                    act(T3[:, h * 128:(h + 1) * 128], bank(2, h * 128, (h + 1) * 128), AF.Square, pres(2), ["G2a", "G2b"],
                        accum_out=ssq[:, h:h + 1])
                act(rso, ssq, AF.Ln, ["G2a", "G2b"], ["rso"], scale=1.0 / 128.0, bias=1e-6)
                act(rso, rso, AF.Exp, ["rso"], ["rso"], scale=-0.5)
                for h in range(4):
                    hs = slice(h * 128, (h + 1) * 128)
                    stt(ohn[:, hs], bank(2, h * 128, (h + 1) * 128), rso[:, h:h + 1], GT[:, hs], ALU.mult, ALU.mult,
                        pres(2) + ["rso", "G3a", "G3b"], ["ohn"])
                for h in range(4):
                    hs = slice(h * 128, (h + 1) * 128)
                    stt(Sst[:, hs], SA[:, hs], F3[:, h * 128 + 127:h * 128 + 128], bank(1, h * 128, (h + 1) * 128), ALU.mult, ALU.add,
                        ["SA%d" % h, "G6"] + pres(1), ["Sst"])
                cp("pool", S0b[:, :], Sst[:, :], ["Sst"], ["S0b"])
                pump()
                pbf = bank(3).bitcast(BF16)
                for h in range(4):
                    tr(pbf[:, h * 128:(h + 1) * 128], ohn[:, h * 128:(h + 1) * 128], ident_b, ["ohn", "ident_b"], pres(3))
                act(oT3[:, 0:4, tsl], v3(pbf[:, 0:512], 128), AF.Copy, pres(3), ["oTh"])
                if m == 0 and tt_ == 0:
                    tap("khat", khat[:, :], ["khat"])
            pump(10 ** 6)
            if m == 0:
                tap("oThg", oT[:, 0:1024], ["oTh"])
                tap("oT", oT[:, :], ["oTh", "oTa"])

            if m == 0:
                S.mark("attn")
            if m + 1 < n_macro:
                emit_rope(m + 1)
                xl_evac = emit_xload(m + 1)
            else:
                xl_evac = []
            while wout_scale:
                wout_scale.pop(0)()

            def ln1_stats(c, XP3=XP3, xTn=xTn):
                mm(bank(2, 0, 256), onesm_f[:, :], XP3[:, c, :], c == 0, c == 7, ["onesm_f", xTn + "c%d" % c], pres(2, 0))
                mm(bank(3, 0, 256), onesm_f[:, :], G[c % 2][:, 0:256], c == 0, c == 7, ["onesm_f", "G%d" % (c % 2)], pres(3, 0))

            for c in range(8):
                b_, hf_ = c % 2, 0
                for k in range(8):
                    mm(bank(b_, 0, 256), woutb[:, k, c * 128:(c + 1) * 128], oT3[:, k, :], k == 0, k == 7, ["oTh", "oTa", WOUT[k]], pres(b_, 0))
                stt(XP3[:, c, :], bank(b_, 0, 256), g1a[:, c:c + 1], XP3[:, c, :], ALU.mult, ALU.add, pres(b_, 0) + ["mod1", xTn + "c%d" % c], [xTn + "c%d" % c])
                ysq = G[c % 2][:, 0:256]
                act(ysq, XP3[:, c, :], AF.Square, [xTn + "c%d" % c], ["G%d" % (c % 2)])
                for _ in range(3):
                    if xl_evac:
                        xl_evac.pop(0)()
                if c >= 1:
                    ln1_stats(c - 1)
            ln1_stats(7)
            while xl_evac:
                xl_evac.pop(0)()
            if m == 0:
                tap("y1", xT[:, :], XC(xTn))
            mean_s, rstd_s = G[5][:, 0:256], G[5][:, 256:512]
            cp("dve", mean_s, bank(2, 0, 256), pres(2, 0), ["G5a"])
            tt("dve", rstd_s, mean_s, mean_s, ALU.mult, ["G5a"], ["G5b"])
            tt("dve", rstd_s, bank(3, 0, 256), rstd_s, ALU.subtract, pres(3, 0) + ["G5b"], ["G5b"])
            act(rstd_s, rstd_s, AF.Ln, ["G5b"], ["G5b"], bias=1e-5)
            act(rstd_s, rstd_s, AF.Exp, ["G5b"], ["G5b"], scale=-0.5)
            stt(mean_s, mean_s, -1.0, rstd_s, ALU.mult, ALU.mult, ["G5a", "G5b"], ["G5a"])
            for c in range(8):
                tt("dve", XP3[:, c, :], XP3[:, c, :], rstd_s, ALU.mult, [xTn + "c%d" % c, "G5b"], [xTn + "c%d" % c])
                tt("pool", XP3[:, c, :], XP3[:, c, :], mean_s, ALU.add, [xTn + "c%d" % c, "G5a"], [xTn + "c%d" % c])
                ts("pool", XP3[:, c, :], XP3[:, c, :], lnca[:, c:c + 1], lnca[:, 8 + c:9 + c], ALU.mult, ALU.add, [xTn + "c%d" % c, "lnca"], [xTn + "c%d" % c])
            if m == 0:
                tap("x1", xT[:, :], XC(xTn))
            dma("pool", v3(x1s_d, T)[:, :, msl], XP3, XC(xTn), [("x1s", m)], key="x1st%d" % m)

        final_keys = list(dbg_keys)
        if do_phase2:
            S.barrier(lambda e: e.dma_start(out=bar_d[:, 0:32], in_=cst_d[0:1, 0:32]))
            for k in range(8):
                dma("pool", w1b[:, k, :].rearrange("p (a b) -> p a b", b=1024), w1_d[k * 128:(k + 1) * 128, :].rearrange("p (a b) -> p a b", b=1024), [], ["w1b%d" % k])
            for f4 in range(8):
                dma("pool", w2b[:, f4 * 4:(f4 + 1) * 4, :], w2_d[f4 * 512:(f4 + 1) * 512, :].rearrange("(f p) c -> p f c", p=128), [],
                    ["w2b%d" % f4])
            XB = [xT3, xTb3]
            XF = [xT, xTb]
            stg = [xin0, qTs]

            def p2_load(m):
                dma("sp", XB[m % 2], v3(x1s_d, T)[:, :, m * MT:(m + 1) * MT], [("x1s", m)], XC("X%d" % (m % 2)))

            def p2_u(m):
                X = XB[m % 2]
                xn = "X%d" % (m % 2)
                for c in range(8):
                    act(uT3[:, c, :], X[:, c, :], AF.Identity, [xn + "c%d" % c, "modT", "scm"], ["uT"], scale=scm[:, c:c + 1], bias=shm[:, c:c + 1])

            def p2_stage1(m, f0=0, f1=32):
                for f in range(f0, f1):
                    b_ = f % 4
                    for k in range(8):
                        mm(bank(b_, 0, 256), w1b[:, k, f * 128:(f + 1) * 128], uT3[:, k, :], k == 0, k == 7,
                           ["uT", "w1b%d" % k], pres(b_))
                    rl = G[f % 2][:, 0:256]
                    act(rl, bank(b_, 0, 256), AF.Relu, pres(b_), ["G%d" % (f % 2)])
                    tt("pool", hTbuf(f), rl, rl, ALU.mult, ["G%d" % (f % 2)], ["hT%d" % f])

            def p2_stats(X, xn, c):
                mm(bank(6, 0, 256), onesm_f[:, :], X[:, c, :], c == 0, c == 7, ["onesm_f", xn + "c%d" % c], pres(6))
                mm(bank(7, 0, 256), onesm_f[:, :], G[2 + c % 2][:, 0:256], c == 0, c == 7, ["onesm_f", "G%d" % (2 + c % 2)], pres(7))

            def p2_stage2(m):
                X = XB[m % 2]
                xn = "X%d" % (m % 2)
                for c in range(8):
                    b_ = 4 + c % 2
                    for f in range(32):
                        mm(bank(b_, 0, 256), w2b[:, f, c * 128:(c + 1) * 128], hTbuf(f), f == 0, f == 31,
                           ["hT%d" % f, "w2b%d" % (f // 4)], pres(b_))
                    stt(X[:, c, :], bank(b_, 0, 256), g1m[:, c:c + 1], X[:, c, :], ALU.mult, ALU.add, pres(b_) + ["mod1", xn + "c%d" % c], [xn + "c%d" % c])
                    ysq = G[2 + c % 2][:, 0:256]
                    act(ysq, X[:, c, :], AF.Square, [xn + "c%d" % c], ["G%d" % (2 + c % 2)])
                    if c >= 1:
                        p2_stats(X, xn, c - 1)
                p2_stats(X, xn, 7)

            def p2_tail(m):
                X = XB[m % 2]
                xn = "X%d" % (m % 2)
                mean_s, var_s, rstd_s, nmr_s = G[4][:, 0:256], G[4][:, 256:512], G[5][:, 0:256], G[5][:, 256:512]
                cp("dve", mean_s, bank(6, 0, 256), pres(6), ["G4a"])
                tt("dve", var_s, mean_s, mean_s, ALU.mult, ["G4a"], ["G4b"])
                tt("dve", var_s, bank(7, 0, 256), var_s, ALU.subtract, pres(7) + ["G4b"], ["G4b"])
                act(rstd_s, var_s, AF.Ln, ["G4b"], ["G5a"], bias=1e-5)
                act(rstd_s, rstd_s, AF.Exp, ["G5a"], ["G5a"], scale=-0.5)
                stt(nmr_s, mean_s, -1.0, rstd_s, ALU.mult, ALU.mult, ["G4a", "G5a"], ["G5b"])
                for c in range(8):
                    tt("dve", X[:, c, :], X[:, c, :], rstd_s, ALU.mult, [xn + "c%d" % c, "G5a"], [xn + "c%d" % c])
                    tt("dve", X[:, c, :], X[:, c, :], nmr_s, ALU.add, [xn + "c%d" % c, "G5b"], [xn + "c%d" % c])
                    ts("dve", X[:, c, :], X[:, c, :], lnc[:, 16 + c:17 + c], lnc[:, 24 + c:25 + c], ALU.mult, ALU.add, [xn + "c%d" % c, "lnc"], [xn + "c%d" % c])

            def p2_out(m):
                X = XB[m % 2]
                xn = "X%d" % (m % 2)
                for tt_ in range(2):
                    t = 2 * m + tt_
                    b0 = 4 + 2 * tt_
                    sg = stg[tt_]
                    sn = "stg%d" % tt_
                    for c in range(8):
                        tr(bank(b0 + c // 4, (c % 4) * 128, (c % 4 + 1) * 128), X[:, c, tt_ * 128:(tt_ + 1) * 128], ident_f, [xn + "c%d" % c, "cst"],
                           pres(b0 + c // 4))
                    act(sg[:, 0:512], bank(b0), AF.Copy, pres(b0), [sn])
                    cp("dve", sg[:, 512:1024], bank(b0 + 1), pres(b0 + 1), [sn])
                    dma("sp", out_d[t * 128:(t + 1) * 128, :], sg[:, :], [sn], [("out", t)], key="outst%d" % tt_)

            p2_load(0)
            p2_u(0)
            for m in range(n_macro):
                p2_stage1(m, 0, 12)
                if m >= 1:
                    p2_out(m - 1)
                if m + 1 < n_macro:
                    p2_load(m + 1)
                p2_stage1(m, 12, 32)
                if m + 1 < n_macro:
                    p2_u(m + 1)
                p2_stage2(m)
                p2_tail(m)
            p2_out(n_macro - 1)
            final_keys.append("outst0")
            final_keys.append("outst1")
        else:
            final_keys.extend("x1st%d" % m for m in range(n_macro))
        print("n_ops", len(S.ops), {e: sum(1 for o in S.ops if o.eng == e) for e in S.ENGS})
        S.emit(st, final_wait_keys=set(final_keys) | set("dbg_" + k for k in dbg))
    return nc


def _consts():
    cst = np.zeros((128, 768), np.float32)
    cst[:, 0:128] = np.eye(128, dtype=np.float32)
    s = np.arange(128)[:, None]
    c = np.arange(128)[None, :]
    same = (s // 64) == (c // 64)
    bdb = (same & (s <= c)).astype(np.float32)
    ref = (c // 64) * 64 + 31
    bdref = bdb - (same & (s <= ref)).astype(np.float32)
    bdlast = (same & (s > c)).astype(np.float32)
    cst[:, 128:256] = bdref
    cst[:, 256:384] = bdb
    cst[:, 384:512] = bdlast
    cst[:, 512:640] = bdb
    cst[:, 640:768] = np.where(s <= c, 0.0, -30000.0)
    misc = np.zeros((128, 4), np.float32)
    inv_freq = (1.0 / (10000.0 ** (np.arange(0, 64, 2, dtype=np.float32) / 64.0))).astype(np.float32)
    p = np.arange(128)
    misc[:, 0] = inv_freq[p % 32]
    misc[:, 1] = np.where((p % 64) < 32, -1.0, 1.0)
    return cst, misc


def _cols(v, n):
    return np.ascontiguousarray(np.asarray(v, np.float32).reshape(n, 128).T)


def make_in_maps(x, c, positions, w_ada, b_ada, w_in, hg_lower_bounds, hg_norm_w, mla_q_norm_w, w_q_up,
                 mla_kv_norm_w, w_kv_up, w_out, ln1_g, ln1_b, w_mlp_in, w_mlp_out, ln2_g, ln2_b):
    f = lambda a: np.asarray(a, np.float32)
    w_in0 = f(w_in)[0]
    perm = np.concatenate([np.arange(0, 512), np.arange(2048, 2304), np.arange(2304, 2560), np.arange(2560, 2624),
                           np.arange(2592, 2624), np.arange(2560, 2592),
                           np.arange(512, 1024), np.arange(1024, 1536), np.arange(1536, 2048)])
    w_in_p = np.ascontiguousarray(w_in0[:, perm])
    wq0 = f(w_q_up)[0]
    qn = np.concatenate([np.arange(h * 192, h * 192 + 128) for h in range(4)])
    qp = np.concatenate([np.arange(h * 192 + 128, h * 192 + 192) for h in range(4)])
    qps = np.concatenate([np.concatenate([np.arange(h * 192 + 160, h * 192 + 192), np.arange(h * 192 + 128, h * 192 + 160)])
                          for h in range(4)])
    w_q_p = np.ascontiguousarray(wq0[:, np.concatenate([qn, qp, qps])])
    wkv0 = f(w_kv_up)[0]
    kn = np.concatenate([np.arange(h * 256, h * 256 + 128) for h in range(4)])
    vv = np.concatenate([np.arange(h * 256 + 128, h * 256 + 256) for h in range(4)])
    w_kv_p = np.ascontiguousarray(wkv0[:, np.concatenate([kn, vv])])
    cst, misc = _consts()
    nw = np.concatenate([_cols(f(mla_q_norm_w)[0], 2), _cols(f(mla_kv_norm_w)[0], 2), _cols(f(hg_norm_w)[0], 4)], axis=1)
    lncols = np.concatenate([_cols(f(ln1_g)[0], 8), _cols(f(ln1_b)[0], 8), _cols(f(ln2_g)[0], 8), _cols(f(ln2_b)[0], 8)], axis=1)
    shared = {
        "w_ada": np.ascontiguousarray(f(w_ada)[0]), "bada": _cols(f(b_ada)[0], 48), "w_in": w_in_p,
        "lb": np.ascontiguousarray(f(hg_lower_bounds)),
        "nw": np.ascontiguousarray(nw), "w_q": w_q_p, "w_kv": w_kv_p, "w_out": np.ascontiguousarray(f(w_out)[0]),
        "lncols": np.ascontiguousarray(lncols), "w1": np.ascontiguousarray(f(w_mlp_in)[0]),
        "w2": np.ascontiguousarray(f(w_mlp_out)[0]), "cst": cst, "misc": misc,
    }
    xs = f(x)
    cs = f(c)
    ps_ = np.asarray(positions, np.int32)
    maps = []
    for b in range(8):
        mp = dict(shared)
        mp["x"] = np.ascontiguousarray(xs[b])
        mp["pos"] = np.ascontiguousarray(ps_[b])
        mp["ccol"] = _cols(cs[b], 8)
        maps.append(mp)
    return maps


_NC_CACHE = {}


def kernel(**inputs):
    maps = make_in_maps(**inputs)
    if "nc" not in _NC_CACHE:
        _NC_CACHE["nc"] = build_nc()
    res = run_bass_kernel_spmd(_NC_CACHE["nc"], maps, core_ids=list(range(8)))
    return np.stack([np.asarray(r["out"], np.float32) for r in res.results], axis=0)
```
